# Optimizing a Trainium2 kernel written in Bass

```python
import jax, jax.numpy as jnp
from jax import lax
import numpy as np

D_MODEL = 1024
BATCH = 8
SEQ = 4096
DEPTH = 2

N_META = 16
GRID_W = 64
BLOCK = 128
FRONT_PAD = BLOCK - N_META
EPS = 1e-6
D_FF = 4 * D_MODEL

A_HEADS = 4
A_DK = 128
A_DV = 128
A_CHUNK = 64
A_CONV = 5
A_QKW = A_HEADS * A_DK
A_VW = A_HEADS * A_DV

B_HEADS = 8
B_KV = 2
B_HD = 64
B_WIN = 128
B_QW = B_HEADS * B_HD
B_KVW = B_KV * B_HD

C_HEADS = 8
C_KV = 2
C_HD = 128
C_QW = C_HEADS * C_HD
C_KVW = C_KV * C_HD
ROPE_THETA = 10000.0

MIX_AB = A_VW + B_QW
AB_SIZES = (A_QKW, A_QKW, A_VW, A_VW, 2 * A_HEADS, 2 * A_HEADS, B_QW, B_KVW, B_KVW)
IN_AB = sum(AB_SIZES)
IN_C = C_QW + 2 * C_KVW

kernel_name = "hybrid_deltanet_swa_axialrope_encoder"


def rmsnorm(x, g):
    xf = x.astype(jnp.float32)
    y = xf * lax.rsqrt(jnp.mean(xf * xf, axis=-1, keepdims=True) + EPS)
    return (y * g.astype(jnp.float32)).astype(x.dtype)


def l2norm(x):
    return x * lax.rsqrt(jnp.sum(x * x, axis=-1, keepdims=True) + EPS)


def front_pad(t, n):
    return jnp.pad(t, [(0, 0), (n, 0)] + [(0, 0)] * (t.ndim - 2))


def split_cols(t, sizes):
    idx = np.cumsum(np.array(sizes))[:-1].tolist()
    return jnp.split(t, idx, axis=-1)


def centred_conv(x, w):
    k = w.shape[0]
    h = k // 2
    n = x.shape[1]
    xp = jnp.pad(x, ((0, 0), (h, h), (0, 0)))
    y = xp[:, 0:n] * w[0]
    for j in range(1, k):
        y = y + xp[:, j:j + n] * w[j]
    return y


def gated_delta_chunked(q, k, v, beta, g):
    f32 = jnp.float32
    q, k, v, beta, g = (t.astype(f32) for t in (q, k, v, beta, g))
    bsz, t_len, nh, dk = q.shape
    dv = v.shape[-1]
    c = A_CHUNK
    nc = t_len // c

    def blk(t):
        return jnp.moveaxis(t.reshape(bsz, nc, c, nh, *t.shape[3:]), 3, 1)

    q, k, v, beta, g = blk(q), blk(k), blk(v), blk(beta), blk(g)
    gc = jnp.cumsum(g, axis=-1)
    incl = jnp.tril(jnp.ones((c, c), bool))
    strict = jnp.tril(jnp.ones((c, c), bool), -1)
    decay = jnp.exp(jnp.where(incl, gc[..., :, None] - gc[..., None, :], -jnp.inf))
    kb = k * beta[..., None]
    a_mat = jnp.where(strict, jnp.einsum('bhnid,bhnjd->bhnij', kb, k) * decay, 0.0)
    rhs = jnp.concatenate([v * beta[..., None], kb * jnp.exp(gc)[..., None]], axis=-1)
    sol = lax.linalg.triangular_solve(a_mat, rhs, left_side=True, lower=True,
                                      unit_diagonal=True)
    u, w = sol[..., :dv], sol[..., dv:]
    qk = jnp.einsum('bhnid,bhnjd->bhnij', q, k) * decay
    q_dec = q * jnp.exp(gc)[..., None]
    k_dec = k * jnp.exp(gc[..., -1:] - gc)[..., None]
    g_last = jnp.exp(gc[..., -1])

    def step(s, xs):
        q_n, k_n, u_n, w_n, qk_n, gl_n = xs
        v_new = u_n - jnp.einsum('bhck,bhkv->bhcv', w_n, s)
        o = jnp.einsum('bhck,bhkv->bhcv', q_n, s) + jnp.einsum('bhij,bhjv->bhiv', qk_n, v_new)
        s = s * gl_n[..., None, None] + jnp.einsum('bhck,bhcv->bhkv', k_n, v_new)
        return s, o

    xs = tuple(jnp.moveaxis(t, 2, 0) for t in (q_dec, k_dec, u, w, qk, g_last))
    s0 = jnp.zeros((bsz, nh, dk, dv), f32)
    _, o = lax.scan(step, s0, xs)
    return jnp.transpose(o, (1, 0, 3, 2, 4)).reshape(bsz, t_len, nh, dv)


def delta_mixer(hq, hk, hv, hz, hb, ha, conv_w, a_log, dt_bias, o_gain):
    bsz, n, _ = hq.shape
    qkv = jax.nn.silu(centred_conv(jnp.concatenate([hq, hk, hv], axis=-1), conv_w))
    q, k, v = split_cols(qkv, (A_QKW, A_QKW, A_VW))
    q = l2norm(q.reshape(bsz, n, A_HEADS, A_DK).astype(jnp.float32)) * (A_DK ** -0.5)
    k = l2norm(k.reshape(bsz, n, A_HEADS, A_DK).astype(jnp.float32))
    v = v.reshape(bsz, n, A_HEADS, A_DV).astype(jnp.float32)
    beta = jax.nn.sigmoid(hb.astype(jnp.float32)).reshape(bsz, n, 2, A_HEADS)
    g = -jnp.exp(a_log.astype(jnp.float32)) * jax.nn.softplus(
        ha.astype(jnp.float32).reshape(bsz, n, 2, A_HEADS) + dt_bias.astype(jnp.float32))
    q, k, v, beta, g = (front_pad(t, FRONT_PAD) for t in (q, k, v, beta, g))
    fwd = gated_delta_chunked(q, k, v, beta[:, :, 0], g[:, :, 0])
    rev = lambda t: jnp.flip(t, axis=1)
    bwd = rev(gated_delta_chunked(rev(q), rev(k), rev(v), rev(beta[:, :, 1]), rev(g[:, :, 1])))
    o = (fwd + bwd)[:, FRONT_PAD:]
    o = rmsnorm(o, o_gain) * jax.nn.silu(hz.reshape(bsz, n, A_HEADS, A_DV).astype(jnp.float32))
    return o.reshape(bsz, n, A_VW).astype(hq.dtype)


def window_mixer(q, k, v, sink):
    bsz, n = q.shape[:2]
    grp = B_HEADS // B_KV
    lp = n + FRONT_PAD
    nb = lp // BLOCK
    qb = front_pad(q, FRONT_PAD).reshape(bsz, nb, BLOCK, B_KV, grp, B_HD)

    def ext(t):
        t = jnp.pad(t, ((0, 0), (FRONT_PAD + BLOCK, BLOCK), (0, 0), (0, 0)))
        t = t.reshape(bsz, nb + 2, BLOCK, B_KV, B_HD)
        return jnp.concatenate([t[:, :-2], t[:, 1:-1], t[:, 2:]], axis=2)

    kw, vw = ext(k), ext(v)
    mk, mv = k[:, :N_META], v[:, :N_META]
    s_win = jnp.einsum('bnqkgd,bnskd->bnkgqs', qb, kw).astype(jnp.float32)
    s_meta = jnp.einsum('bnqkgd,bmkd->bnkgqm', qb, mk).astype(jnp.float32)
    r = jnp.arange(BLOCK)[:, None]
    cpos = jnp.arange(3 * BLOCK)[None, :]
    dist = jnp.abs(BLOCK + r - cpos)
    pk = (jnp.arange(nb)[:, None] - 1) * BLOCK + cpos
    in_seq = (pk >= BLOCK) & (pk < lp)
    valid = (dist <= B_WIN)[None] & in_seq[:, None, :]
    slopes = jnp.exp2(-8.0 * (jnp.arange(B_HEADS, dtype=jnp.float32) + 1.0) / B_HEADS)
    bias = -slopes.reshape(B_KV, grp, 1, 1) * dist.astype(jnp.float32)
    s_win = jnp.where(valid[None, :, None, None], s_win + bias, -jnp.inf)
    s_sink = jnp.broadcast_to(sink.astype(jnp.float32).reshape(B_KV, grp, 1, 1),
                              (bsz, nb, B_KV, grp, BLOCK, 1))
    p = jax.nn.softmax(jnp.concatenate([s_win, s_meta, s_sink], axis=-1), axis=-1)
    p_win = p[..., :3 * BLOCK].astype(v.dtype)
    p_meta = p[..., 3 * BLOCK:3 * BLOCK + N_META].astype(v.dtype)
    o = (jnp.einsum('bnkgqs,bnskd->bnqkgd', p_win, vw)
         + jnp.einsum('bnkgqm,bmkd->bnqkgd', p_meta, mv))
    return o.reshape(bsz, lp, B_HEADS * B_HD)[:, FRONT_PAD:]


def ab_mixer(u, w_in, conv_w, a_log, dt_bias, a_out_g, bq_g, bk_g, sink, w_out):
    bsz, n, _ = u.shape
    proj = u @ w_in
    qa, ka, va, za, ba, aa, qb, kb, vb = split_cols(proj, AB_SIZES)
    ya = delta_mixer(qa, ka, va, za, ba, aa, conv_w, a_log, dt_bias, a_out_g)
    qb = rmsnorm(qb.reshape(bsz, n, B_HEADS, B_HD), bq_g) * (B_HD ** -0.5)
    kb = rmsnorm(kb.reshape(bsz, n, B_KV, B_HD), bk_g)
    vb = vb.reshape(bsz, n, B_KV, B_HD)
    yb = window_mixer(qb, kb, vb, sink)
    return jnp.concatenate([ya, yb.astype(ya.dtype)], axis=-1) @ w_out


def axial_rope(rows):
    row = jnp.repeat(jnp.arange(rows), GRID_W)
    col = jnp.tile(jnp.arange(GRID_W), rows)
    meta = jnp.arange(N_META) - N_META
    row = jnp.concatenate([meta, row]).astype(jnp.float32)
    col = jnp.concatenate([meta, col]).astype(jnp.float32)
    axis_dim = C_HD // 2
    freqs = ROPE_THETA ** (-jnp.arange(0, axis_dim, 2, dtype=jnp.float32) / axis_dim)
    ang = jnp.concatenate([row[:, None] * freqs, col[:, None] * freqs], axis=-1)
    return jnp.cos(ang), jnp.sin(ang)


def apply_rope(x, cos, sin):
    xf = x.astype(jnp.float32).reshape(*x.shape[:-1], x.shape[-1] // 2, 2)
    x0, x1 = xf[..., 0], xf[..., 1]
    c = cos[None, :, None, :]
    s = sin[None, :, None, :]
    out = jnp.stack([x0 * c - x1 * s, x0 * s + x1 * c], axis=-1)
    return out.reshape(x.shape).astype(x.dtype)


def dense_mixer(q, k, v):
    bsz, n = q.shape[:2]
    grp = C_HEADS // C_KV
    lp = n + FRONT_PAD
    nb = lp // BLOCK
    qb = jnp.moveaxis(front_pad(q, FRONT_PAD).reshape(bsz, nb, BLOCK, C_KV, grp, C_HD), 1, 0)

    def one_block(qi):
        s = jnp.einsum('bqkgd,bskd->bkgqs', qi, k).astype(jnp.float32)
        p = jax.nn.softmax(s, axis=-1).astype(v.dtype)
        return jnp.einsum('bkgqs,bskd->bqkgd', p, v)

    o = lax.map(one_block, qb)
    return jnp.moveaxis(o, 0, 1).reshape(bsz, lp, C_QW)[:, FRONT_PAD:]


def c_mixer(u, w_qkv, qg, kg, w_out, cos, sin):
    bsz, n, _ = u.shape
    q, k, v = split_cols(u @ w_qkv, (C_QW, C_KVW, C_KVW))
    q = apply_rope(rmsnorm(q.reshape(bsz, n, C_HEADS, C_HD), qg), cos, sin) * (C_HD ** -0.5)
    k = apply_rope(rmsnorm(k.reshape(bsz, n, C_KV, C_HD), kg), cos, sin)
    v = v.reshape(bsz, n, C_KV, C_HD)
    return dense_mixer(q, k, v) @ w_out


def setup_inputs(seed: int = 0) -> dict:
    key = jax.random.key(seed)
    ks = jax.random.split(key, 20)
    n_even = (DEPTH + 1) // 2
    n_odd = DEPTH // 2
    nrm = lambda k, shape, scale: jax.random.normal(k, shape, jnp.float32) * scale
    gain = lambda k, shape: 1.0 + 0.02 * jax.random.normal(k, shape, jnp.float32)
    dt = jnp.exp(jax.random.uniform(ks[6], (n_even, 2, A_HEADS), jnp.float32,
                                    np.log(1e-3), np.log(1e-1)))
    return {
        "x": nrm(ks[0], (BATCH, SEQ, D_MODEL), 1.0),
        "meta_tokens": nrm(ks[1], (N_META, D_MODEL), 1.0),
        "attn_norm_g": gain(ks[2], (DEPTH, D_MODEL)),
        "mlp_norm_g": gain(ks[3], (DEPTH, D_MODEL)),
        "w_in_ab": nrm(ks[4], (n_even, D_MODEL, IN_AB), D_MODEL ** -0.5),
        "conv_w_a": nrm(ks[5], (n_even, A_CONV, 2 * A_QKW + A_VW), A_CONV ** -0.5),
        "a_log": jnp.log(jax.random.uniform(ks[7], (n_even, 2, A_HEADS), jnp.float32, 1.0, 16.0)),
        "dt_bias": dt + jnp.log(-jnp.expm1(-dt)),
        "a_out_norm_g": gain(ks[8], (n_even, A_DV)),
        "b_q_norm_g": gain(ks[9], (n_even, B_HD)),
        "b_k_norm_g": gain(ks[10], (n_even, B_HD)),
        "b_sink": nrm(ks[11], (n_even, B_HEADS), 0.5),
        "w_out_ab": nrm(ks[12], (n_even, MIX_AB, D_MODEL), MIX_AB ** -0.5),
        "w_qkv_c": nrm(ks[13], (n_odd, D_MODEL, IN_C), D_MODEL ** -0.5),
        "c_q_norm_g": gain(ks[14], (n_odd, C_HD)),
        "c_k_norm_g": gain(ks[15], (n_odd, C_HD)),
        "w_out_c": nrm(ks[16], (n_odd, C_QW, D_MODEL), C_QW ** -0.5),
        "w_ff1": nrm(ks[17], (DEPTH, D_MODEL, D_FF), D_MODEL ** -0.5),
        "w_ff2": nrm(ks[18], (DEPTH, D_FF, D_MODEL), D_FF ** -0.5),
    }


def reference(x, meta_tokens, attn_norm_g, mlp_norm_g, w_in_ab, conv_w_a, a_log, dt_bias,
              a_out_norm_g, b_q_norm_g, b_k_norm_g, b_sink, w_out_ab, w_qkv_c, c_q_norm_g,
              c_k_norm_g, w_out_c, w_ff1, w_ff2):
    bsz, n_tok, _ = x.shape
    rows = n_tok // GRID_W
    meta = jnp.broadcast_to(meta_tokens.astype(x.dtype)[None], (bsz, N_META, x.shape[-1]))
    h = jnp.concatenate([meta, x], axis=1)
    cos, sin = axial_rope(rows)
    for layer in range(DEPTH):
        i = layer // 2
        u = rmsnorm(h, attn_norm_g[layer])
        if layer % 2 == 0:
            mix = ab_mixer(u, w_in_ab[i], conv_w_a[i], a_log[i], dt_bias[i], a_out_norm_g[i],
                           b_q_norm_g[i], b_k_norm_g[i], b_sink[i], w_out_ab[i])
        else:
            mix = c_mixer(u, w_qkv_c[i], c_q_norm_g[i], c_k_norm_g[i], w_out_c[i], cos, sin)
        h = h + mix.astype(h.dtype)
        u = rmsnorm(h, mlp_norm_g[layer])
        h = h + jnp.square(jax.nn.relu(u @ w_ff1[layer])) @ w_ff2[layer]
    return h[:, N_META:]
```

```python
import contextlib
import numpy as np
import concourse.bass as bass
import concourse.mybir as mybir
from concourse.bass_utils import run_bass_kernel_spmd

F32 = mybir.dt.float32
BF16 = mybir.dt.bfloat16
AF = mybir.ActivationFunctionType
ALU = mybir.AluOpType
AX = mybir.AxisListType

D = 1024
SEQ = 4096
NMETA = 16
FP = 112
LP = 4224
NT = 33
HSMUL = 1
DFF = 4096
EPS = 1e-6
IN_AB = 2832
IN_C = 1536


class Res:
    __slots__ = ("name", "w", "r")

    def __init__(self, name):
        self.name = name
        self.w = None
        self.r = []


class Sched:
    NSLOT = 12

    def __init__(self, nc, stack):
        self.nc = nc
        self.eng = {"pe": nc.tensor, "act": nc.scalar, "dve": nc.vector,
                    "pool": nc.gpsimd, "sp": nc.sync}
        self.count = {e: 0 for e in self.eng}
        self.waited = {e: {} for e in self.eng}
        self.sem = {}
        for e in self.eng:
            self.sem[e] = stack.enter_context(nc.semaphore("s_" + e))
        self.dslot = {}
        for q in ("sp", "act", "pool"):
            for s in range(self.NSLOT):
                self.sem[("d", q, s)] = stack.enter_context(nc.semaphore(f"d_{q}_{s}"))
                self.dslot[(q, s)] = 0
        self.dnext = {q: 0 for q in ("sp", "act", "pool")}
        self.n_ops = 0

    def _deps(self, reads, writes):
        deps = []
        for r in reads:
            if r.w is not None:
                deps.append(r.w)
        for w in writes:
            if w.w is not None:
                deps.append(w.w)
            deps.extend(w.r)
        return deps

    def _waits(self, e, deps):
        out = []
        wd = self.waited[e]
        best = {}
        for (k, v) in deps:
            if wd.get(k, 0) >= v or (e == "pe" and k == "pe"):
                continue
            if best.get(k, 0) < v:
                best[k] = v
        for k, v in best.items():
            wd[k] = v
            out.append((k, v))
        return out

    def _mark(self, ev, reads, writes):
        for r in reads:
            r.r.append(ev)
        for w in writes:
            w.w = ev
            w.r = []

    def _run(self, e, waits, fn, sig):
        eng = self.eng[e]
        for (k, v) in waits:
            eng.wait_ge(self.sem[k], v)
        if fn is None:
            return
        ins = fn(eng)
        ins.then_inc(self.sem[sig[0]], sig[1])

    def op(self, e, fn, reads=(), writes=()):
        waits = self._waits(e, self._deps(reads, writes))
        self.count[e] += 1
        ev = (e, self.count[e])
        self._run(e, waits, fn, (e, 1))
        self._mark(ev, reads, writes)
        self.n_ops += 1
        return ev

    def dma(self, q, out, in_, reads=(), writes=(), **kw):
        s = self.dnext[q]
        self.dnext[q] = (s + 1) % self.NSLOT
        key = ("d", q, s)
        deps = self._deps(reads, writes)
        prev = self.dslot[(q, s)]
        if prev:
            deps.append((key, prev))
        waits = self._waits(q, deps)
        val = prev + 16
        self.dslot[(q, s)] = val
        ev = (key, val)
        self._run(q, waits, lambda eng: eng.dma_start(out=out, in_=in_, **kw), (key, 16))
        self._mark(ev, reads, writes)
        self.n_ops += 1
        return ev

    def _all_events(self):
        deps = []
        for (q, s), v in self.dslot.items():
            if v:
                deps.append((("d", q, s), v))
        for e in ("pe", "act", "dve", "pool"):
            if self.count[e]:
                deps.append((e, self.count[e]))
        return deps

    def barrier(self):
        deps = self._all_events()
        for e in self.eng:
            self._run(e, self._waits(e, deps), None, None)

    def finish(self):
        self._run("sp", self._waits("sp", self._all_events()), None, None)


class K:
    pass


class Rec:
    def __init__(self, k):
        self.k = k
        self.ops = []

    def op(self, *a, **kw):
        self.ops.append(lambda: self.k.S.op(*a, **kw))

    def dma(self, *a, **kw):
        self.ops.append(lambda: self.k.S.dma(*a, **kw))

    def raw(self, fn):
        self.ops.append(fn)


def emit_lockstep(k, bodies, d=2):
    for g0 in range(0, len(bodies), d):
        recs = []
        for b in bodies[g0:g0 + d]:
            r = Rec(k)
            b(r)
            recs.append(r.ops)
        for p in range(max(len(o) for o in recs)):
            for ops in recs:
                if p < len(ops):
                    ops[p]()


def emit_skewed(k, bodies, depth=2):
    recs = []
    for b in bodies:
        r = Rec(k)
        b(r)
        recs.append(r.ops)
    if not recs:
        return
    L = max(len(o) for o in recs)
    step = max(1, (L + depth - 1) // depth)
    items = []
    for t, ops in enumerate(recs):
        for p, f in enumerate(ops):
            items.append((t * step + p, t, f))
    items.sort(key=lambda x: (x[0], x[1]))
    for it in items:
        it[2]()


def _sb(k, st, name, shape, dt):
    k.uid = getattr(k, "uid", 0) + 1
    t = st.enter_context(k.nc.sbuf_tensor(f"sb{k.uid}_{name}", shape, dt))
    return t, Res(name)


def bank(k, b, n=512, off=0):
    return k.ps[:, b * 512 + off:b * 512 + off + n]


def bank_bf(k, b):
    return k.psb[:, b * 1024:(b + 1) * 1024]


def load_weight_bf16(k, dst, src, kchunks, ncols):
    S = k.S
    step = min(ncols, 2048)
    for c in range(kchunks):
        for n0 in range(0, ncols, step):
            n1 = min(ncols, n0 + step)
            S.dma("pool", dst[0][:, c, n0:n1], src[c * 128:(c + 1) * 128, n0:n1], writes=[dst[1]])


def rms_to_uT(k, ht, gt, ub, stat, uT_ap, uT_res, tbank, pad0=False, S=None):
    S = S or k.S
    ss, rs = stat
    S.op("act", lambda e: e.activation(out=ub[0][:], in_=ht[0][:], func=AF.Square, accum_out=ss[0][:]),
         reads=[ht[1]], writes=[ub[1], ss[1]])
    S.op("act", lambda e: e.activation(out=rs[0][:], in_=ss[0][:], func=AF.Ln, scale=1.0 / D, bias=EPS),
         reads=[ss[1]], writes=[rs[1]])
    S.op("act", lambda e: e.activation(out=rs[0][:], in_=rs[0][:], func=AF.Exp, scale=-0.5), reads=[rs[1]], writes=[rs[1]])
    S.op("dve", lambda e: e.scalar_tensor_tensor(out=ub[0][:], in0=ht[0][:], scalar=rs[0][:, 0:1], in1=gt[0][:],
                                                 op0=ALU.mult, op1=ALU.mult),
         reads=[ht[1], rs[1], gt[1]], writes=[ub[1]])
    pT = bank_bf(k, tbank)

    def tr(e):
        for c in range(8):
            i = e.transpose(out=pT[:, c * 128:(c + 1) * 128], in_=ub[0][:, c * 128:(c + 1) * 128],
                            identity=k.ident[0][:])
        return i
    S.op("pe", tr, reads=[ub[1], k.ident[1]], writes=[k.PB[tbank]])
    S.op("act", lambda e: e.activation(out=uT_ap, in_=pT.rearrange("p (c n) -> p c n", c=8), func=AF.Copy),
         reads=[k.PB[tbank]], writes=[uT_res])


def convert_jobs(k):
    S = k.S
    d = k.d
    k.wres = {(l, i): Res(f"scrw{l}{i}") for l in range(2) for i in range(2)}
    jobs = []
    for l in range(2):
        for c in range(8):
            for n0 in range(0, DFF, 2048):
                jobs.append(lambda l=l, c=c, n0=n0: S.dma("pool", d["scr_w1"][l, c * 128:(c + 1) * 128, n0:n0 + 2048],
                                                         d["w_ff1"][l, c * 128:(c + 1) * 128, n0:n0 + 2048], writes=[k.wres[(l, 0)]]))
        for f in range(0, 32, 2):
            jobs.append(lambda l=l, f=f: S.dma("pool", d["scr_w2"][l, f * 128:(f + 2) * 128, :].rearrange("(a p) n -> p a n", p=128),
                                               d["w_ff2"][l, f * 128:(f + 2) * 128, :].rearrange("(a p) n -> p a n", p=128),
                                               writes=[k.wres[(l, 1)]]))
    return jobs


def mlp_phase(k, src, g_row, w1d, w2d, out_fn, tiles, l=None):
    S = k.S
    nc = k.nc
    with contextlib.ExitStack() as st:
        w1 = _sb(k, st, "m_w1", [128, 8, DFF], BF16)
        w2 = _sb(k, st, "m_w2", [128, 32, D], BF16)
        gt = _sb(k, st, "m_gt", [128, D], F32)
        hts = [[_sb(k, st, f"m_ht{a}{b}", [128, D], F32) for b in range(2)] for a in range(2)]
        hos = [_sb(k, st, f"m_ho{b}", [128, D], F32) for b in range(2)]
        ub = _sb(k, st, "m_ub", [128, D], BF16)
        uTs = [_sb(k, st, f"m_uT{a}", [128, 8, 256], BF16) for a in range(2)]
        aT = _sb(k, st, "m_aT", [128, 32, 256], BF16)
        aTr = [Res(f"m_aT{f}") for f in range(32)]
        rr = [_sb(k, st, f"m_r{i}", [128, 512], F32) for i in range(3)]
        stat = (_sb(k, st, "m_ss", [128, 1], F32), _sb(k, st, "m_rs", [128, 1], F32))
        S.dma("sp", gt[0][:], g_row.partition_broadcast(128), writes=[gt[1]])
        if l is not None and getattr(k, "wres", None):
            for c in range(8):
                S.dma("sp" if c % 2 == 0 else "act", w1[0][:, c, :], k.d["scr_w1"][l, c * 128:(c + 1) * 128, :],
                      reads=[k.wres[(l, 0)]], writes=[w1[1]])
            for f in range(0, 32, 4):
                S.dma("sp" if (f // 4) % 2 == 0 else "act", w2[0][:, f:f + 4, :],
                      k.d["scr_w2"][l, f * 128:(f + 4) * 128, :].rearrange("(a p) n -> p a n", p=128),
                      reads=[k.wres[(l, 1)]], writes=[w2[1]])
        else:
            load_weight_bf16(k, w1, w1d, 8, DFF)
            load_weight_bf16(k, w2, w2d, 32, D)
        groups = [tiles[i:i + 2] for i in range(0, len(tiles), 2)]
        hb = [Res(f"m_hb{i}") for i in range(4)]

        def prep(gi):
            grp = groups[gi]
            a = gi % 2
            for j, t in enumerate(grp):
                ht = hts[a][j]
                S.dma("sp", ht[0][:], src(t), writes=[ht[1]])
                rms_to_uT(k, ht, gt, ub, stat, uTs[a][0][:, :, j * 128:(j + 1) * 128], uTs[a][1], 0)

        def ff1(gi):
            grp = groups[gi]
            a = gi % 2
            n = 128 * len(grp)
            for fp in range(16):
                b = 1 + fp % 2
                pa = bank(k, b).rearrange("p (two n) -> p two n", two=2)[:, :, 0:n]

                def mm(e, fp=fp, b=b):
                    for ff in range(2):
                        f = 2 * fp + ff
                        for c in range(8):
                            i = e.matmul(out=bank(k, b, n, ff * 256), lhsT=w1[0][:, c, f * 128:(f + 1) * 128],
                                         rhs=uTs[a][0][:, c, 0:n], start=(c == 0), stop=(c == 7))
                    return i
                S.op("pe", mm, reads=[w1[1], uTs[a][1]], writes=[k.PB[b]])
                r = rr[fp % 3]
                rv = r[0][:].rearrange("p (two n) -> p two n", two=2)[:, :, 0:n]
                S.op("act", lambda e, pa=pa, rv=rv: e.activation(out=rv, in_=pa, func=AF.Relu),
                     reads=[k.PB[b]], writes=[r[1]])
                S.op("pool", lambda e, rv=rv, fp=fp: e.tensor_tensor(out=aT[0][:, 2 * fp:2 * fp + 2, 0:n], in0=rv, in1=rv,
                                                                   op=ALU.mult),
                     reads=[r[1]], writes=[aTr[2 * fp], aTr[2 * fp + 1]])

        def ff2(gi):
            grp = groups[gi]
            a = gi % 2
            for j, t in enumerate(grp):
                for nb in range(2):
                    b = 3 + 2 * j + nb

                    def mm(e, j=j, nb=nb, b=b):
                        for f in range(32):
                            i = e.matmul(out=bank(k, b), lhsT=aT[0][:, f, j * 128:(j + 1) * 128],
                                         rhs=w2[0][:, f, nb * 512:(nb + 1) * 512], start=(f == 0), stop=(f == 31))
                        return i
                    S.op("pe", mm, reads=[w2[1]] + aTr, writes=[k.PB[b]])
                    S.op("dve", lambda e, j=j, nb=nb, b=b: e.tensor_tensor(
                        out=hos[j][0][:, nb * 512:(nb + 1) * 512], in0=bank(k, b),
                        in1=hts[a][j][0][:, nb * 512:(nb + 1) * 512], op=ALU.add),
                        reads=[k.PB[b], hts[a][j][1]], writes=[hos[j][1]])
                S.dma("act", out_fn(t), hos[j][0][:], reads=[hos[j][1]])

        prep(0)
        for gi in range(len(groups)):
            ff1(gi)
            if gi + 1 < len(groups):
                prep(gi + 1)
            ff2(gi)
        S.barrier()


def layer_c(k, src, dst, l):
    S = k.S
    d = k.d
    with contextlib.ExitStack() as st:
        QT = _sb(k, st, "c_QT", [128, 8, LP], BF16)
        KT = _sb(k, st, "c_KT", [128, 2, LP], BF16)
        V = _sb(k, st, "c_V", [128, NT, 256], BF16)
        QTr = [Res(f"c_QT{t}") for t in range(NT)]
        with contextlib.ExitStack() as s1:
            wqkv = _sb(k, s1, "c_wqkv", [128, 8, IN_C], BF16)
            gt = _sb(k, s1, "c_gt", [128, D], F32)
            GQK = _sb(k, s1, "c_GQK", [128, 10, 128], F32)
            hts = [_sb(k, s1, f"c_ht{i}", [128, D], F32) for i in range(2)]
            css = [_sb(k, s1, f"c_cs{i}", [128, 128], F32) for i in range(2)]
            B2 = []
            for i in range(2):
                B2.append(dict(
                    ub=_sb(k, s1, f"c_ub{i}", [128, D], BF16), uT=_sb(k, s1, f"c_uT{i}", [128, 8, 128], BF16),
                    qkv=_sb(k, s1, f"c_qkv{i}", [128, IN_C], F32), sq=_sb(k, s1, f"c_sq{i}", [128, 10, 128], F32),
                    t1=_sb(k, s1, f"c_t1{i}", [128, 10, 64], F32), t2=_sb(k, s1, f"c_t2{i}", [128, 10, 64], F32),
                    qr=_sb(k, s1, f"c_qr{i}", [128, 10, 64, 2], BF16), ssq=_sb(k, s1, f"c_ssq{i}", [128, 10], F32),
                    stat=(_sb(k, s1, f"c_ss{i}", [128, 1], F32), _sb(k, s1, f"c_rs{i}", [128, 1], F32))))
            load_weight_bf16(k, wqkv, d["w_qkv_c"], 8, IN_C)
            S.dma("sp", gt[0][:], d["attn_norm_g"][l:l + 1, :].partition_broadcast(128), writes=[gt[1]])
            S.dma("sp", GQK[0][:, 0:8, :], d["c_q_norm_g"].partition_broadcast(128).unsqueeze(1).broadcast_to([128, 8, 128]),
                  writes=[GQK[1]])
            S.dma("sp", GQK[0][:, 8:10, :], d["c_k_norm_g"].partition_broadcast(128).unsqueeze(1).broadcast_to([128, 2, 128]),
                  writes=[GQK[1]])
            S.op("act", lambda e: e.mul(out=GQK[0][:, 0:8, :], in_=GQK[0][:, 0:8, :], mul=128.0 ** -0.5),
                 reads=[GQK[1]], writes=[GQK[1]])
            def body(S, t):
                ht = hts[t % 2]
                cs = css[t % 2]
                bb = B2[t % 2]
                ub, uT, qkv, sq, t1, t2, qr, ssq, stat = (bb[n] for n in ("ub", "uT", "qkv", "sq", "t1", "t2", "qr", "ssq", "stat"))
                qn = sq
                tb = 0 if t % 2 == 0 else 6
                qb_ = 4 if t % 2 == 0 else 7
                S.dma("sp", ht[0][:], src(t), writes=[ht[1]])
                S.dma("sp", cs[0][:], d["rope"][t * 128:(t + 1) * 128, :], writes=[cs[1]])
                rms_to_uT(k, ht, gt, ub, stat, uT[0][:], uT[1], tb, S=S)
                for nb in range(3):
                    def mm(e, nb=nb):
                        for c in range(8):
                            i = e.matmul(out=bank(k, 1 + nb), lhsT=uT[0][:, c, :], rhs=wqkv[0][:, c, nb * 512:(nb + 1) * 512],
                                         start=(c == 0), stop=(c == 7))
                        return i
                    S.op("pe", mm, reads=[uT[1], wqkv[1]], writes=[k.PB[1 + nb]])
                S.op("act", lambda e: e.activation(out=qkv[0][:], in_=k.ps[:, 512:2048], func=AF.Copy),
                     reads=[k.PB[1], k.PB[2], k.PB[3]], writes=[qkv[1]])
                if t == 0:
                    S.op("dve", lambda e: e.tensor_scalar(out=V[0][:, t, :], in0=qkv[0][:, 1280:1536], scalar1=k.padmask[0][:, 0:1],
                                                          scalar2=None, op0=ALU.mult),
                         reads=[qkv[1], k.padmask[1]], writes=[QTr[t]])
                else:
                    S.op("pool", lambda e, t=t: e.tensor_copy(out=V[0][:, t, :], in_=qkv[0][:, 1280:1536]),
                         reads=[qkv[1]], writes=[QTr[t]])
                qk3 = qkv[0][:, 0:1280].rearrange("p (h d) -> p h d", h=10)
                S.op("act", lambda e: e.activation(out=sq[0][:], in_=qk3, func=AF.Square), reads=[qkv[1]], writes=[sq[1]])
                S.op("dve", lambda e: e.tensor_reduce(out=ssq[0][:], in_=sq[0][:], axis=AX.X, op=ALU.add),
                     reads=[sq[1]], writes=[ssq[1]])
                S.op("act", lambda e: e.activation(out=ssq[0][:], in_=ssq[0][:], func=AF.Ln, scale=1.0 / 128, bias=EPS),
                     reads=[ssq[1]], writes=[ssq[1]])
                S.op("act", lambda e: e.activation(out=ssq[0][:], in_=ssq[0][:], func=AF.Exp, scale=-0.5), reads=[ssq[1]], writes=[ssq[1]])
                rb = ssq[0][:].unsqueeze(2).broadcast_to([128, 10, 128])
                S.op("dve", lambda e: e.tensor_tensor(out=qn[0][:], in0=qk3, in1=rb, op=ALU.mult),
                     reads=[qkv[1], ssq[1]], writes=[qn[1]])
                S.op("pool", lambda e: e.tensor_tensor(out=qn[0][:], in0=qn[0][:], in1=GQK[0][:], op=ALU.mult),
                     reads=[qn[1], GQK[1]], writes=[qn[1]])
                q4 = qn[0][:].rearrange("p h (i two) -> p h i two", two=2)
                x0 = q4[:, :, :, 0]
                x1 = q4[:, :, :, 1]
                cosb = cs[0][:, 0:64].unsqueeze(1).broadcast_to([128, 10, 64])
                sinb = cs[0][:, 64:128].unsqueeze(1).broadcast_to([128, 10, 64])
                S.op("dve", lambda e: e.tensor_tensor(out=t1[0][:], in0=x0, in1=cosb, op=ALU.mult),
                     reads=[qn[1], cs[1]], writes=[t1[1]])
                S.op("pool", lambda e: e.tensor_tensor(out=t2[0][:], in0=x1, in1=sinb, op=ALU.mult),
                     reads=[qn[1], cs[1]], writes=[t2[1]])
                S.op("dve", lambda e: e.tensor_tensor(out=qr[0][:, :, :, 0], in0=t1[0][:], in1=t2[0][:], op=ALU.subtract),
                     reads=[t1[1], t2[1]], writes=[qr[1]])
                S.op("dve", lambda e: e.tensor_tensor(out=t1[0][:], in0=x0, in1=sinb, op=ALU.mult),
                     reads=[qn[1], cs[1]], writes=[t1[1]])
                S.op("pool", lambda e: e.tensor_tensor(out=t2[0][:], in0=x1, in1=cosb, op=ALU.mult),
                     reads=[qn[1], cs[1]], writes=[t2[1]])
                S.op("dve", lambda e: e.tensor_tensor(out=qr[0][:, :, :, 1], in0=t1[0][:], in1=t2[0][:], op=ALU.add),
                     reads=[t1[1], t2[1]], writes=[qr[1]])
                qrf = qr[0][:].rearrange("p h i two -> p (h i two)")
                pa = bank_bf(k, qb_)
                pb = bank_bf(k, 5)

                def tr(e):
                    for h in range(8):
                        i = e.transpose(out=pa[:, h * 128:(h + 1) * 128], in_=qrf[:, h * 128:(h + 1) * 128], identity=k.ident[0][:])
                    return i
                S.op("pe", tr, reads=[qr[1], k.ident[1]], writes=[k.PB[qb_]])

                def trk(e):
                    for h in range(8, 10):
                        i = e.transpose(out=pb[:, (h - 8) * 128:(h - 7) * 128], in_=qrf[:, h * 128:(h + 1) * 128], identity=k.ident[0][:])
                    return i
                S.op("pe", trk, reads=[qr[1], k.ident[1]], writes=[k.PB[5]])
                S.op("act", lambda e, t=t: e.activation(out=QT[0][:, :, t * 128:(t + 1) * 128],
                                                        in_=pa.rearrange("p (h n) -> p h n", h=8), func=AF.Copy),
                     reads=[k.PB[qb_]], writes=[QTr[t]])
                S.op("dve", lambda e, t=t: e.tensor_copy(out=KT[0][:, :, t * 128:(t + 1) * 128],
                                                         in_=pb[:, 0:256].rearrange("p (h n) -> p h n", h=2)),
                     reads=[k.PB[5]], writes=[QTr[t]])
            emit_skewed(k, [(lambda S, t=t: body(S, t)) for t in range(NT)])
            S.barrier()
        with contextlib.ExitStack() as s2:
            wout = _sb(k, s2, "c_wout", [128, 8, D], BF16)
            OT = _sb(k, s2, "c_OT", [128, 8, 512], BF16)
            OTr = [Res(f"c_OT{h}") for h in range(8)]
            Pt = [_sb(k, s2, f"c_P{i}", [128, 2, 512], BF16) for i in range(4)]
            accs = [[_sb(k, s2, f"c_acc{i}{j}", [128, 2, 512], F32) for j in range(2)] for i in range(2)]
            onesf = _sb(k, s2, "c_onesf", [128, 128], F32)
            rden = [_sb(k, s2, f"c_rden{i}", [128, 512], F32) for i in range(2)]
            hts = [_sb(k, s2, f"c_h2{i}", [128, D], F32) for i in range(2)]
            hos = [_sb(k, s2, f"c_ho{i}", [128, D], F32) for i in range(2)]
            load_weight_bf16(k, wout, d["w_out_c"], 8, D)
            S.op("pool", lambda e: e.memset(onesf[0][:], 1.0), writes=[onesf[1]])
            groups = [(g * 512, 512) for g in range(LP // 512)] + ([(LP // 512 * 512, LP % 512)] if LP % 512 else [])
            pairs = [tuple(range(j, min(j + 2, NT))) for j in range(0, NT, 2)]
            SBP = [(0, 1), (6, 7)]
            combo = 0
            pcount = 0
            tcnt = [0]
            pending = []
            for (q0, qn_) in groups:
                for h in range(8):
                    kv = h // 4
                    bo = 2 + 2 * (combo % 2)
                    bd = bo + 1
                    acc = accs[combo % 2]
                    combo += 1
                    qsl = QT[0][:, h, q0:q0 + qn_]
                    qres = [QTr[tt] for tt in range(q0 // 128, (q0 + qn_) // 128)]

                    def smm(pi, kv=kv, qsl=qsl, qres=qres):
                        pr = pairs[pi]
                        bp = SBP[pi % 2]

                        def f(e):
                            for idx, j in enumerate(pr):
                                i = e.matmul(out=bank(k, bp[idx], qn_), lhsT=KT[0][:, kv, j * 128:(j + 1) * 128], rhs=qsl,
                                             start=True, stop=True)
                            return i
                        S.op("pe", f, reads=[QTr[j] for j in pr] + qres, writes=[k.PB[bp[idx]] for idx in range(len(pr))])
                    smm(0)
                    used = [False, False]
                    pe_started = [False]
                    for pi, pr in enumerate(pairs):
                        n = len(pr)
                        bp = SBP[pi % 2]
                        P = Pt[pcount % 4]
                        pcount += 1
                        sv = k.ps[:, bp[0] * 512:(bp[0] + 2) * 512].rearrange("p (two n) -> p two n", two=2)[:, 0:n, 0:qn_]
                        pv_ = P[0][:, 0:n, 0:qn_]
                        S.op("act", lambda e, sv=sv, pv_=pv_: e.activation(out=pv_, in_=sv, func=AF.Exp),
                             reads=[k.PB[bp[idx]] for idx in range(n)], writes=[P[1]])
                        if pi + 1 < len(pairs):
                            smm(pi + 1)
                        if pi == min(2, len(pairs) - 1):
                            while pending:
                                pending.pop(0)()

                        def pvf(e, P=P, pr=pr):
                            for idx, j in enumerate(pr):
                                i = e.matmul(out=bank(k, bo, qn_), lhsT=V[0][:, j, kv * 128:(kv + 1) * 128], rhs=P[0][:, idx, 0:qn_],
                                             start=(j == 0), stop=(j == NT - 1))
                            return i
                        S.op("pe", pvf, reads=[P[1]] + [QTr[j] for j in pr], writes=[k.PB[bo]])
                        mode = pi % 3
                        if mode == 0:
                            def df(e, P=P, pr=pr):
                                for idx, j in enumerate(pr):
                                    om = k.ones0 if j == 0 else k.ones
                                    i = e.matmul(out=bank(k, bd, qn_), lhsT=om[0][:], rhs=P[0][:, idx, 0:qn_],
                                                 start=(not pe_started[0]), stop=False)
                                    pe_started[0] = True
                                return i
                            S.op("pe", df, reads=[P[1], k.ones[1], k.ones0[1]], writes=[k.PB[bd]])
                        else:
                            ai = mode - 1
                            ac = acc[ai]
                            eng = "dve" if ai == 0 else "pool"
                            if not used[ai]:
                                used[ai] = True
                                if n < 2:
                                    S.op(eng, lambda e, ac=ac: e.memset(ac[0][:], 0.0), writes=[ac[1]])
                                S.op(eng, lambda e, ac=ac, pv_=pv_: e.tensor_copy(out=ac[0][:, 0:n, 0:qn_], in_=pv_),
                                     reads=[P[1]], writes=[ac[1]])
                                if n == 2 and qn_ < 512:
                                    pass
                            else:
                                S.op(eng, lambda e, ac=ac, pv_=pv_: e.tensor_tensor(out=ac[0][:, 0:n, 0:qn_], in0=ac[0][:, 0:n, 0:qn_],
                                                                                   in1=pv_, op=ALU.add),
                                     reads=[P[1], ac[1]], writes=[ac[1]])
                    def make_tail(acc=acc, used=list(used), bo=bo, bd=bd, rd=rden[combo % 2], h=h, qn_=qn_):
                        def tail():
                            srcs = [a_ for ai, a_ in enumerate(acc) if used[ai]]
                            if len(srcs) == 2:
                                S.op("pool", lambda e: e.tensor_tensor(out=acc[0][0][:, :, 0:qn_], in0=acc[0][0][:, :, 0:qn_],
                                                                       in1=acc[1][0][:, :, 0:qn_], op=ALU.add),
                                     reads=[acc[0][1], acc[1][1]], writes=[acc[0][1]])
                            if srcs:
                                def ff(e):
                                    e.matmul(out=bank(k, bd, qn_), lhsT=onesf[0][:], rhs=srcs[0][0][:, 0, 0:qn_], start=False, stop=False)
                                    return e.matmul(out=bank(k, bd, qn_), lhsT=onesf[0][:], rhs=srcs[0][0][:, 1, 0:qn_], start=False, stop=True)
                                S.op("pe", ff, reads=[onesf[1], srcs[0][1]], writes=[k.PB[bd]])
                            S.op("dve", lambda e: e.reciprocal(out=rd[0][:, 0:qn_], in_=bank(k, bd, qn_)),
                                 reads=[k.PB[bd]], writes=[rd[1]])
                            S.op("dve", lambda e: e.tensor_tensor(out=OT[0][:, h, 0:qn_], in0=bank(k, bo, qn_),
                                                                  in1=rd[0][:, 0:qn_], op=ALU.mult),
                                 reads=[k.PB[bo], rd[1]], writes=[OTr[h]])
                        return tail
                    pending.append(make_tail())
                def make_outproj(q0=q0, qn_=qn_, bpar=combo % 2):
                    def outproj():
                        for tt in range(qn_ // 128):
                            t = q0 // 128 + tt
                            ht = hts[tcnt[0] % 2]
                            ho = hos[tcnt[0] % 2]
                            tcnt[0] += 1
                            S.dma("sp", ht[0][:], src(t), writes=[ht[1]])
                            for nb in range(2):
                                b = 3 + 2 * (1 - bpar)

                                def mm(e, tt=tt, nb=nb, b=b):
                                    for h in range(8):
                                        i = e.matmul(out=bank(k, b), lhsT=OT[0][:, h, tt * 128:(tt + 1) * 128],
                                                     rhs=wout[0][:, h, nb * 512:(nb + 1) * 512], start=(h == 0), stop=(h == 7))
                                    return i
                                S.op("pe", mm, reads=OTr + [wout[1]], writes=[k.PB[b]])
                                S.op("dve", lambda e, nb=nb, b=b, ht=ht, ho=ho: e.tensor_tensor(
                                    out=ho[0][:, nb * 512:(nb + 1) * 512], in0=bank(k, b), in1=ht[0][:, nb * 512:(nb + 1) * 512],
                                    op=ALU.add), reads=[k.PB[b], ht[1]], writes=[ho[1]])
                            S.dma("act", dst(t), ho[0][:], reads=[ho[1]])
                    return outproj
                pending.append(make_outproj())
            while pending:
                pending.pop(0)()
            S.barrier()


def input_specs():
  return [
    ("x", [SEQ, D]), ("meta_tokens", [NMETA, D]), ("attn_norm_g", [2, D]), ("mlp_norm_g", [2, D]),
    ("w_in_ab", [D, IN_AB]), ("conv_w_a", [5, 1536]), ("a_log", [2, 4]), ("dt_bias", [2, 4]),
    ("a_out_norm_g", [128]), ("b_q_norm_g", [64]), ("b_k_norm_g", [64]), ("b_sink", [8]),
    ("w_out_ab", [D, D]), ("w_qkv_c", [D, IN_C]), ("c_q_norm_g", [128]), ("c_k_norm_g", [128]),
    ("w_out_c", [D, D]), ("w_ff1", [2, D, DFF]), ("w_ff2", [2, DFF, D]),
    ("rope", [LP, 128]), ("padmask", [128, 1]), ("etab", [3, 2, 128, 512]), ("tri", [4, 128, 128]),
]


def build_nc(layers=(0, 1), dbg="cm", mlp_tiles=None, dbg0="vdom"):
    nc = bass.Bass("TRN2", target_bir_lowering=False)
    k = K()
    k.nc = nc
    d = {}
    for name, shape in input_specs():
        d[name] = nc.dram_tensor(name, shape, F32, kind="ExternalInput").ap()
    out = nc.dram_tensor("out", [SEQ, D], F32, kind="ExternalOutput").ap()
    d["scr_z"] = nc.dram_tensor("scr_z", [LP, 512], BF16, kind="ExternalOutput").ap()
    d["scr_yb"] = nc.dram_tensor("scr_yb", [LP, 512], BF16, kind="ExternalOutput").ap()
    d["scr_a"] = nc.dram_tensor("scr_a", [LP, 1536], BF16, kind="ExternalOutput").ap()
    d["scr_o"] = nc.dram_tensor("scr_o", [2, LP, 512], F32, kind="ExternalOutput").ap()
    d["scr_w1"] = nc.dram_tensor("scr_w1", [2, D, DFF], BF16, kind="ExternalOutput").ap()
    d["scr_w2"] = nc.dram_tensor("scr_w2", [2, DFF, D], BF16, kind="ExternalOutput").ap()
    H0s = nc.dram_tensor("H0s", [128, D], F32, kind="Internal").ap()
    k.d = d
    with contextlib.ExitStack() as st:
        S = Sched(nc, st)
        k.S = S
        k.ps = st.enter_context(nc.psum_tensor("ps", [128, 4096], F32))
        k.psb = k.ps.bitcast(BF16)
        k.PB = [Res(f"PB{i}") for i in range(8)]
        k.ident = _sb(k, st, "ident", [128, 128], BF16)
        k.ones = _sb(k, st, "ones", [128, 128], BF16)
        k.ones0 = _sb(k, st, "ones0", [128, 128], BF16)
        k.padmask = _sb(k, st, "padmask", [128, 1], F32)
        zt = _sb(k, st, "zt", [128, D], F32)
        S.dma("sp", k.padmask[0][:], d["padmask"], writes=[k.padmask[1]])
        S.op("pool", lambda e: e.memset(k.ident[0][:], 0.0), writes=[k.ident[1]])
        S.op("pool", lambda e: e.affine_select(out=k.ident[0][:], in_=k.ident[0][:], pattern=[[-1, 128]],
                                               compare_op=ALU.not_equal, fill=1.0, base=0, channel_multiplier=1),
             reads=[k.ident[1]], writes=[k.ident[1]])
        S.op("pool", lambda e: e.memset(k.ones[0][:], 1.0), writes=[k.ones[1]])
        S.op("dve", lambda e: e.tensor_scalar(out=k.ones0[0][:], in0=k.ones[0][:], scalar1=k.padmask[0][:, 0:1], scalar2=None,
                                              op0=ALU.mult), reads=[k.ones[1], k.padmask[1]], writes=[k.ones0[1]])
        S.op("pool", lambda e: e.memset(zt[0][:], 0.0), writes=[zt[1]])
        S.dma("sp", H0s[0:FP, :], zt[0][0:FP, :], reads=[zt[1]])
        S.dma("sp", H0s[FP:128, :], d["meta_tokens"])
        S.barrier()

        def xin(t):
            return H0s if t == 0 else d["x"][(t - 1) * 128:t * 128, :]

        def hbuf(t):
            return H0s if t == 0 else out[(t - 1) * 128:t * 128, :]
        cur = xin
        if 0 in layers:
            k.jobs = convert_jobs(k)
            with contextlib.ExitStack() as sl:
                uT_all, uTr, BETA, G = layer_ab_front(k, cur, 0, sl)
                if "v" in dbg0:
                    layer_ab_conv(k, uT_all, uTr)
                if "d" in dbg0:
                    layer_ab_delta(k, BETA, G)
            if "o" in dbg0:
                layer_ab_out(k, cur, hbuf)
            if "m" in dbg0:
                mlp_phase(k, hbuf, d["mlp_norm_g"][0:1, :], d["w_ff1"][0], d["w_ff2"][0], hbuf, list(range(NT)), l=0)
            cur = hbuf
        if 1 in layers:
            layer_c(k, cur, hbuf, 1)
            mlp_phase(k, hbuf, d["mlp_norm_g"][1:2, :], d["w_ff1"][1], d["w_ff2"][1], hbuf, mlp_tiles or list(range(1, NT)), l=1)
        S.finish()
    return nc


def rope_table():
    rows = SEQ // 64
    row = np.repeat(np.arange(rows), 64)
    col = np.tile(np.arange(64), rows)
    meta = np.arange(NMETA) - NMETA
    row = np.concatenate([meta, row]).astype(np.float32)
    col = np.concatenate([meta, col]).astype(np.float32)
    freqs = (np.float32(10000.0) ** (-np.arange(0, 64, 2, dtype=np.float32) / np.float32(64))).astype(np.float32)
    ang = np.concatenate([row[:, None] * freqs, col[:, None] * freqs], axis=-1).astype(np.float32)
    tab = np.zeros((LP, 128), np.float32)
    tab[:FP, :64] = 1.0
    tab[FP:, :64] = np.cos(ang)
    tab[FP:, 64:] = np.sin(ang)
    return tab


def const_inputs():
    pm = np.ones((128, 1), np.float32)
    pm[:FP] = 0.0
    r = np.arange(128)[:, None]
    c = np.arange(128)[None, :]
    et = np.zeros((3, 2, 128, 512), np.float32)
    for off in (-1, 0, 1):
        dist = np.abs(128 * off + r - c).astype(np.float32)
        for g in range(2):
            for a in range(4):
                h = 4 * g + a
                slope = np.float32(2.0) ** np.float32(-8.0 * (h + 1.0) / 8.0)
                et[off + 1, g, :, a * 128:(a + 1) * 128] = np.where(dist <= 128, np.exp(-slope * dist), 0.0)
    tri = np.stack([(r <= c), (r >= c), (r < c), (r > c)]).astype(np.float32)
    return {"rope": rope_table(), "padmask": pm, "etab": et, "tri": tri}


def make_in_maps(inputs, ncores=8):
    c = const_inputs()
    sq = lambda a: np.ascontiguousarray(np.asarray(a, dtype=np.float32))
    shared = {
        "meta_tokens": sq(inputs["meta_tokens"]), "attn_norm_g": sq(inputs["attn_norm_g"]),
        "mlp_norm_g": sq(inputs["mlp_norm_g"]), "w_in_ab": sq(inputs["w_in_ab"][0]),
        "conv_w_a": sq(inputs["conv_w_a"][0]), "a_log": sq(inputs["a_log"][0]), "dt_bias": sq(inputs["dt_bias"][0]),
        "a_out_norm_g": sq(inputs["a_out_norm_g"][0]), "b_q_norm_g": sq(inputs["b_q_norm_g"][0]),
        "b_k_norm_g": sq(inputs["b_k_norm_g"][0]), "b_sink": sq(inputs["b_sink"][0]),
        "w_out_ab": sq(inputs["w_out_ab"][0]), "w_qkv_c": sq(inputs["w_qkv_c"][0]),
        "c_q_norm_g": sq(inputs["c_q_norm_g"][0]), "c_k_norm_g": sq(inputs["c_k_norm_g"][0]),
        "w_out_c": sq(inputs["w_out_c"][0]), "w_ff1": sq(inputs["w_ff1"]), "w_ff2": sq(inputs["w_ff2"]),
    }
    shared.update(c)
    maps = []
    for b in range(ncores):
        m = dict(shared)
        m["x"] = sq(inputs["x"][b])
        maps.append(m)
    return maps


def kernel(**inputs):
    nc = build_nc()
    in_maps = make_in_maps(inputs, 8)
    res = run_bass_kernel_spmd(nc, in_maps, core_ids=list(range(8)))
    return np.stack([np.asarray(r["out"], dtype=np.float32) for r in res.results], axis=0)


NTOK = 1296


def layer_ab_front(k, src, l, st):
    S = k.S
    d = k.d
    uT_all = _sb(k, st, "a_uTall", [128, 8, LP], BF16)
    uTr = [Res(f"a_uT{t}") for t in range(NT)]
    BETA = _sb(k, st, "a_BETA", [128, NT, 8], F32)
    G = _sb(k, st, "a_G", [128, NT, 8], F32)
    with contextlib.ExitStack() as sB:
        qT_all = _sb(k, sB, "b_qT", [128, NT, 512], BF16)
        kT_all = _sb(k, sB, "b_kT", [128, LP], BF16)
        Vaug = _sb(k, sB, "b_Vaug", [128, NT, 2, 65], BF16)
        Vm = _sb(k, sB, "b_Vm", [16, 2, 65], BF16)
        Br = [Res(f"b_t{t}") for t in range(NT)]
        with contextlib.ExitStack() as s1:
            wt = _sb(k, s1, "a_wtok", [128, 8, NTOK], BF16)
            gt = _sb(k, s1, "a_gt", [128, D], F32)
            GB = _sb(k, s1, "a_GB", [128, 10, 64], F32)
            dtb = _sb(k, s1, "a_dtb", [128, 8], F32)
            negA = _sb(k, s1, "a_negA", [128, 8], F32)
            hts = [_sb(k, s1, f"a_ht{i}", [128, D], F32) for i in range(2)]
            szs = [_sb(k, s1, f"a_sz{i}", [128, 512], BF16) for i in range(2)]
            A2 = []
            for i in range(2):
                A2.append(dict(
                    ub=_sb(k, s1, f"a_ub{i}", [128, D], BF16), pj=_sb(k, s1, f"a_pj{i}", [128, NTOK], F32),
                    sq=_sb(k, s1, f"a_sq{i}", [128, 10, 64], F32), qkn=_sb(k, s1, f"a_qkn{i}", [128, 10, 64], BF16),
                    ssq=_sb(k, s1, f"a_ssq{i}", [128, 10], F32), ga=_sb(k, s1, f"a_ga{i}", [128, 8], F32),
                    ez=_sb(k, s1, f"a_ez{i}", [128, 512], F32),
                    stat=(_sb(k, s1, f"a_ss{i}", [128, 1], F32), _sb(k, s1, f"a_rs{i}", [128, 1], F32))))
            win = d["w_in_ab"]
            for c in range(8):
                rows = slice(c * 128, (c + 1) * 128)
                S.dma("pool", wt[0][:, c, 0:528], win[rows, 1536:2064], writes=[wt[1]])
                for par in range(2):
                    S.dma("pool", wt[0][:, c, 528:1040].rearrange("p (a par e) -> p a par e", a=4, par=2)[:, :, par, :],
                          win[rows, 2064 + par * 256:2064 + (par + 1) * 256].rearrange("p (a e) -> p a e", a=4), writes=[wt[1]])
                S.dma("pool", wt[0][:, c, 1040:1296], win[rows, 2576:2832], writes=[wt[1]])
            S.dma("sp", gt[0][:], d["attn_norm_g"][l:l + 1, :].partition_broadcast(128), writes=[gt[1]])
            S.dma("sp", GB[0][:, 0:8, :], d["b_q_norm_g"].partition_broadcast(128).unsqueeze(1).broadcast_to([128, 8, 64]),
                  writes=[GB[1]])
            S.dma("sp", GB[0][:, 8:10, :], d["b_k_norm_g"].partition_broadcast(128).unsqueeze(1).broadcast_to([128, 2, 64]),
                  writes=[GB[1]])
            S.op("act", lambda e: e.mul(out=GB[0][:, 0:8, :], in_=GB[0][:, 0:8, :], mul=64.0 ** -0.5),
                 reads=[GB[1]], writes=[GB[1]])
            S.dma("sp", dtb[0][:], d["dt_bias"].rearrange("a b -> (a b)").partition_broadcast(128), writes=[dtb[1]])
            S.dma("sp", negA[0][:], d["a_log"].rearrange("a b -> (a b)").partition_broadcast(128), writes=[negA[1]])
            S.op("act", lambda e: e.activation(out=negA[0][:], in_=negA[0][:], func=AF.Exp), reads=[negA[1]], writes=[negA[1]])
            S.op("act", lambda e: e.mul(out=negA[0][:], in_=negA[0][:], mul=-1.0), reads=[negA[1]], writes=[negA[1]])
            S.op("pool", lambda e: e.memset(Vaug[0][:], 1.0), writes=Br)
            S.op("pool", lambda e: e.memset(Vm[0][:], 1.0), writes=[Br[0]])
            jobs = getattr(k, "jobs", [])
            per = 1 if jobs else 0

            def body(S, t):
                for _ in range(per):
                    if jobs:
                        S.raw(jobs.pop(0))
                ht = hts[t % 2]
                ab_ = A2[t % 2]
                ub, pj, sq, qkn, ssq, ga, stat = (ab_[n] for n in ("ub", "pj", "sq", "qkn", "ssq", "ga", "stat"))
                tb = 0 if t % 2 == 0 else 6
                qb_ = 5 if t % 2 == 0 else 7
                S.dma("sp", ht[0][:], src(t), writes=[ht[1]])
                rms_to_uT(k, ht, gt, ub, stat, uT_all[0][:, :, t * 128:(t + 1) * 128], uTr[t], tb, S=S)
                for nb, (c0, c1) in enumerate([(0, 512), (512, 1024), (1024, NTOK)]):
                    def mm(e, nb=nb, c0=c0, c1=c1):
                        for c in range(8):
                            i = e.matmul(out=bank(k, 1 + nb, c1 - c0), lhsT=uT_all[0][:, c, t * 128:(t + 1) * 128],
                                         rhs=wt[0][:, c, c0:c1], start=(c == 0), stop=(c == 7))
                        return i
                    S.op("pe", mm, reads=[uTr[t], wt[1]], writes=[k.PB[1 + nb]])
                S.op("act", lambda e: e.activation(out=pj[0][:], in_=k.ps[:, 512:512 + NTOK], func=AF.Copy),
                     reads=[k.PB[1], k.PB[2], k.PB[3]], writes=[pj[1]])
                if t == 0:
                    def mmv(e):
                        for c in range(8):
                            i = e.matmul(out=k.ps[0:16, 2048:2176], lhsT=uT_all[0][:, c, FP:128], rhs=wt[0][:, c, 1168:1296],
                                         start=(c == 0), stop=(c == 7))
                        return i
                    S.op("pe", mmv, reads=[uTr[0], wt[1]], writes=[k.PB[4]])
                    S.op("act", lambda e: e.activation(out=Vm[0][:, :, 0:64], in_=k.ps[0:16, 2048:2176].rearrange("p (g e) -> p g e", g=2),
                                                       func=AF.Copy), reads=[k.PB[4]], writes=[Br[0]])
                sz = szs[t % 2]
                S.op("act", lambda e, sz=sz: e.activation(out=sz[0][:], in_=pj[0][:, 0:512], func=AF.Silu), reads=[pj[1]], writes=[sz[1]])
                S.dma("act", d["scr_z"][t * 128:(t + 1) * 128, :], sz[0][:], reads=[sz[1]])
                S.op("act", lambda e, t=t: e.activation(out=BETA[0][:, t, :], in_=pj[0][:, 512:520], func=AF.Exp, scale=-1.0),
                     reads=[pj[1]], writes=[BETA[1]])
                S.op("dve", lambda e, t=t: e.tensor_scalar(out=BETA[0][:, t, :], in0=BETA[0][:, t, :], scalar1=1.0, scalar2=None, op0=ALU.add),
                     reads=[BETA[1]], writes=[BETA[1]])
                S.op("dve", lambda e, t=t: e.reciprocal(out=BETA[0][:, t, :], in_=BETA[0][:, t, :]), reads=[BETA[1]], writes=[BETA[1]])
                S.op("dve", lambda e: e.tensor_tensor(out=ga[0][:], in0=pj[0][:, 520:528], in1=dtb[0][:], op=ALU.add),
                     reads=[pj[1], dtb[1]], writes=[ga[1]])
                S.op("act", lambda e: e.activation(out=ga[0][:], in_=ga[0][:], func=AF.Exp), reads=[ga[1]], writes=[ga[1]])
                S.op("act", lambda e: e.activation(out=ga[0][:], in_=ga[0][:], func=AF.Ln, bias=1.0), reads=[ga[1]], writes=[ga[1]])
                S.op("dve", lambda e, t=t: e.tensor_tensor(out=G[0][:, t, :], in0=ga[0][:], in1=negA[0][:], op=ALU.mult),
                     reads=[ga[1], negA[1]], writes=[G[1]])
                if t == 0:
                    S.op("dve", lambda e: e.tensor_scalar(out=BETA[0][:, 0, :], in0=BETA[0][:, 0, :], scalar1=k.padmask[0][:, 0:1],
                                                          scalar2=None, op0=ALU.mult), reads=[BETA[1], k.padmask[1]], writes=[BETA[1]])
                    S.op("dve", lambda e: e.tensor_scalar(out=G[0][:, 0, :], in0=G[0][:, 0, :], scalar1=k.padmask[0][:, 0:1],
                                                          scalar2=None, op0=ALU.mult), reads=[G[1], k.padmask[1]], writes=[G[1]])
                qk3 = pj[0][:, 528:1168].rearrange("p (h e) -> p h e", h=10)
                S.op("act", lambda e: e.activation(out=sq[0][:], in_=qk3, func=AF.Square), reads=[pj[1]], writes=[sq[1]])
                S.op("dve", lambda e: e.tensor_reduce(out=ssq[0][:], in_=sq[0][:], axis=AX.X, op=ALU.add), reads=[sq[1]], writes=[ssq[1]])
                S.op("act", lambda e: e.activation(out=ssq[0][:], in_=ssq[0][:], func=AF.Ln, scale=1.0 / 64, bias=EPS),
                     reads=[ssq[1]], writes=[ssq[1]])
                S.op("act", lambda e: e.activation(out=ssq[0][:], in_=ssq[0][:], func=AF.Exp, scale=-0.5), reads=[ssq[1]], writes=[ssq[1]])
                S.op("dve", lambda e: e.tensor_tensor(out=sq[0][:], in0=qk3, in1=ssq[0][:].unsqueeze(2).broadcast_to([128, 10, 64]),
                                                      op=ALU.mult), reads=[pj[1], ssq[1]], writes=[sq[1]])
                S.op("pool", lambda e: e.tensor_tensor(out=qkn[0][:], in0=sq[0][:], in1=GB[0][:], op=ALU.mult),
                     reads=[sq[1], GB[1]], writes=[qkn[1]])
                qkf = qkn[0][:].rearrange("p h e -> p (h e)")
                pa = bank_bf(k, qb_)

                def tr(e):
                    for a in range(5):
                        i = e.transpose(out=pa[:, a * 128:(a + 1) * 128], in_=qkf[:, a * 128:(a + 1) * 128], identity=k.ident[0][:])
                    return i
                S.op("pe", tr, reads=[qkn[1], k.ident[1]], writes=[k.PB[qb_]])
                S.op("act", lambda e, t=t: e.activation(out=qT_all[0][:, t, :], in_=pa[:, 0:512], func=AF.Copy),
                     reads=[k.PB[qb_]], writes=[Br[t]])
                S.op("dve", lambda e, t=t: e.tensor_copy(out=kT_all[0][:, t * 128:(t + 1) * 128], in_=pa[:, 512:640]),
                     reads=[k.PB[qb_]], writes=[Br[t]])
                S.op("pool", lambda e, t=t: e.tensor_copy(out=Vaug[0][:, t, :, 0:64],
                                                          in_=pj[0][:, 1168:1296].rearrange("p (g e) -> p g e", g=2)),
                     reads=[pj[1]], writes=[Br[t]])
            emit_skewed(k, [(lambda S, t=t: body(S, t)) for t in range(NT)])
            S.barrier()
        with contextlib.ExitStack() as s2:
            ET = _sb(k, s2, "b_E", [128, 6, 512], F32)
            esk = _sb(k, s2, "b_esk", [128, 8], F32)
            exs = [_sb(k, s2, f"b_ex{i}", [128, 512], F32) for i in range(2)]
            Pw = [[_sb(k, s2, f"b_Pw{g}{j}", [128, 512], BF16) for j in range(3)] for g in range(2)]
            Pm = [_sb(k, s2, f"b_Pm{g}", [16, 512], BF16) for g in range(2)]
            den = _sb(k, s2, "b_den", [128, 4], F32)
            ybs = [_sb(k, s2, f"b_yb{i}", [128, 8, 64], BF16) for i in range(2)]
            S.dma("sp", ET[0][:], d["etab"].rearrange("o g p n -> p (o g) n"), writes=[ET[1]])
            S.dma("sp", esk[0][:], d["b_sink"].partition_broadcast(128), writes=[esk[1]])
            S.op("act", lambda e: e.activation(out=esk[0][:], in_=esk[0][:], func=AF.Exp), reads=[esk[1]], writes=[esk[1]])
            sc = 0
            for i in range(NT):
                yb = ybs[i % 2]
                js = [j for j in (i - 1, i, i + 1) if 1 <= j <= NT - 1]
                for g in range(2):
                    prt = slice(64 * g, 64 * g + 64)
                    qs = qT_all[0][prt, i, :]
                    for ji, j in enumerate(js):
                        bs = sc % 2
                        ex = exs[sc % 2]
                        sc += 1
                        S.op("pe", lambda e, j=j, bs=bs: e.matmul(out=bank(k, bs), lhsT=kT_all[0][prt, j * 128:(j + 1) * 128], rhs=qs,
                                                                  start=True, stop=True), reads=[Br[j], Br[i]], writes=[k.PB[bs]])
                        S.op("act", lambda e, ex=ex, bs=bs: e.activation(out=ex[0][:], in_=bank(k, bs), func=AF.Exp),
                             reads=[k.PB[bs]], writes=[ex[1]])
                        S.op("dve", lambda e, ex=ex, ji=ji, j=j: e.tensor_tensor(out=Pw[g][ji][0][:], in0=ex[0][:],
                                                                                  in1=ET[0][:, (j - i + 1) * 2 + g, :], op=ALU.mult),
                             reads=[ex[1], ET[1]], writes=[Pw[g][ji][1]])
                    S.op("pe", lambda e: e.matmul(out=k.ps[0:16, 1024:1536], lhsT=kT_all[0][prt, FP:128], rhs=qs, start=True, stop=True),
                         reads=[Br[0], Br[i]], writes=[k.PB[2]])
                    S.op("act", lambda e: e.activation(out=Pm[g][0][:], in_=k.ps[0:16, 1024:1536], func=AF.Exp),
                         reads=[k.PB[2]], writes=[Pm[g][1]])
                    bo = 3 + g

                    def pv(e, g=g, bo=bo):
                        for a in range(4):
                            o = k.ps[:, bo * 512 + a * 65:bo * 512 + (a + 1) * 65]
                            for ji, j in enumerate(js):
                                e.matmul(out=o, lhsT=Pw[g][ji][0][:, a * 128:(a + 1) * 128], rhs=Vaug[0][:, j, g, :],
                                         start=(ji == 0), stop=False)
                            i_ = e.matmul(out=o, lhsT=Pm[g][0][:, a * 128:(a + 1) * 128], rhs=Vm[0][:, g, :],
                                          start=(len(js) == 0), stop=True)
                        return i_
                    S.op("pe", pv, reads=[Pw[g][ji][1] for ji in range(len(js))] + [Pm[g][1]] + [Br[j] for j in js] + [Br[0]],
                         writes=[k.PB[bo]])
                    o4 = k.ps[:, bo * 512:bo * 512 + 260].rearrange("p (a e) -> p a e", a=4)
                    S.op("dve", lambda e, g=g, o4=o4: e.tensor_tensor(out=den[0][:], in0=o4[:, :, 64], in1=esk[0][:, 4 * g:4 * g + 4],
                                                                      op=ALU.add), reads=[k.PB[bo], esk[1]], writes=[den[1]])
                    S.op("dve", lambda e: e.reciprocal(out=den[0][:], in_=den[0][:]), reads=[den[1]], writes=[den[1]])
                    S.op("dve", lambda e, g=g, o4=o4, yb=yb: e.tensor_tensor(out=yb[0][:, 4 * g:4 * g + 4, :], in0=o4[:, :, 0:64],
                                                                             in1=den[0][:].unsqueeze(2).broadcast_to([128, 4, 64]),
                                                                             op=ALU.mult), reads=[k.PB[bo], den[1]], writes=[yb[1]])
                S.dma("act", d["scr_yb"][i * 128:(i + 1) * 128, :], yb[0][:].rearrange("p h e -> p (h e)"), reads=[yb[1]])
            S.barrier()
    return uT_all, uTr, BETA, G


def tok_groups():
    g = [(i * 512, 512) for i in range(LP // 512)]
    if LP % 512:
        g.append((LP // 512 * 512, LP % 512))
    return g


def layer_ab_conv(k, uT_all, uTr):
    S = k.S
    d = k.d
    with contextlib.ExitStack() as st:
        wf = _sb(k, st, "v_wf", [128, 8, 1536], BF16)
        cw = _sb(k, st, "v_cw", [128, 12, 5], F32)
        raws = [_sb(k, st, f"v_raw{i}", [128, LP + 4], F32) for i in range(2)]
        acc = _sb(k, st, "v_acc", [128, LP], F32)
        acts = [_sb(k, st, f"v_act{i}", [128, LP], BF16) for i in range(2)]
        stg = _sb(k, st, "v_stg", [128, NT, 512], BF16)
        load_weight_bf16(k, wf, d["w_in_ab"][:, 0:1536], 8, 1536)
        for j in range(5):
            S.dma("sp", cw[0][:, :, j], d["conv_w_a"][j, :].rearrange("(c p) -> p c", p=128), writes=[cw[1]],
                  allow_slow_non_contiguous=True)
        for r in raws:
            S.op("pool", lambda e, r=r: e.memset(r[0][:], 0.0), writes=[r[1]])
        def proj(cc):
            rw = raws[cc % 2]
            for gi, (q0, qn_) in enumerate(tok_groups()):
                b = gi % 2

                def mm(e, q0=q0, qn_=qn_, b=b):
                    for c in range(8):
                        i = e.matmul(out=bank(k, b, qn_), lhsT=wf[0][:, c, cc * 128:(cc + 1) * 128], rhs=uT_all[0][:, c, q0:q0 + qn_],
                                     start=(c == 0), stop=(c == 7))
                    return i
                S.op("pe", mm, reads=[wf[1]] + [uTr[t] for t in range(q0 // 128, (q0 + qn_) // 128)], writes=[k.PB[b]])
                S.op("act", lambda e, q0=q0, qn_=qn_, b=b: e.activation(out=rw[0][:, 2 + q0:2 + q0 + qn_], in_=bank(k, b, qn_), func=AF.Copy),
                     reads=[k.PB[b]], writes=[rw[1]])
        def rest(cc):
            rw = raws[cc % 2]
            act = acts[cc % 2]
            S.op("dve", lambda e: e.tensor_scalar(out=acc[0][:], in0=rw[0][:, 0:LP], scalar1=cw[0][:, cc, 0:1], scalar2=None, op0=ALU.mult),
                 reads=[rw[1], cw[1]], writes=[acc[1]])
            for j in range(1, 5):
                S.op("dve", lambda e, j=j: e.scalar_tensor_tensor(out=acc[0][:], in0=rw[0][:, j:j + LP], scalar=cw[0][:, cc, j:j + 1],
                                                                  in1=acc[0][:], op0=ALU.mult, op1=ALU.add),
                     reads=[rw[1], cw[1], acc[1]], writes=[acc[1]])
            S.op("act", lambda e: e.activation(out=act[0][:], in_=acc[0][:], func=AF.Silu), reads=[acc[1]], writes=[act[1]])
            for t0 in range(0, NT, 8):
                nt_ = min(8, NT - t0)
                b = 2 + (t0 // 8) % 2
                pT = bank_bf(k, b)

                def tr(e, t0=t0, nt_=nt_, pT=pT):
                    for tt in range(nt_):
                        i = e.transpose(out=pT[:, tt * 128:(tt + 1) * 128], in_=act[0][:, (t0 + tt) * 128:(t0 + tt + 1) * 128],
                                        identity=k.ident[0][:])
                    return i
                S.op("pe", tr, reads=[act[1], k.ident[1]], writes=[k.PB[b]])
                S.op("act" if (t0 // 8) % 2 else "dve",
                     (lambda e, t0=t0, nt_=nt_, pT=pT: e.activation(out=stg[0][:, t0:t0 + nt_, (cc % 4) * 128:(cc % 4 + 1) * 128],
                                                                    in_=pT[:, 0:nt_ * 128].rearrange("p (t n) -> p t n", t=nt_), func=AF.Copy))
                     if (t0 // 8) % 2 else
                     (lambda e, t0=t0, nt_=nt_, pT=pT: e.tensor_copy(out=stg[0][:, t0:t0 + nt_, (cc % 4) * 128:(cc % 4 + 1) * 128],
                                                                     in_=pT[:, 0:nt_ * 128].rearrange("p (t n) -> p t n", t=nt_))),
                     reads=[k.PB[b]], writes=[stg[1]])
            if cc % 4 == 3:
                qi = cc // 4
                for t0 in range(0, NT, 8):
                    nt_ = min(8, NT - t0)
                    S.dma("sp", d["scr_a"][t0 * 128:(t0 + nt_) * 128, qi * 512:(qi + 1) * 512].rearrange("(t p) c -> p t c", p=128),
                          stg[0][:, t0:t0 + nt_, :], reads=[stg[1]])

        proj(0)
        for cc in range(12):
            if cc + 1 < 12:
                proj(cc + 1)
            rest(cc)
        S.barrier()


def layer_ab_delta(k, BETA, G):
    S = k.S
    d = k.d
    bctr = [0]

    def nb():
        b = bctr[0] % 8
        bctr[0] += 1
        return b

    with contextlib.ExitStack() as st:
        tri = _sb(k, st, "d_tri", [128, 4, 128], F32)
        identf = _sb(k, st, "d_identf", [128, 128], F32)
        onesf = _sb(k, st, "d_onesf", [128, 128], F32)
        S.dma("sp", tri[0][:], d["tri"].rearrange("m p n -> p m n"), writes=[tri[1]])
        S.op("dve", lambda e: e.tensor_tensor(out=identf[0][:], in0=tri[0][:, 0, :], in1=tri[0][:, 1, :], op=ALU.mult),
             reads=[tri[1]], writes=[identf[1]])
        S.op("pool", lambda e: e.memset(onesf[0][:], 1.0), writes=[onesf[1]])
        W = []
        for dd in range(2):
            w = {}
            def mk(name, shape, dt, dd=dd, w=w):
                w[name] = _sb(k, st, f"d_{name}{dd}", shape, dt)
            mk("At", [128, 1536], BF16)
            mk("sq", [128, 8, 128], F32)
            mk("ssq", [128, 8], F32)
            mk("ex12", [128, 12], F32)
            for nm in ("kn", "qn", "kg", "kd", "qd", "qdT", "qkT", "Rb", "wT", "vnew", "Sb", "Rb1", "Qb0", "Qb1", "QTb0", "QTb1"):
                mk(nm, [128, 4, 128], BF16)
            mk("kTq", [128, 8, 128], BF16)
            for nm in ("gSL", "dec", "decS", "decI", "tmp", "Bp", "Q0", "Q1", "QT1", "R0", "R1", "ub", "S32", "osb"):
                mk(nm, [128, 4, 128], F32)
            S.op("pool", lambda e, w=w: e.memset(w["S32"][0][:], 0.0), writes=[w["S32"][1]])
            S.op("pool", lambda e, w=w: e.memset(w["Sb"][0][:], 0.0), writes=[w["Sb"][1]])
            W.append(w)

        def b3(ap2):
            return ap2.unsqueeze(2).broadcast_to([128, 4, 128])

        def m3(ap2):
            return ap2.unsqueeze(1).broadcast_to([128, 4, 128])

        def bk4(b):
            return bank(k, b).rearrange("p (h n) -> p h n", h=4)

        class Rec:
            def __init__(self):
                self.ops = []

            def op(self, *a, **kw):
                self.ops.append(lambda: k.S.op(*a, **kw))

            def dma(self, *a, **kw):
                self.ops.append(lambda: k.S.dma(*a, **kw))

        def delta_tile(dd, n, S):
            w = W[dd]
            U = tri[0][:, 0, :] if dd == 0 else tri[0][:, 1, :]
            SL = tri[0][:, 3, :] if dd == 0 else tri[0][:, 2, :]
            MincT = U
            MstrT = tri[0][:, 2, :] if dd == 0 else tri[0][:, 3, :]
            g4 = G[0][:, n, 4 * dd:4 * dd + 4]
            b4 = BETA[0][:, n, 4 * dd:4 * dd + 4]
            At, sq, ssq, ex12 = w["At"], w["sq"], w["ssq"], w["ex12"]
            S.dma("sp", At[0][:], d["scr_a"][n * 128:(n + 1) * 128, :], writes=[At[1]])
            A3 = At[0][:].rearrange("p (h n) -> p h n", h=12)
            S.op("act", lambda e: e.activation(out=sq[0][:], in_=A3[:, 0:8, :], func=AF.Square), reads=[At[1]], writes=[sq[1]])
            S.op("dve", lambda e: e.tensor_reduce(out=ssq[0][:], in_=sq[0][:], axis=AX.X, op=ALU.add), reads=[sq[1]], writes=[ssq[1]])
            S.op("act", lambda e: e.activation(out=ssq[0][:], in_=ssq[0][:], func=AF.Ln, bias=EPS), reads=[ssq[1]], writes=[ssq[1]])
            S.op("act", lambda e: e.activation(out=ssq[0][:], in_=ssq[0][:], func=AF.Exp, scale=-0.5), reads=[ssq[1]], writes=[ssq[1]])
            S.op("dve", lambda e: e.tensor_scalar(out=ssq[0][:, 0:4], in0=ssq[0][:, 0:4], scalar1=128.0 ** -0.5, scalar2=None, op0=ALU.mult),
                 reads=[ssq[1]], writes=[ssq[1]])
            bg = nb()

            def mmg(e):
                e.matmul(out=k.ps[:, bg * 512:bg * 512 + 4], lhsT=U, rhs=g4, start=True, stop=True)
                return e.matmul(out=k.ps[:, bg * 512 + 4:bg * 512 + 8], lhsT=onesf[0][:], rhs=g4, start=True, stop=True)
            S.op("pe", mmg, reads=[tri[1], onesf[1], G[1]], writes=[k.PB[bg]])
            S.op("act", lambda e: e.activation(out=ex12[0][:, 0:8], in_=k.ps[:, bg * 512:bg * 512 + 8], func=AF.Copy),
                 reads=[k.PB[bg]], writes=[ex12[1]])
            S.op("dve", lambda e: e.tensor_tensor(out=ex12[0][:, 8:12], in0=ex12[0][:, 4:8], in1=ex12[0][:, 0:4], op=ALU.subtract),
                 reads=[ex12[1]], writes=[ex12[1]])
            S.op("act", lambda e: e.activation(out=ex12[0][:], in_=ex12[0][:], func=AF.Exp), reads=[ex12[1]], writes=[ex12[1]])
            eg, glast, ekd = ex12[0][:, 0:4], ex12[0][:, 4:8], ex12[0][:, 8:12]
            kn, qn, kg, kd, qd = w["kn"], w["qn"], w["kg"], w["kd"], w["qd"]
            S.op("dve", lambda e: e.tensor_tensor(out=qn[0][:], in0=A3[:, 0:4, :], in1=b3(ssq[0][:, 0:4]), op=ALU.mult),
                 reads=[At[1], ssq[1]], writes=[qn[1]])
            S.op("pool", lambda e: e.tensor_tensor(out=kn[0][:], in0=A3[:, 4:8, :], in1=b3(ssq[0][:, 4:8]), op=ALU.mult),
                 reads=[At[1], ssq[1]], writes=[kn[1]])
            S.op("dve", lambda e: e.tensor_tensor(out=kg[0][:], in0=kn[0][:], in1=b3(eg), op=ALU.mult), reads=[kn[1], ex12[1]], writes=[kg[1]])
            S.op("pool", lambda e: e.tensor_tensor(out=kd[0][:], in0=kn[0][:], in1=b3(ekd), op=ALU.mult), reads=[kn[1], ex12[1]], writes=[kd[1]])
            S.op("dve", lambda e: e.tensor_tensor(out=qd[0][:], in0=qn[0][:], in1=b3(eg), op=ALU.mult), reads=[qn[1], ex12[1]], writes=[qd[1]])
            bt1, bt2 = nb(), nb()
            p1, p2 = bank_bf(k, bt1), bank_bf(k, bt2)

            def tr1(e):
                for h in range(4):
                    e.transpose(out=p1[:, h * 128:(h + 1) * 128], in_=kn[0][:, h, :], identity=k.ident[0][:])
                for h in range(4):
                    i = e.transpose(out=p1[:, (4 + h) * 128:(5 + h) * 128], in_=qn[0][:, h, :], identity=k.ident[0][:])
                return i
            S.op("pe", tr1, reads=[kn[1], qn[1], k.ident[1]], writes=[k.PB[bt1]])

            def tr2(e):
                for h in range(4):
                    i = e.transpose(out=p2[:, h * 128:(h + 1) * 128], in_=qd[0][:, h, :], identity=k.ident[0][:])
                return i
            S.op("pe", tr2, reads=[qd[1], k.ident[1]], writes=[k.PB[bt2]])
            kTq, qdT = w["kTq"], w["qdT"]
            S.op("act", lambda e: e.activation(out=kTq[0][:], in_=p1.rearrange("p (h n) -> p h n", h=8), func=AF.Copy),
                 reads=[k.PB[bt1]], writes=[kTq[1]])
            S.op("act", lambda e: e.activation(out=qdT[0][:], in_=p2[:, 0:512].rearrange("p (h n) -> p h n", h=4), func=AF.Copy),
                 reads=[k.PB[bt2]], writes=[qdT[1]])
            gSL, dec, decS, decI = w["gSL"], w["dec"], w["decS"], w["decI"]
            S.op("pool", lambda e: e.tensor_tensor(out=gSL[0][:], in0=m3(SL), in1=b3(g4), op=ALU.mult), reads=[tri[1], G[1]], writes=[gSL[1]])
            bd = nb()

            def mmd(e):
                for h in range(4):
                    i = e.matmul(out=bank(k, bd, 128, h * 128), lhsT=gSL[0][:, h, :], rhs=U, start=True, stop=True)
                return i
            S.op("pe", mmd, reads=[gSL[1], tri[1]], writes=[k.PB[bd]])
            S.op("act", lambda e: e.activation(out=dec[0][:], in_=bk4(bd), func=AF.Exp), reads=[k.PB[bd]], writes=[dec[1]])
            S.op("dve", lambda e: e.tensor_tensor(out=decS[0][:], in0=dec[0][:], in1=m3(MstrT), op=ALU.mult), reads=[dec[1], tri[1]], writes=[decS[1]])
            S.op("pool", lambda e: e.tensor_tensor(out=decI[0][:], in0=dec[0][:], in1=m3(MincT), op=ALU.mult), reads=[dec[1], tri[1]], writes=[decI[1]])
            bkk, bkq = nb(), nb()

            def mmk(e):
                for h in range(4):
                    i = e.matmul(out=bank(k, bkk, 128, h * 128), lhsT=kTq[0][:, h, :], rhs=kTq[0][:, h, :], start=True, stop=True)
                return i
            S.op("pe", mmk, reads=[kTq[1]], writes=[k.PB[bkk]])

            def mmq(e):
                for h in range(4):
                    i = e.matmul(out=bank(k, bkq, 128, h * 128), lhsT=kTq[0][:, h, :], rhs=kTq[0][:, 4 + h, :], start=True, stop=True)
                return i
            S.op("pe", mmq, reads=[kTq[1]], writes=[k.PB[bkq]])
            tmp, Bp, qkT = w["tmp"], w["Bp"], w["qkT"]
            S.op("dve", lambda e: e.tensor_tensor(out=tmp[0][:], in0=bk4(bkk), in1=decS[0][:], op=ALU.mult), reads=[k.PB[bkk], decS[1]], writes=[tmp[1]])
            S.op("pool", lambda e: e.tensor_tensor(out=Bp[0][:], in0=tmp[0][:], in1=b3(b4), op=ALU.mult), reads=[tmp[1], BETA[1]], writes=[Bp[1]])
            S.op("dve", lambda e: e.tensor_tensor(out=qkT[0][:], in0=bk4(bkq), in1=decI[0][:], op=ALU.mult), reads=[k.PB[bkq], decI[1]], writes=[qkT[1]])
            Q = [w["Q0"], w["Q1"]]
            QT = [Bp, w["QT1"]]
            R = [w["R0"], w["R1"]]
            ba = nb()

            def tra(e):
                for h in range(4):
                    i = e.transpose(out=bank(k, ba, 128, h * 128), in_=Bp[0][:, h, :], identity=identf[0][:])
                return i
            S.op("pe", tra, reads=[Bp[1], identf[1]], writes=[k.PB[ba]])
            S.op("act", lambda e: e.activation(out=Q[0][0][:], in_=bk4(ba), func=AF.Copy), reads=[k.PB[ba]], writes=[Q[0][1]])
            S.op("pool", lambda e: e.tensor_tensor(out=R[0][0][:], in0=m3(identf[0][:]), in1=Bp[0][:], op=ALU.subtract),
                 reads=[identf[1], Bp[1]], writes=[R[0][1]])
            Qb = [w["Qb0"], w["Qb1"]]
            QTb = [w["QTb0"], w["QTb1"]]
            Rbb = [w["Rb"], w["Rb1"]]
            qc, qtc, rc = Q[0], QT[0], R[0]
            f32_q = [Q[1], Q[0]]
            f32_qt = [w["QT1"], w["tmp"]]
            f32_r = [R[1], R[0]]
            for lev in range(1, 7):
                lowp = lev >= 4
                b1 = nb()

                def mq(e, qc=qc, qtc=qtc, b1=b1):
                    for h in range(4):
                        i = e.matmul(out=bank(k, b1, 128, h * 128), lhsT=qtc[0][:, h, :], rhs=qc[0][:, h, :], start=True, stop=True)
                    return i
                S.op("pe", mq, reads=[qtc[1], qc[1]], writes=[k.PB[b1]])
                if lev < 6:
                    b2 = nb()

                    def mqt(e, qc=qc, qtc=qtc, b2=b2):
                        for h in range(4):
                            i = e.matmul(out=bank(k, b2, 128, h * 128), lhsT=qc[0][:, h, :], rhs=qtc[0][:, h, :], start=True, stop=True)
                        return i
                    S.op("pe", mqt, reads=[qtc[1], qc[1]], writes=[k.PB[b2]])
                qn_ = Qb[lev % 2] if lowp else f32_q[(lev - 1) % 2]
                S.op("act", lambda e, qn_=qn_, b1=b1: e.activation(out=qn_[0][:], in_=bk4(b1), func=AF.Copy), reads=[k.PB[b1]], writes=[qn_[1]])
                q_next = qn_
                if lev == 3:
                    q_next = Qb[1]
                    S.op("act", lambda e, q_next=q_next, b1=b1: e.activation(out=q_next[0][:], in_=bk4(b1), func=AF.Copy),
                         reads=[k.PB[b1]], writes=[q_next[1]])
                if lev < 6:
                    qt_new = QTb[lev % 2] if lev >= 3 else f32_qt[(lev - 1) % 2]
                    S.op("dve", lambda e, qt_new=qt_new, b2=b2: e.tensor_copy(out=qt_new[0][:], in_=bk4(b2)), reads=[k.PB[b2]], writes=[qt_new[1]])
                    qtc = qt_new
                b3_ = nb()

                def mr(e, qn_=qn_, rc=rc, b3_=b3_):
                    for h in range(4):
                        i = e.matmul(out=bank(k, b3_, 128, h * 128), lhsT=qn_[0][:, h, :], rhs=rc[0][:, h, :], start=True, stop=True)
                    return i
                S.op("pe", mr, reads=[qn_[1], rc[1]], writes=[k.PB[b3_]])
                rn = Rbb[lev % 2] if lev >= 3 else f32_r[(lev - 1) % 2]
                S.op("dve", lambda e, rc=rc, rn=rn, b3_=b3_: e.tensor_tensor(out=rn[0][:], in0=bk4(b3_), in1=rc[0][:], op=ALU.add),
                     reads=[k.PB[b3_], rc[1]], writes=[rn[1]])
                rc = rn
                qc = q_next
            Rb = rc
            ub_, wT = w["ub"], w["wT"]
            bu, bw = nb(), nb()

            def mu(e):
                for h in range(4):
                    i = e.matmul(out=bank(k, bu, 128, h * 128), lhsT=Rb[0][:, h, :], rhs=A3[:, 8 + h, :], start=True, stop=True)
                return i
            S.op("pe", mu, reads=[Rb[1], At[1]], writes=[k.PB[bu]])

            def mw(e):
                for h in range(4):
                    i = e.matmul(out=bank(k, bw, 128, h * 128), lhsT=kg[0][:, h, :], rhs=Rb[0][:, h, :], start=True, stop=True)
                return i
            S.op("pe", mw, reads=[Rb[1], kg[1]], writes=[k.PB[bw]])
            S.op("dve", lambda e: e.tensor_tensor(out=ub_[0][:], in0=bk4(bu), in1=b3(b4), op=ALU.mult), reads=[k.PB[bu], BETA[1]], writes=[ub_[1]])
            S.op("act", lambda e: e.activation(out=wT[0][:], in_=bk4(bw), func=AF.Copy), reads=[k.PB[bw]], writes=[wT[1]])
            S32, Sb, vnew, osb = w["S32"], w["Sb"], w["vnew"], w["osb"]
            tmp2 = w["dec"]
            b1 = nb()

            def m1(e):
                for h in range(4):
                    i = e.matmul(out=bank(k, b1, 128, h * 128), lhsT=wT[0][:, h, :], rhs=Sb[0][:, h, :], start=True, stop=True)
                return i
            S.op("pe", m1, reads=[wT[1], Sb[1]], writes=[k.PB[b1]])
            S.op("dve", lambda e: e.tensor_tensor(out=tmp2[0][:], in0=bk4(b1), in1=b3(b4), op=ALU.mult), reads=[k.PB[b1], BETA[1]], writes=[tmp2[1]])
            S.op("dve", lambda e: e.tensor_tensor(out=vnew[0][:], in0=ub_[0][:], in1=tmp2[0][:], op=ALU.subtract),
                 reads=[ub_[1], tmp2[1]], writes=[vnew[1]])
            b2, b3b = nb(), nb()

            def m2(e):
                for h in range(4):
                    e.matmul(out=bank(k, b2, 128, h * 128), lhsT=qdT[0][:, h, :], rhs=Sb[0][:, h, :], start=True, stop=False)
                    i = e.matmul(out=bank(k, b2, 128, h * 128), lhsT=qkT[0][:, h, :], rhs=vnew[0][:, h, :], start=False, stop=True)
                return i
            S.op("pe", m2, reads=[qdT[1], Sb[1], qkT[1], vnew[1]], writes=[k.PB[b2]])

            def m3_(e):
                for h in range(4):
                    i = e.matmul(out=bank(k, b3b, 128, h * 128), lhsT=kd[0][:, h, :], rhs=vnew[0][:, h, :], start=True, stop=True)
                return i
            S.op("pe", m3_, reads=[kd[1], vnew[1]], writes=[k.PB[b3b]])
            S.op("act", lambda e: e.activation(out=osb[0][:], in_=bk4(b2), func=AF.Copy), reads=[k.PB[b2]], writes=[osb[1]])
            S.dma("act", d["scr_o"][dd, n * 128:(n + 1) * 128, :], osb[0][:].rearrange("p h n -> p (h n)"), reads=[osb[1]])
            S.op("dve", lambda e: e.tensor_tensor(out=tmp2[0][:], in0=S32[0][:], in1=b3(glast), op=ALU.mult),
                 reads=[S32[1], ex12[1]], writes=[tmp2[1]])
            S.op("dve", lambda e: e.tensor_tensor(out=S32[0][:], in0=bk4(b3b), in1=tmp2[0][:], op=ALU.add),
                 reads=[k.PB[b3b], tmp2[1]], writes=[S32[1]])
            S.op("act", lambda e: e.activation(out=Sb[0][:], in_=S32[0][:], func=AF.Copy), reads=[S32[1]], writes=[Sb[1]])

        jobs = getattr(k, "jobs", [])
        for step in range(NT):
            ra, rb = Rec(), Rec()
            njob = (len(jobs) + (NT - step) - 1) // (NT - step) if jobs else 0
            for _ in range(njob):
                ra.ops.append(jobs.pop(0))
            delta_tile(0, step, ra)
            delta_tile(1, NT - 1 - step, rb)
            for i in range(max(len(ra.ops), len(rb.ops))):
                if i < len(ra.ops):
                    ra.ops[i]()
                if i < len(rb.ops):
                    rb.ops[i]()
        S.barrier()


def layer_ab_out(k, src, dst):
    S = k.S
    d = k.d
    with contextlib.ExitStack() as st:
        wout = _sb(k, st, "o_wout", [128, 8, D], BF16)
        GO = _sb(k, st, "o_GO", [128, 128], F32)
        ofs = [_sb(k, st, f"o_of{i}", [128, 4, 128], F32) for i in range(2)]
        obs = [_sb(k, st, f"o_ob{i}", [128, 4, 128], F32) for i in range(2)]
        szs = [_sb(k, st, f"o_sz{i}", [128, 4, 128], BF16) for i in range(2)]
        mixs = [_sb(k, st, f"o_mix{i}", [128, D], BF16) for i in range(2)]
        hts = [_sb(k, st, f"o_ht{i}", [128, D], F32) for i in range(2)]
        hos = [_sb(k, st, f"o_ho{i}", [128, D], F32) for i in range(2)]
        sqs = [_sb(k, st, f"o_sq{i}", [128, 4, 128], F32) for i in range(2)]
        ssqs = [_sb(k, st, f"o_ssq{i}", [128, 4], F32) for i in range(2)]
        mixTs = [_sb(k, st, f"o_mixT{i}", [128, 8, 128], BF16) for i in range(2)]
        load_weight_bf16(k, wout, d["w_out_ab"], 8, D)
        S.dma("sp", GO[0][:], d["a_out_norm_g"].partition_broadcast(128), writes=[GO[1]])
        def body(S, t):
            of, ob, sz, mix, ht, ho = ofs[t % 2], obs[t % 2], szs[t % 2], mixs[t % 2], hts[t % 2], hos[t % 2]
            sq, ssq, mixT = sqs[t % 2], ssqs[t % 2], mixTs[t % 2]
            tbk = 0 if t % 2 == 0 else 5
            rows = slice(t * 128, (t + 1) * 128)
            S.dma("sp", of[0][:].rearrange("p h n -> p (h n)"), d["scr_o"][0, rows, :], writes=[of[1]])
            S.dma("sp", ob[0][:].rearrange("p h n -> p (h n)"), d["scr_o"][1, rows, :], writes=[ob[1]])
            S.dma("sp", sz[0][:].rearrange("p h n -> p (h n)"), d["scr_z"][rows, :], writes=[sz[1]])
            S.dma("sp", mix[0][:, 512:1024], d["scr_yb"][rows, :], writes=[mix[1]])
            S.dma("sp", ht[0][:], src(t), writes=[ht[1]])
            S.op("dve", lambda e: e.tensor_tensor(out=of[0][:], in0=of[0][:], in1=ob[0][:], op=ALU.add), reads=[of[1], ob[1]], writes=[of[1]])
            S.op("act", lambda e: e.activation(out=sq[0][:], in_=of[0][:], func=AF.Square), reads=[of[1]], writes=[sq[1]])
            S.op("dve", lambda e: e.tensor_reduce(out=ssq[0][:], in_=sq[0][:], axis=AX.X, op=ALU.add), reads=[sq[1]], writes=[ssq[1]])
            S.op("act", lambda e: e.activation(out=ssq[0][:], in_=ssq[0][:], func=AF.Ln, scale=1.0 / 128, bias=EPS), reads=[ssq[1]], writes=[ssq[1]])
            S.op("act", lambda e: e.activation(out=ssq[0][:], in_=ssq[0][:], func=AF.Exp, scale=-0.5), reads=[ssq[1]], writes=[ssq[1]])
            S.op("dve", lambda e: e.tensor_tensor(out=sq[0][:], in0=of[0][:], in1=ssq[0][:].unsqueeze(2).broadcast_to([128, 4, 128]), op=ALU.mult),
                 reads=[of[1], ssq[1]], writes=[sq[1]])
            S.op("pool", lambda e: e.tensor_tensor(out=sq[0][:], in0=sq[0][:], in1=GO[0][:].unsqueeze(1).broadcast_to([128, 4, 128]), op=ALU.mult),
                 reads=[sq[1], GO[1]], writes=[sq[1]])
            S.op("dve", lambda e: e.tensor_tensor(out=mix[0][:, 0:512].rearrange("p (h n) -> p h n", h=4), in0=sq[0][:], in1=sz[0][:], op=ALU.mult),
                 reads=[sq[1], sz[1]], writes=[mix[1]])
            pT = bank_bf(k, tbk)

            def tr(e):
                for c in range(8):
                    i = e.transpose(out=pT[:, c * 128:(c + 1) * 128], in_=mix[0][:, c * 128:(c + 1) * 128], identity=k.ident[0][:])
                return i
            S.op("pe", tr, reads=[mix[1], k.ident[1]], writes=[k.PB[tbk]])
            S.op("act", lambda e: e.activation(out=mixT[0][:], in_=pT.rearrange("p (c n) -> p c n", c=8), func=AF.Copy),
                 reads=[k.PB[tbk]], writes=[mixT[1]])
            for nbk in range(2):
                b = 1 + nbk + 2 * (t % 2)

                def mm(e, nbk=nbk, b=b):
                    for c in range(8):
                        i = e.matmul(out=bank(k, b), lhsT=mixT[0][:, c, :], rhs=wout[0][:, c, nbk * 512:(nbk + 1) * 512],
                                     start=(c == 0), stop=(c == 7))
                    return i
                S.op("pe", mm, reads=[mixT[1], wout[1]], writes=[k.PB[b]])
                S.op("dve", lambda e, nbk=nbk, b=b: e.tensor_tensor(out=ho[0][:, nbk * 512:(nbk + 1) * 512], in0=bank(k, b),
                                                                    in1=ht[0][:, nbk * 512:(nbk + 1) * 512], op=ALU.add),
                     reads=[k.PB[b], ht[1]], writes=[ho[1]])
            if t == 0:
                S.op("dve", lambda e: e.tensor_scalar(out=ho[0][:], in0=ho[0][:], scalar1=k.padmask[0][:, 0:1], scalar2=None, op0=ALU.mult),
                     reads=[ho[1], k.padmask[1]], writes=[ho[1]])
            S.dma("act", dst(t), ho[0][:], reads=[ho[1]])
        emit_skewed(k, [(lambda S, t=t: body(S, t)) for t in range(NT)])
        S.barrier()
```

```python
import contextlib
import numpy as np
import concourse.bass as bass
import concourse.mybir as mybir
from concourse.bass_utils import run_bass_kernel_spmd

F32 = mybir.dt.float32
BF16 = mybir.dt.bfloat16
AF = mybir.ActivationFunctionType
ALU = mybir.AluOpType
AX = mybir.AxisListType

D = 1024
SEQ = 4096
NMETA = 16
FP = 112
LP = 4224
NT = 33
HSMUL = 1
DFF = 4096
EPS = 1e-6
IN_AB = 2832
IN_C = 1536


class Res:
    __slots__ = ("name", "w", "r")

    def __init__(self, name):
        self.name = name
        self.w = None
        self.r = []


class Sched:
    NSLOT = 12

    def __init__(self, nc, stack):
        self.nc = nc
        self.eng = {"pe": nc.tensor, "act": nc.scalar, "dve": nc.vector,
                    "pool": nc.gpsimd, "sp": nc.sync}
        self.count = {e: 0 for e in self.eng}
        self.waited = {e: {} for e in self.eng}
        self.sem = {}
        for e in self.eng:
            self.sem[e] = stack.enter_context(nc.semaphore("s_" + e))
        self.dslot = {}
        for q in ("sp", "act", "pool"):
            for s in range(self.NSLOT):
                self.sem[("d", q, s)] = stack.enter_context(nc.semaphore(f"d_{q}_{s}"))
                self.dslot[(q, s)] = 0
        self.dnext = {q: 0 for q in ("sp", "act", "pool")}
        self.n_ops = 0

    def _deps(self, reads, writes):
        deps = []
        for r in reads:
            if r.w is not None:
                deps.append(r.w)
        for w in writes:
            if w.w is not None:
                deps.append(w.w)
            deps.extend(w.r)
        return deps

    def _waits(self, e, deps):
        out = []
        wd = self.waited[e]
        best = {}
        for (k, v) in deps:
            if wd.get(k, 0) >= v or (e == "pe" and k == "pe"):
                continue
            if best.get(k, 0) < v:
                best[k] = v
        for k, v in best.items():
            wd[k] = v
            out.append((k, v))
        return out

    def _mark(self, ev, reads, writes):
        for r in reads:
            r.r.append(ev)
        for w in writes:
            w.w = ev
            w.r = []

    def _run(self, e, waits, fn, sig):
        eng = self.eng[e]
        for (k, v) in waits:
            eng.wait_ge(self.sem[k], v)
        if fn is None:
            return
        ins = fn(eng)
        ins.then_inc(self.sem[sig[0]], sig[1])

    def op(self, e, fn, reads=(), writes=()):
        waits = self._waits(e, self._deps(reads, writes))
        self.count[e] += 1
        ev = (e, self.count[e])
        self._run(e, waits, fn, (e, 1))
        self._mark(ev, reads, writes)
        self.n_ops += 1
        return ev

    def dma(self, q, out, in_, reads=(), writes=(), **kw):
        s = self.dnext[q]
        self.dnext[q] = (s + 1) % self.NSLOT
        key = ("d", q, s)
        deps = self._deps(reads, writes)
        prev = self.dslot[(q, s)]
        if prev:
            deps.append((key, prev))
        waits = self._waits(q, deps)
        val = prev + 16
        self.dslot[(q, s)] = val
        ev = (key, val)
        self._run(q, waits, lambda eng: eng.dma_start(out=out, in_=in_, **kw), (key, 16))
        self._mark(ev, reads, writes)
        self.n_ops += 1
        return ev

    def _all_events(self):
        deps = []
        for (q, s), v in self.dslot.items():
            if v:
                deps.append((("d", q, s), v))
        for e in ("pe", "act", "dve", "pool"):
            if self.count[e]:
                deps.append((e, self.count[e]))
        return deps

    def barrier(self):
        deps = self._all_events()
        for e in self.eng:
            self._run(e, self._waits(e, deps), None, None)

    def finish(self):
        self._run("sp", self._waits("sp", self._all_events()), None, None)


class K:
    pass


class Rec:
    def __init__(self, k):
        self.k = k
        self.ops = []

    def op(self, *a, **kw):
        self.ops.append(lambda: self.k.S.op(*a, **kw))

    def dma(self, *a, **kw):
        self.ops.append(lambda: self.k.S.dma(*a, **kw))

    def raw(self, fn):
        self.ops.append(fn)


def emit_lockstep(k, bodies, d=2):
    for g0 in range(0, len(bodies), d):
        recs = []
        for b in bodies[g0:g0 + d]:
            r = Rec(k)
            b(r)
            recs.append(r.ops)
        for p in range(max(len(o) for o in recs)):
            for ops in recs:
                if p < len(ops):
                    ops[p]()


def emit_skewed(k, bodies, depth=2):
    recs = []
    for b in bodies:
        r = Rec(k)
        b(r)
        recs.append(r.ops)
    if not recs:
        return
    L = max(len(o) for o in recs)
    step = max(1, (L + depth - 1) // depth)
    items = []
    for t, ops in enumerate(recs):
        for p, f in enumerate(ops):
            items.append((t * step + p, t, f))
    items.sort(key=lambda x: (x[0], x[1]))
    for it in items:
        it[2]()


def _sb(k, st, name, shape, dt):
    k.uid = getattr(k, "uid", 0) + 1
    t = st.enter_context(k.nc.sbuf_tensor(f"sb{k.uid}_{name}", shape, dt))
    return t, Res(name)


def bank(k, b, n=512, off=0):
    return k.ps[:, b * 512 + off:b * 512 + off + n]


def bank_bf(k, b):
    return k.psb[:, b * 1024:(b + 1) * 1024]


def load_weight_bf16(k, dst, src, kchunks, ncols):
    S = k.S
    step = min(ncols, 2048)
    for c in range(kchunks):
        for n0 in range(0, ncols, step):
            n1 = min(ncols, n0 + step)
            S.dma("pool", dst[0][:, c, n0:n1], src[c * 128:(c + 1) * 128, n0:n1], writes=[dst[1]])


def rms_to_uT(k, ht, gt, ub, stat, uT_ap, uT_res, tbank, pad0=False, S=None):
    S = S or k.S
    ss, rs = stat
    S.op("act", lambda e: e.activation(out=ub[0][:], in_=ht[0][:], func=AF.Square, accum_out=ss[0][:]),
         reads=[ht[1]], writes=[ub[1], ss[1]])
    S.op("act", lambda e: e.activation(out=rs[0][:], in_=ss[0][:], func=AF.Ln, scale=1.0 / D, bias=EPS),
         reads=[ss[1]], writes=[rs[1]])
    S.op("act", lambda e: e.activation(out=rs[0][:], in_=rs[0][:], func=AF.Exp, scale=-0.5), reads=[rs[1]], writes=[rs[1]])
    S.op("dve", lambda e: e.scalar_tensor_tensor(out=ub[0][:], in0=ht[0][:], scalar=rs[0][:, 0:1], in1=gt[0][:],
                                                 op0=ALU.mult, op1=ALU.mult),
         reads=[ht[1], rs[1], gt[1]], writes=[ub[1]])
    pT = bank_bf(k, tbank)

    def tr(e):
        for c in range(8):
            i = e.transpose(out=pT[:, c * 128:(c + 1) * 128], in_=ub[0][:, c * 128:(c + 1) * 128],
                            identity=k.ident[0][:])
        return i
    S.op("pe", tr, reads=[ub[1], k.ident[1]], writes=[k.PB[tbank]])
    S.op("act", lambda e: e.activation(out=uT_ap, in_=pT.rearrange("p (c n) -> p c n", c=8), func=AF.Copy),
         reads=[k.PB[tbank]], writes=[uT_res])


def convert_jobs(k):
    S = k.S
    d = k.d
    k.wres = {(l, i): Res(f"scrw{l}{i}") for l in range(2) for i in range(2)}
    jobs = []
    for l in range(2):
        for c in range(8):
            for n0 in range(0, DFF, 2048):
                jobs.append(lambda l=l, c=c, n0=n0: S.dma("pool", d["scr_w1"][l, c * 128:(c + 1) * 128, n0:n0 + 2048],
                                                         d["w_ff1"][l, c * 128:(c + 1) * 128, n0:n0 + 2048], writes=[k.wres[(l, 0)]]))
        for f in range(0, 32, 2):
            jobs.append(lambda l=l, f=f: S.dma("pool", d["scr_w2"][l, f * 128:(f + 2) * 128, :].rearrange("(a p) n -> p a n", p=128),
                                               d["w_ff2"][l, f * 128:(f + 2) * 128, :].rearrange("(a p) n -> p a n", p=128),
                                               writes=[k.wres[(l, 1)]]))
    return jobs


def mlp_phase(k, src, g_row, w1d, w2d, out_fn, tiles, l=None):
    S = k.S
    nc = k.nc
    with contextlib.ExitStack() as st:
        w1 = _sb(k, st, "m_w1", [128, 8, DFF], BF16)
        w2 = _sb(k, st, "m_w2", [128, 32, D], BF16)
        gt = _sb(k, st, "m_gt", [128, D], F32)
        hts = [[_sb(k, st, f"m_ht{a}{b}", [128, D], F32) for b in range(2)] for a in range(2)]
        hos = [_sb(k, st, f"m_ho{b}", [128, D], F32) for b in range(2)]
        ub = _sb(k, st, "m_ub", [128, D], BF16)
        uTs = [_sb(k, st, f"m_uT{a}", [128, 8, 256], BF16) for a in range(2)]
        aT = _sb(k, st, "m_aT", [128, 32, 256], BF16)
        aTr = [Res(f"m_aT{f}") for f in range(32)]
        rr = [_sb(k, st, f"m_r{i}", [128, 512], F32) for i in range(3)]
        stat = (_sb(k, st, "m_ss", [128, 1], F32), _sb(k, st, "m_rs", [128, 1], F32))
        S.dma("sp", gt[0][:], g_row.partition_broadcast(128), writes=[gt[1]])
        if l is not None and getattr(k, "wres", None):
            for c in range(8):
                S.dma("sp" if c % 2 == 0 else "act", w1[0][:, c, :], k.d["scr_w1"][l, c * 128:(c + 1) * 128, :],
                      reads=[k.wres[(l, 0)]], writes=[w1[1]])
            for f in range(0, 32, 4):
                S.dma("sp" if (f // 4) % 2 == 0 else "act", w2[0][:, f:f + 4, :],
                      k.d["scr_w2"][l, f * 128:(f + 4) * 128, :].rearrange("(a p) n -> p a n", p=128),
                      reads=[k.wres[(l, 1)]], writes=[w2[1]])
        else:
            load_weight_bf16(k, w1, w1d, 8, DFF)
            load_weight_bf16(k, w2, w2d, 32, D)
        groups = [tiles[i:i + 2] for i in range(0, len(tiles), 2)]
        hb = [Res(f"m_hb{i}") for i in range(4)]

        def prep(gi):
            grp = groups[gi]
            a = gi % 2
            for j, t in enumerate(grp):
                ht = hts[a][j]
                S.dma("sp", ht[0][:], src(t), writes=[ht[1]])
                rms_to_uT(k, ht, gt, ub, stat, uTs[a][0][:, :, j * 128:(j + 1) * 128], uTs[a][1], 0)

        def ff1(gi):
            grp = groups[gi]
            a = gi % 2
            n = 128 * len(grp)
            for fp in range(16):
                b = 1 + fp % 2
                pa = bank(k, b).rearrange("p (two n) -> p two n", two=2)[:, :, 0:n]

                def mm(e, fp=fp, b=b):
                    for ff in range(2):
                        f = 2 * fp + ff
                        for c in range(8):
                            i = e.matmul(out=bank(k, b, n, ff * 256), lhsT=w1[0][:, c, f * 128:(f + 1) * 128],
                                         rhs=uTs[a][0][:, c, 0:n], start=(c == 0), stop=(c == 7))
                    return i
                S.op("pe", mm, reads=[w1[1], uTs[a][1]], writes=[k.PB[b]])
                r = rr[fp % 3]
                rv = r[0][:].rearrange("p (two n) -> p two n", two=2)[:, :, 0:n]
                S.op("act", lambda e, pa=pa, rv=rv: e.activation(out=rv, in_=pa, func=AF.Relu),
                     reads=[k.PB[b]], writes=[r[1]])
                S.op("pool", lambda e, rv=rv, fp=fp: e.tensor_tensor(out=aT[0][:, 2 * fp:2 * fp + 2, 0:n], in0=rv, in1=rv,
                                                                   op=ALU.mult),
                     reads=[r[1]], writes=[aTr[2 * fp], aTr[2 * fp + 1]])

        def ff2(gi):
            grp = groups[gi]
            a = gi % 2
            for j, t in enumerate(grp):
                for nb in range(2):
                    b = 3 + 2 * j + nb

                    def mm(e, j=j, nb=nb, b=b):
                        for f in range(32):
                            i = e.matmul(out=bank(k, b), lhsT=aT[0][:, f, j * 128:(j + 1) * 128],
                                         rhs=w2[0][:, f, nb * 512:(nb + 1) * 512], start=(f == 0), stop=(f == 31))
                        return i
                    S.op("pe", mm, reads=[w2[1]] + aTr, writes=[k.PB[b]])
                    S.op("dve", lambda e, j=j, nb=nb, b=b: e.tensor_tensor(
                        out=hos[j][0][:, nb * 512:(nb + 1) * 512], in0=bank(k, b),
                        in1=hts[a][j][0][:, nb * 512:(nb + 1) * 512], op=ALU.add),
                        reads=[k.PB[b], hts[a][j][1]], writes=[hos[j][1]])
                S.dma("act", out_fn(t), hos[j][0][:], reads=[hos[j][1]])

        prep(0)
        for gi in range(len(groups)):
            ff1(gi)
            if gi + 1 < len(groups):
                prep(gi + 1)
            ff2(gi)
        S.barrier()


def layer_c(k, src, dst, l):
    S = k.S
    d = k.d
    with contextlib.ExitStack() as st:
        QT = _sb(k, st, "c_QT", [128, 8, LP], BF16)
        KT = _sb(k, st, "c_KT", [128, 2, LP], BF16)
        V = _sb(k, st, "c_V", [128, NT, 256], BF16)
        QTr = [Res(f"c_QT{t}") for t in range(NT)]
        with contextlib.ExitStack() as s1:
            wqkv = _sb(k, s1, "c_wqkv", [128, 8, IN_C], BF16)
            gt = _sb(k, s1, "c_gt", [128, D], F32)
            GQK = _sb(k, s1, "c_GQK", [128, 10, 128], F32)
            hts = [_sb(k, s1, f"c_ht{i}", [128, D], F32) for i in range(2)]
            css = [_sb(k, s1, f"c_cs{i}", [128, 128], F32) for i in range(2)]
            B2 = []
            for i in range(2):
                B2.append(dict(
                    ub=_sb(k, s1, f"c_ub{i}", [128, D], BF16), uT=_sb(k, s1, f"c_uT{i}", [128, 8, 128], BF16),
                    qkv=_sb(k, s1, f"c_qkv{i}", [128, IN_C], F32), sq=_sb(k, s1, f"c_sq{i}", [128, 10, 128], F32),
                    t1=_sb(k, s1, f"c_t1{i}", [128, 10, 64], F32), t2=_sb(k, s1, f"c_t2{i}", [128, 10, 64], F32),
                    qr=_sb(k, s1, f"c_qr{i}", [128, 10, 64, 2], BF16), ssq=_sb(k, s1, f"c_ssq{i}", [128, 10], F32),
                    stat=(_sb(k, s1, f"c_ss{i}", [128, 1], F32), _sb(k, s1, f"c_rs{i}", [128, 1], F32))))
            load_weight_bf16(k, wqkv, d["w_qkv_c"], 8, IN_C)
            S.dma("sp", gt[0][:], d["attn_norm_g"][l:l + 1, :].partition_broadcast(128), writes=[gt[1]])
            S.dma("sp", GQK[0][:, 0:8, :], d["c_q_norm_g"].partition_broadcast(128).unsqueeze(1).broadcast_to([128, 8, 128]),
                  writes=[GQK[1]])
            S.dma("sp", GQK[0][:, 8:10, :], d["c_k_norm_g"].partition_broadcast(128).unsqueeze(1).broadcast_to([128, 2, 128]),
                  writes=[GQK[1]])
            S.op("act", lambda e: e.mul(out=GQK[0][:, 0:8, :], in_=GQK[0][:, 0:8, :], mul=128.0 ** -0.5),
                 reads=[GQK[1]], writes=[GQK[1]])
            def body(S, t):
                ht = hts[t % 2]
                cs = css[t % 2]
                bb = B2[t % 2]
                ub, uT, qkv, sq, t1, t2, qr, ssq, stat = (bb[n] for n in ("ub", "uT", "qkv", "sq", "t1", "t2", "qr", "ssq", "stat"))
                qn = sq
                tb = 0 if t % 2 == 0 else 6
                qb_ = 4 if t % 2 == 0 else 7
                S.dma("sp", ht[0][:], src(t), writes=[ht[1]])
                S.dma("sp", cs[0][:], d["rope"][t * 128:(t + 1) * 128, :], writes=[cs[1]])
                rms_to_uT(k, ht, gt, ub, stat, uT[0][:], uT[1], tb, S=S)
                for nb in range(3):
                    def mm(e, nb=nb):
                        for c in range(8):
                            i = e.matmul(out=bank(k, 1 + nb), lhsT=uT[0][:, c, :], rhs=wqkv[0][:, c, nb * 512:(nb + 1) * 512],
                                         start=(c == 0), stop=(c == 7))
                        return i
                    S.op("pe", mm, reads=[uT[1], wqkv[1]], writes=[k.PB[1 + nb]])
                S.op("act", lambda e: e.activation(out=qkv[0][:], in_=k.ps[:, 512:2048], func=AF.Copy),
                     reads=[k.PB[1], k.PB[2], k.PB[3]], writes=[qkv[1]])
                if t == 0:
                    S.op("dve", lambda e: e.tensor_scalar(out=V[0][:, t, :], in0=qkv[0][:, 1280:1536], scalar1=k.padmask[0][:, 0:1],
                                                          scalar2=None, op0=ALU.mult),
                         reads=[qkv[1], k.padmask[1]], writes=[QTr[t]])
                else:
                    S.op("pool", lambda e, t=t: e.tensor_copy(out=V[0][:, t, :], in_=qkv[0][:, 1280:1536]),
                         reads=[qkv[1]], writes=[QTr[t]])
                qk3 = qkv[0][:, 0:1280].rearrange("p (h d) -> p h d", h=10)
                S.op("act", lambda e: e.activation(out=sq[0][:], in_=qk3, func=AF.Square), reads=[qkv[1]], writes=[sq[1]])
                S.op("dve", lambda e: e.tensor_reduce(out=ssq[0][:], in_=sq[0][:], axis=AX.X, op=ALU.add),
                     reads=[sq[1]], writes=[ssq[1]])
                S.op("act", lambda e: e.activation(out=ssq[0][:], in_=ssq[0][:], func=AF.Ln, scale=1.0 / 128, bias=EPS),
                     reads=[ssq[1]], writes=[ssq[1]])
                S.op("act", lambda e: e.activation(out=ssq[0][:], in_=ssq[0][:], func=AF.Exp, scale=-0.5), reads=[ssq[1]], writes=[ssq[1]])
                rb = ssq[0][:].unsqueeze(2).broadcast_to([128, 10, 128])
                S.op("dve", lambda e: e.tensor_tensor(out=qn[0][:], in0=qk3, in1=rb, op=ALU.mult),
                     reads=[qkv[1], ssq[1]], writes=[qn[1]])
                S.op("pool", lambda e: e.tensor_tensor(out=qn[0][:], in0=qn[0][:], in1=GQK[0][:], op=ALU.mult),
                     reads=[qn[1], GQK[1]], writes=[qn[1]])
                q4 = qn[0][:].rearrange("p h (i two) -> p h i two", two=2)
                x0 = q4[:, :, :, 0]
                x1 = q4[:, :, :, 1]
                cosb = cs[0][:, 0:64].unsqueeze(1).broadcast_to([128, 10, 64])
                sinb = cs[0][:, 64:128].unsqueeze(1).broadcast_to([128, 10, 64])
                S.op("dve", lambda e: e.tensor_tensor(out=t1[0][:], in0=x0, in1=cosb, op=ALU.mult),
                     reads=[qn[1], cs[1]], writes=[t1[1]])
                S.op("pool", lambda e: e.tensor_tensor(out=t2[0][:], in0=x1, in1=sinb, op=ALU.mult),
                     reads=[qn[1], cs[1]], writes=[t2[1]])
                S.op("dve", lambda e: e.tensor_tensor(out=qr[0][:, :, :, 0], in0=t1[0][:], in1=t2[0][:], op=ALU.subtract),
                     reads=[t1[1], t2[1]], writes=[qr[1]])
                S.op("dve", lambda e: e.tensor_tensor(out=t1[0][:], in0=x0, in1=sinb, op=ALU.mult),
                     reads=[qn[1], cs[1]], writes=[t1[1]])
                S.op("pool", lambda e: e.tensor_tensor(out=t2[0][:], in0=x1, in1=cosb, op=ALU.mult),
                     reads=[qn[1], cs[1]], writes=[t2[1]])
                S.op("dve", lambda e: e.tensor_tensor(out=qr[0][:, :, :, 1], in0=t1[0][:], in1=t2[0][:], op=ALU.add),
                     reads=[t1[1], t2[1]], writes=[qr[1]])
                qrf = qr[0][:].rearrange("p h i two -> p (h i two)")
                pa = bank_bf(k, qb_)
                pb = bank_bf(k, 5)

                def tr(e):
                    for h in range(8):
                        i = e.transpose(out=pa[:, h * 128:(h + 1) * 128], in_=qrf[:, h * 128:(h + 1) * 128], identity=k.ident[0][:])
                    return i
                S.op("pe", tr, reads=[qr[1], k.ident[1]], writes=[k.PB[qb_]])

                def trk(e):
                    for h in range(8, 10):
                        i = e.transpose(out=pb[:, (h - 8) * 128:(h - 7) * 128], in_=qrf[:, h * 128:(h + 1) * 128], identity=k.ident[0][:])
                    return i
                S.op("pe", trk, reads=[qr[1], k.ident[1]], writes=[k.PB[5]])
                S.op("act", lambda e, t=t: e.activation(out=QT[0][:, :, t * 128:(t + 1) * 128],
                                                        in_=pa.rearrange("p (h n) -> p h n", h=8), func=AF.Copy),
                     reads=[k.PB[qb_]], writes=[QTr[t]])
                S.op("dve", lambda e, t=t: e.tensor_copy(out=KT[0][:, :, t * 128:(t + 1) * 128],
                                                         in_=pb[:, 0:256].rearrange("p (h n) -> p h n", h=2)),
                     reads=[k.PB[5]], writes=[QTr[t]])
            emit_skewed(k, [(lambda S, t=t: body(S, t)) for t in range(NT)])
            S.barrier()
        with contextlib.ExitStack() as s2:
            wout = _sb(k, s2, "c_wout", [128, 8, D], BF16)
            OT = _sb(k, s2, "c_OT", [128, 8, 512], BF16)
            OTr = [Res(f"c_OT{h}") for h in range(8)]
            Pt = [_sb(k, s2, f"c_P{i}", [128, 2, 512], BF16) for i in range(4)]
            accs = [[_sb(k, s2, f"c_acc{i}{j}", [128, 2, 512], F32) for j in range(2)] for i in range(2)]
            onesf = _sb(k, s2, "c_onesf", [128, 128], F32)
            rden = [_sb(k, s2, f"c_rden{i}", [128, 512], F32) for i in range(2)]
            hts = [_sb(k, s2, f"c_h2{i}", [128, D], F32) for i in range(2)]
            hos = [_sb(k, s2, f"c_ho{i}", [128, D], F32) for i in range(2)]
            load_weight_bf16(k, wout, d["w_out_c"], 8, D)
            S.op("pool", lambda e: e.memset(onesf[0][:], 1.0), writes=[onesf[1]])
            groups = [(g * 512, 512) for g in range(LP // 512)] + ([(LP // 512 * 512, LP % 512)] if LP % 512 else [])
            pairs = [tuple(range(j, min(j + 2, NT))) for j in range(0, NT, 2)]
            SBP = [(0, 1), (6, 7)]
            combo = 0
            pcount = 0
            tcnt = [0]
            pending = []
            for (q0, qn_) in groups:
                for h in range(8):
                    kv = h // 4
                    bo = 2 + 2 * (combo % 2)
                    bd = bo + 1
                    acc = accs[combo % 2]
                    combo += 1
                    qsl = QT[0][:, h, q0:q0 + qn_]
                    qres = [QTr[tt] for tt in range(q0 // 128, (q0 + qn_) // 128)]

                    def smm(pi, kv=kv, qsl=qsl, qres=qres):
                        pr = pairs[pi]
                        bp = SBP[pi % 2]

                        def f(e):
                            for idx, j in enumerate(pr):
                                i = e.matmul(out=bank(k, bp[idx], qn_), lhsT=KT[0][:, kv, j * 128:(j + 1) * 128], rhs=qsl,
                                             start=True, stop=True)
                            return i
                        S.op("pe", f, reads=[QTr[j] for j in pr] + qres, writes=[k.PB[bp[idx]] for idx in range(len(pr))])
                    smm(0)
                    used = [False, False]
                    pe_started = [False]
                    for pi, pr in enumerate(pairs):
                        n = len(pr)
                        bp = SBP[pi % 2]
                        P = Pt[pcount % 4]
                        pcount += 1
                        sv = k.ps[:, bp[0] * 512:(bp[0] + 2) * 512].rearrange("p (two n) -> p two n", two=2)[:, 0:n, 0:qn_]
                        pv_ = P[0][:, 0:n, 0:qn_]
                        S.op("act", lambda e, sv=sv, pv_=pv_: e.activation(out=pv_, in_=sv, func=AF.Exp),
                             reads=[k.PB[bp[idx]] for idx in range(n)], writes=[P[1]])
                        if pi + 1 < len(pairs):
                            smm(pi + 1)
                        if pi == min(2, len(pairs) - 1):
                            while pending:
                                pending.pop(0)()

                        def pvf(e, P=P, pr=pr):
                            for idx, j in enumerate(pr):
                                i = e.matmul(out=bank(k, bo, qn_), lhsT=V[0][:, j, kv * 128:(kv + 1) * 128], rhs=P[0][:, idx, 0:qn_],
                                             start=(j == 0), stop=(j == NT - 1))
                            return i
                        S.op("pe", pvf, reads=[P[1]] + [QTr[j] for j in pr], writes=[k.PB[bo]])
                        mode = pi % 3
                        if mode == 0:
                            def df(e, P=P, pr=pr):
                                for idx, j in enumerate(pr):
                                    om = k.ones0 if j == 0 else k.ones
                                    i = e.matmul(out=bank(k, bd, qn_), lhsT=om[0][:], rhs=P[0][:, idx, 0:qn_],
                                                 start=(not pe_started[0]), stop=False)
                                    pe_started[0] = True
                                return i
                            S.op("pe", df, reads=[P[1], k.ones[1], k.ones0[1]], writes=[k.PB[bd]])
                        else:
                            ai = mode - 1
                            ac = acc[ai]
                            eng = "dve" if ai == 0 else "pool"
                            if not used[ai]:
                                used[ai] = True
                                if n < 2:
                                    S.op(eng, lambda e, ac=ac: e.memset(ac[0][:], 0.0), writes=[ac[1]])
                                S.op(eng, lambda e, ac=ac, pv_=pv_: e.tensor_copy(out=ac[0][:, 0:n, 0:qn_], in_=pv_),
                                     reads=[P[1]], writes=[ac[1]])
                                if n == 2 and qn_ < 512:
                                    pass
                            else:
                                S.op(eng, lambda e, ac=ac, pv_=pv_: e.tensor_tensor(out=ac[0][:, 0:n, 0:qn_], in0=ac[0][:, 0:n, 0:qn_],
                                                                                   in1=pv_, op=ALU.add),
                                     reads=[P[1], ac[1]], writes=[ac[1]])
                    def make_tail(acc=acc, used=list(used), bo=bo, bd=bd, rd=rden[combo % 2], h=h, qn_=qn_):
                        def tail():
                            srcs = [a_ for ai, a_ in enumerate(acc) if used[ai]]
                            if len(srcs) == 2:
                                S.op("pool", lambda e: e.tensor_tensor(out=acc[0][0][:, :, 0:qn_], in0=acc[0][0][:, :, 0:qn_],
                                                                       in1=acc[1][0][:, :, 0:qn_], op=ALU.add),
                                     reads=[acc[0][1], acc[1][1]], writes=[acc[0][1]])
                            if srcs:
                                def ff(e):
                                    e.matmul(out=bank(k, bd, qn_), lhsT=onesf[0][:], rhs=srcs[0][0][:, 0, 0:qn_], start=False, stop=False)
                                    return e.matmul(out=bank(k, bd, qn_), lhsT=onesf[0][:], rhs=srcs[0][0][:, 1, 0:qn_], start=False, stop=True)
                                S.op("pe", ff, reads=[onesf[1], srcs[0][1]], writes=[k.PB[bd]])
                            S.op("dve", lambda e: e.reciprocal(out=rd[0][:, 0:qn_], in_=bank(k, bd, qn_)),
                                 reads=[k.PB[bd]], writes=[rd[1]])
                            S.op("dve", lambda e: e.tensor_tensor(out=OT[0][:, h, 0:qn_], in0=bank(k, bo, qn_),
                                                                  in1=rd[0][:, 0:qn_], op=ALU.mult),
                                 reads=[k.PB[bo], rd[1]], writes=[OTr[h]])
                        return tail
                    pending.append(make_tail())
                def make_outproj(q0=q0, qn_=qn_, bpar=combo % 2):
                    def outproj():
                        for tt in range(qn_ // 128):
                            t = q0 // 128 + tt
                            ht = hts[tcnt[0] % 2]
                            ho = hos[tcnt[0] % 2]
                            tcnt[0] += 1
                            S.dma("sp", ht[0][:], src(t), writes=[ht[1]])
                            for nb in range(2):
                                b = 3 + 2 * (1 - bpar)

                                def mm(e, tt=tt, nb=nb, b=b):
                                    for h in range(8):
                                        i = e.matmul(out=bank(k, b), lhsT=OT[0][:, h, tt * 128:(tt + 1) * 128],
                                                     rhs=wout[0][:, h, nb * 512:(nb + 1) * 512], start=(h == 0), stop=(h == 7))
                                    return i
                                S.op("pe", mm, reads=OTr + [wout[1]], writes=[k.PB[b]])
                                S.op("dve", lambda e, nb=nb, b=b, ht=ht, ho=ho: e.tensor_tensor(
                                    out=ho[0][:, nb * 512:(nb + 1) * 512], in0=bank(k, b), in1=ht[0][:, nb * 512:(nb + 1) * 512],
                                    op=ALU.add), reads=[k.PB[b], ht[1]], writes=[ho[1]])
                            S.dma("act", dst(t), ho[0][:], reads=[ho[1]])
                    return outproj
                pending.append(make_outproj())
            while pending:
                pending.pop(0)()
            S.barrier()


def input_specs():
  return [
    ("x", [SEQ, D]), ("meta_tokens", [NMETA, D]), ("attn_norm_g", [2, D]), ("mlp_norm_g", [2, D]),
    ("w_in_ab", [D, IN_AB]), ("conv_w_a", [5, 1536]), ("a_log", [2, 4]), ("dt_bias", [2, 4]),
    ("a_out_norm_g", [128]), ("b_q_norm_g", [64]), ("b_k_norm_g", [64]), ("b_sink", [8]),
    ("w_out_ab", [D, D]), ("w_qkv_c", [D, IN_C]), ("c_q_norm_g", [128]), ("c_k_norm_g", [128]),
    ("w_out_c", [D, D]), ("w_ff1", [2, D, DFF]), ("w_ff2", [2, DFF, D]),
    ("rope", [LP, 128]), ("padmask", [128, 1]), ("etab", [3, 2, 128, 512]), ("tri", [4, 128, 128]),
]


def build_nc(layers=(0, 1), dbg="cm", mlp_tiles=None, dbg0="vdom"):
    nc = bass.Bass("TRN2", target_bir_lowering=False)
    k = K()
    k.nc = nc
    d = {}
    for name, shape in input_specs():
        d[name] = nc.dram_tensor(name, shape, F32, kind="ExternalInput").ap()
    out = nc.dram_tensor("out", [SEQ, D], F32, kind="ExternalOutput").ap()
    d["scr_z"] = nc.dram_tensor("scr_z", [LP, 512], BF16, kind="ExternalOutput").ap()
    d["scr_yb"] = nc.dram_tensor("scr_yb", [LP, 512], BF16, kind="ExternalOutput").ap()
    d["scr_a"] = nc.dram_tensor("scr_a", [LP, 1536], BF16, kind="ExternalOutput").ap()
    d["scr_o"] = nc.dram_tensor("scr_o", [2, LP, 512], F32, kind="ExternalOutput").ap()
    d["scr_w1"] = nc.dram_tensor("scr_w1", [2, D, DFF], BF16, kind="ExternalOutput").ap()
    d["scr_w2"] = nc.dram_tensor("scr_w2", [2, DFF, D], BF16, kind="ExternalOutput").ap()
    H0s = nc.dram_tensor("H0s", [128, D], F32, kind="Internal").ap()
    k.d = d
    with contextlib.ExitStack() as st:
        S = Sched(nc, st)
        k.S = S
        k.ps = st.enter_context(nc.psum_tensor("ps", [128, 4096], F32))
        k.psb = k.ps.bitcast(BF16)
        k.PB = [Res(f"PB{i}") for i in range(8)]
        k.ident = _sb(k, st, "ident", [128, 128], BF16)
        k.ones = _sb(k, st, "ones", [128, 128], BF16)
        k.ones0 = _sb(k, st, "ones0", [128, 128], BF16)
        k.padmask = _sb(k, st, "padmask", [128, 1], F32)
        zt = _sb(k, st, "zt", [128, D], F32)
        S.dma("sp", k.padmask[0][:], d["padmask"], writes=[k.padmask[1]])
        S.op("pool", lambda e: e.memset(k.ident[0][:], 0.0), writes=[k.ident[1]])
        S.op("pool", lambda e: e.affine_select(out=k.ident[0][:], in_=k.ident[0][:], pattern=[[-1, 128]],
                                               compare_op=ALU.not_equal, fill=1.0, base=0, channel_multiplier=1),
             reads=[k.ident[1]], writes=[k.ident[1]])
        S.op("pool", lambda e: e.memset(k.ones[0][:], 1.0), writes=[k.ones[1]])
        S.op("dve", lambda e: e.tensor_scalar(out=k.ones0[0][:], in0=k.ones[0][:], scalar1=k.padmask[0][:, 0:1], scalar2=None,
                                              op0=ALU.mult), reads=[k.ones[1], k.padmask[1]], writes=[k.ones0[1]])
        S.op("pool", lambda e: e.memset(zt[0][:], 0.0), writes=[zt[1]])
        S.dma("sp", H0s[0:FP, :], zt[0][0:FP, :], reads=[zt[1]])
        S.dma("sp", H0s[FP:128, :], d["meta_tokens"])
        S.barrier()

        def xin(t):
            return H0s if t == 0 else d["x"][(t - 1) * 128:t * 128, :]

        def hbuf(t):
            return H0s if t == 0 else out[(t - 1) * 128:t * 128, :]
        cur = xin
        if 0 in layers:
            k.jobs = convert_jobs(k)
            with contextlib.ExitStack() as sl:
                uT_all, uTr, BETA, G = layer_ab_front(k, cur, 0, sl)
                if "v" in dbg0:
                    layer_ab_conv(k, uT_all, uTr)
                if "d" in dbg0:
                    layer_ab_delta(k, BETA, G)
            if "o" in dbg0:
                layer_ab_out(k, cur, hbuf)
            if "m" in dbg0:
                mlp_phase(k, hbuf, d["mlp_norm_g"][0:1, :], d["w_ff1"][0], d["w_ff2"][0], hbuf, list(range(NT)), l=0)
            cur = hbuf
        if 1 in layers:
            layer_c(k, cur, hbuf, 1)
            mlp_phase(k, hbuf, d["mlp_norm_g"][1:2, :], d["w_ff1"][1], d["w_ff2"][1], hbuf, mlp_tiles or list(range(1, NT)), l=1)
        S.finish()
    return nc


def rope_table():
    rows = SEQ // 64
    row = np.repeat(np.arange(rows), 64)
    col = np.tile(np.arange(64), rows)
    meta = np.arange(NMETA) - NMETA
    row = np.concatenate([meta, row]).astype(np.float32)
    col = np.concatenate([meta, col]).astype(np.float32)
    freqs = (np.float32(10000.0) ** (-np.arange(0, 64, 2, dtype=np.float32) / np.float32(64))).astype(np.float32)
    ang = np.concatenate([row[:, None] * freqs, col[:, None] * freqs], axis=-1).astype(np.float32)
    tab = np.zeros((LP, 128), np.float32)
    tab[:FP, :64] = 1.0
    tab[FP:, :64] = np.cos(ang)
    tab[FP:, 64:] = np.sin(ang)
    return tab


def const_inputs():
    pm = np.ones((128, 1), np.float32)
    pm[:FP] = 0.0
    r = np.arange(128)[:, None]
    c = np.arange(128)[None, :]
    et = np.zeros((3, 2, 128, 512), np.float32)
    for off in (-1, 0, 1):
        dist = np.abs(128 * off + r - c).astype(np.float32)
        for g in range(2):
            for a in range(4):
                h = 4 * g + a
                slope = np.float32(2.0) ** np.float32(-8.0 * (h + 1.0) / 8.0)
                et[off + 1, g, :, a * 128:(a + 1) * 128] = np.where(dist <= 128, np.exp(-slope * dist), 0.0)
    tri = np.stack([(r <= c), (r >= c), (r < c), (r > c)]).astype(np.float32)
    return {"rope": rope_table(), "padmask": pm, "etab": et, "tri": tri}


def make_in_maps(inputs, ncores=8):
    c = const_inputs()
    sq = lambda a: np.ascontiguousarray(np.asarray(a, dtype=np.float32))
    shared = {
        "meta_tokens": sq(inputs["meta_tokens"]), "attn_norm_g": sq(inputs["attn_norm_g"]),
        "mlp_norm_g": sq(inputs["mlp_norm_g"]), "w_in_ab": sq(inputs["w_in_ab"][0]),
        "conv_w_a": sq(inputs["conv_w_a"][0]), "a_log": sq(inputs["a_log"][0]), "dt_bias": sq(inputs["dt_bias"][0]),
        "a_out_norm_g": sq(inputs["a_out_norm_g"][0]), "b_q_norm_g": sq(inputs["b_q_norm_g"][0]),
        "b_k_norm_g": sq(inputs["b_k_norm_g"][0]), "b_sink": sq(inputs["b_sink"][0]),
        "w_out_ab": sq(inputs["w_out_ab"][0]), "w_qkv_c": sq(inputs["w_qkv_c"][0]),
        "c_q_norm_g": sq(inputs["c_q_norm_g"][0]), "c_k_norm_g": sq(inputs["c_k_norm_g"][0]),
        "w_out_c": sq(inputs["w_out_c"][0]), "w_ff1": sq(inputs["w_ff1"]), "w_ff2": sq(inputs["w_ff2"]),
    }
    shared.update(c)
    maps = []
    for b in range(ncores):
        m = dict(shared)
        m["x"] = sq(inputs["x"][b])
        maps.append(m)
    return maps


def kernel(**inputs):
    nc = build_nc()
    in_maps = make_in_maps(inputs, 8)
    res = run_bass_kernel_spmd(nc, in_maps, core_ids=list(range(8)))
    return np.stack([np.asarray(r["out"], dtype=np.float32) for r in res.results], axis=0)


NTOK = 1296


def layer_ab_front(k, src, l, st):
    S = k.S
    d = k.d
    uT_all = _sb(k, st, "a_uTall", [128, 8, LP], BF16)
    uTr = [Res(f"a_uT{t}") for t in range(NT)]
    BETA = _sb(k, st, "a_BETA", [128, NT, 8], F32)
    G = _sb(k, st, "a_G", [128, NT, 8], F32)
    with contextlib.ExitStack() as sB:
        qT_all = _sb(k, sB, "b_qT", [128, NT, 512], BF16)
        kT_all = _sb(k, sB, "b_kT", [128, LP], BF16)
        Vaug = _sb(k, sB, "b_Vaug", [128, NT, 2, 65], BF16)
        Vm = _sb(k, sB, "b_Vm", [16, 2, 65], BF16)
        Br = [Res(f"b_t{t}") for t in range(NT)]
        with contextlib.ExitStack() as s1:
            wt = _sb(k, s1, "a_wtok", [128, 8, NTOK], BF16)
            gt = _sb(k, s1, "a_gt", [128, D], F32)
            GB = _sb(k, s1, "a_GB", [128, 10, 64], F32)
            dtb = _sb(k, s1, "a_dtb", [128, 8], F32)
            negA = _sb(k, s1, "a_negA", [128, 8], F32)
            hts = [_sb(k, s1, f"a_ht{i}", [128, D], F32) for i in range(2)]
            szs = [_sb(k, s1, f"a_sz{i}", [128, 512], BF16) for i in range(2)]
            A2 = []
            for i in range(2):
                A2.append(dict(
                    ub=_sb(k, s1, f"a_ub{i}", [128, D], BF16), pj=_sb(k, s1, f"a_pj{i}", [128, NTOK], F32),
                    sq=_sb(k, s1, f"a_sq{i}", [128, 10, 64], F32), qkn=_sb(k, s1, f"a_qkn{i}", [128, 10, 64], BF16),
                    ssq=_sb(k, s1, f"a_ssq{i}", [128, 10], F32), ga=_sb(k, s1, f"a_ga{i}", [128, 8], F32),
                    ez=_sb(k, s1, f"a_ez{i}", [128, 512], F32),
                    stat=(_sb(k, s1, f"a_ss{i}", [128, 1], F32), _sb(k, s1, f"a_rs{i}", [128, 1], F32))))
            win = d["w_in_ab"]
            for c in range(8):
                rows = slice(c * 128, (c + 1) * 128)
                S.dma("pool", wt[0][:, c, 0:528], win[rows, 1536:2064], writes=[wt[1]])
                for par in range(2):
                    S.dma("pool", wt[0][:, c, 528:1040].rearrange("p (a par e) -> p a par e", a=4, par=2)[:, :, par, :],
                          win[rows, 2064 + par * 256:2064 + (par + 1) * 256].rearrange("p (a e) -> p a e", a=4), writes=[wt[1]])
                S.dma("pool", wt[0][:, c, 1040:1296], win[rows, 2576:2832], writes=[wt[1]])
            S.dma("sp", gt[0][:], d["attn_norm_g"][l:l + 1, :].partition_broadcast(128), writes=[gt[1]])
            S.dma("sp", GB[0][:, 0:8, :], d["b_q_norm_g"].partition_broadcast(128).unsqueeze(1).broadcast_to([128, 8, 64]),
                  writes=[GB[1]])
            S.dma("sp", GB[0][:, 8:10, :], d["b_k_norm_g"].partition_broadcast(128).unsqueeze(1).broadcast_to([128, 2, 64]),
                  writes=[GB[1]])
            S.op("act", lambda e: e.mul(out=GB[0][:, 0:8, :], in_=GB[0][:, 0:8, :], mul=64.0 ** -0.5),
                 reads=[GB[1]], writes=[GB[1]])
            S.dma("sp", dtb[0][:], d["dt_bias"].rearrange("a b -> (a b)").partition_broadcast(128), writes=[dtb[1]])
            S.dma("sp", negA[0][:], d["a_log"].rearrange("a b -> (a b)").partition_broadcast(128), writes=[negA[1]])
            S.op("act", lambda e: e.activation(out=negA[0][:], in_=negA[0][:], func=AF.Exp), reads=[negA[1]], writes=[negA[1]])
            S.op("act", lambda e: e.mul(out=negA[0][:], in_=negA[0][:], mul=-1.0), reads=[negA[1]], writes=[negA[1]])
            S.op("pool", lambda e: e.memset(Vaug[0][:], 1.0), writes=Br)
            S.op("pool", lambda e: e.memset(Vm[0][:], 1.0), writes=[Br[0]])
            jobs = getattr(k, "jobs", [])
            per = 1 if jobs else 0

            def body(S, t):
                for _ in range(per):
                    if jobs:
                        S.raw(jobs.pop(0))
                ht = hts[t % 2]
                ab_ = A2[t % 2]
                ub, pj, sq, qkn, ssq, ga, stat = (ab_[n] for n in ("ub", "pj", "sq", "qkn", "ssq", "ga", "stat"))
                tb = 0 if t % 2 == 0 else 6
                qb_ = 5 if t % 2 == 0 else 7
                S.dma("sp", ht[0][:], src(t), writes=[ht[1]])
                rms_to_uT(k, ht, gt, ub, stat, uT_all[0][:, :, t * 128:(t + 1) * 128], uTr[t], tb, S=S)
                for nb, (c0, c1) in enumerate([(0, 512), (512, 1024), (1024, NTOK)]):
                    def mm(e, nb=nb, c0=c0, c1=c1):
                        for c in range(8):
                            i = e.matmul(out=bank(k, 1 + nb, c1 - c0), lhsT=uT_all[0][:, c, t * 128:(t + 1) * 128],
                                         rhs=wt[0][:, c, c0:c1], start=(c == 0), stop=(c == 7))
                        return i
                    S.op("pe", mm, reads=[uTr[t], wt[1]], writes=[k.PB[1 + nb]])
                S.op("act", lambda e: e.activation(out=pj[0][:], in_=k.ps[:, 512:512 + NTOK], func=AF.Copy),
                     reads=[k.PB[1], k.PB[2], k.PB[3]], writes=[pj[1]])
                if t == 0:
                    def mmv(e):
                        for c in range(8):
                            i = e.matmul(out=k.ps[0:16, 2048:2176], lhsT=uT_all[0][:, c, FP:128], rhs=wt[0][:, c, 1168:1296],
                                         start=(c == 0), stop=(c == 7))
                        return i
                    S.op("pe", mmv, reads=[uTr[0], wt[1]], writes=[k.PB[4]])
                    S.op("act", lambda e: e.activation(out=Vm[0][:, :, 0:64], in_=k.ps[0:16, 2048:2176].rearrange("p (g e) -> p g e", g=2),
                                                       func=AF.Copy), reads=[k.PB[4]], writes=[Br[0]])
                sz = szs[t % 2]
                S.op("act", lambda e, sz=sz: e.activation(out=sz[0][:], in_=pj[0][:, 0:512], func=AF.Silu), reads=[pj[1]], writes=[sz[1]])
                S.dma("act", d["scr_z"][t * 128:(t + 1) * 128, :], sz[0][:], reads=[sz[1]])
                S.op("act", lambda e, t=t: e.activation(out=BETA[0][:, t, :], in_=pj[0][:, 512:520], func=AF.Exp, scale=-1.0),
                     reads=[pj[1]], writes=[BETA[1]])
                S.op("dve", lambda e, t=t: e.tensor_scalar(out=BETA[0][:, t, :], in0=BETA[0][:, t, :], scalar1=1.0, scalar2=None, op0=ALU.add),
                     reads=[BETA[1]], writes=[BETA[1]])
                S.op("dve", lambda e, t=t: e.reciprocal(out=BETA[0][:, t, :], in_=BETA[0][:, t, :]), reads=[BETA[1]], writes=[BETA[1]])
                S.op("dve", lambda e: e.tensor_tensor(out=ga[0][:], in0=pj[0][:, 520:528], in1=dtb[0][:], op=ALU.add),
                     reads=[pj[1], dtb[1]], writes=[ga[1]])
                S.op("act", lambda e: e.activation(out=ga[0][:], in_=ga[0][:], func=AF.Exp), reads=[ga[1]], writes=[ga[1]])
                S.op("act", lambda e: e.activation(out=ga[0][:], in_=ga[0][:], func=AF.Ln, bias=1.0), reads=[ga[1]], writes=[ga[1]])
                S.op("dve", lambda e, t=t: e.tensor_tensor(out=G[0][:, t, :], in0=ga[0][:], in1=negA[0][:], op=ALU.mult),
                     reads=[ga[1], negA[1]], writes=[G[1]])
                if t == 0:
                    S.op("dve", lambda e: e.tensor_scalar(out=BETA[0][:, 0, :], in0=BETA[0][:, 0, :], scalar1=k.padmask[0][:, 0:1],
                                                          scalar2=None, op0=ALU.mult), reads=[BETA[1], k.padmask[1]], writes=[BETA[1]])
                    S.op("dve", lambda e: e.tensor_scalar(out=G[0][:, 0, :], in0=G[0][:, 0, :], scalar1=k.padmask[0][:, 0:1],
                                                          scalar2=None, op0=ALU.mult), reads=[G[1], k.padmask[1]], writes=[G[1]])
                qk3 = pj[0][:, 528:1168].rearrange("p (h e) -> p h e", h=10)
                S.op("act", lambda e: e.activation(out=sq[0][:], in_=qk3, func=AF.Square), reads=[pj[1]], writes=[sq[1]])
                S.op("dve", lambda e: e.tensor_reduce(out=ssq[0][:], in_=sq[0][:], axis=AX.X, op=ALU.add), reads=[sq[1]], writes=[ssq[1]])
                S.op("act", lambda e: e.activation(out=ssq[0][:], in_=ssq[0][:], func=AF.Ln, scale=1.0 / 64, bias=EPS),
                     reads=[ssq[1]], writes=[ssq[1]])
                S.op("act", lambda e: e.activation(out=ssq[0][:], in_=ssq[0][:], func=AF.Exp, scale=-0.5), reads=[ssq[1]], writes=[ssq[1]])
                S.op("dve", lambda e: e.tensor_tensor(out=sq[0][:], in0=qk3, in1=ssq[0][:].unsqueeze(2).broadcast_to([128, 10, 64]),
                                                      op=ALU.mult), reads=[pj[1], ssq[1]], writes=[sq[1]])
                S.op("pool", lambda e: e.tensor_tensor(out=qkn[0][:], in0=sq[0][:], in1=GB[0][:], op=ALU.mult),
                     reads=[sq[1], GB[1]], writes=[qkn[1]])
                qkf = qkn[0][:].rearrange("p h e -> p (h e)")
                pa = bank_bf(k, qb_)

                def tr(e):
                    for a in range(5):
                        i = e.transpose(out=pa[:, a * 128:(a + 1) * 128], in_=qkf[:, a * 128:(a + 1) * 128], identity=k.ident[0][:])
                    return i
                S.op("pe", tr, reads=[qkn[1], k.ident[1]], writes=[k.PB[qb_]])
                S.op("act", lambda e, t=t: e.activation(out=qT_all[0][:, t, :], in_=pa[:, 0:512], func=AF.Copy),
                     reads=[k.PB[qb_]], writes=[Br[t]])
                S.op("dve", lambda e, t=t: e.tensor_copy(out=kT_all[0][:, t * 128:(t + 1) * 128], in_=pa[:, 512:640]),
                     reads=[k.PB[qb_]], writes=[Br[t]])
                S.op("pool", lambda e, t=t: e.tensor_copy(out=Vaug[0][:, t, :, 0:64],
                                                          in_=pj[0][:, 1168:1296].rearrange("p (g e) -> p g e", g=2)),
                     reads=[pj[1]], writes=[Br[t]])
            emit_skewed(k, [(lambda S, t=t: body(S, t)) for t in range(NT)])
            S.barrier()
        with contextlib.ExitStack() as s2:
            ET = _sb(k, s2, "b_E", [128, 6, 512], F32)
            esk = _sb(k, s2, "b_esk", [128, 8], F32)
            exs = [_sb(k, s2, f"b_ex{i}", [128, 512], F32) for i in range(2)]
            Pw = [[_sb(k, s2, f"b_Pw{g}{j}", [128, 512], BF16) for j in range(3)] for g in range(2)]
            Pm = [_sb(k, s2, f"b_Pm{g}", [16, 512], BF16) for g in range(2)]
            den = _sb(k, s2, "b_den", [128, 4], F32)
            ybs = [_sb(k, s2, f"b_yb{i}", [128, 8, 64], BF16) for i in range(2)]
            S.dma("sp", ET[0][:], d["etab"].rearrange("o g p n -> p (o g) n"), writes=[ET[1]])
            S.dma("sp", esk[0][:], d["b_sink"].partition_broadcast(128), writes=[esk[1]])
            S.op("act", lambda e: e.activation(out=esk[0][:], in_=esk[0][:], func=AF.Exp), reads=[esk[1]], writes=[esk[1]])
            sc = 0
            for i in range(NT):
                yb = ybs[i % 2]
                js = [j for j in (i - 1, i, i + 1) if 1 <= j <= NT - 1]
                for g in range(2):
                    prt = slice(64 * g, 64 * g + 64)
                    qs = qT_all[0][prt, i, :]
                    for ji, j in enumerate(js):
                        bs = sc % 2
                        ex = exs[sc % 2]
                        sc += 1
                        S.op("pe", lambda e, j=j, bs=bs: e.matmul(out=bank(k, bs), lhsT=kT_all[0][prt, j * 128:(j + 1) * 128], rhs=qs,
                                                                  start=True, stop=True), reads=[Br[j], Br[i]], writes=[k.PB[bs]])
                        S.op("act", lambda e, ex=ex, bs=bs: e.activation(out=ex[0][:], in_=bank(k, bs), func=AF.Exp),
                             reads=[k.PB[bs]], writes=[ex[1]])
                        S.op("dve", lambda e, ex=ex, ji=ji, j=j: e.tensor_tensor(out=Pw[g][ji][0][:], in0=ex[0][:],
                                                                                  in1=ET[0][:, (j - i + 1) * 2 + g, :], op=ALU.mult),
                             reads=[ex[1], ET[1]], writes=[Pw[g][ji][1]])
                    S.op("pe", lambda e: e.matmul(out=k.ps[0:16, 1024:1536], lhsT=kT_all[0][prt, FP:128], rhs=qs, start=True, stop=True),
                         reads=[Br[0], Br[i]], writes=[k.PB[2]])
                    S.op("act", lambda e: e.activation(out=Pm[g][0][:], in_=k.ps[0:16, 1024:1536], func=AF.Exp),
                         reads=[k.PB[2]], writes=[Pm[g][1]])
                    bo = 3 + g

                    def pv(e, g=g, bo=bo):
                        for a in range(4):
                            o = k.ps[:, bo * 512 + a * 65:bo * 512 + (a + 1) * 65]
                            for ji, j in enumerate(js):
                                e.matmul(out=o, lhsT=Pw[g][ji][0][:, a * 128:(a + 1) * 128], rhs=Vaug[0][:, j, g, :],
                                         start=(ji == 0), stop=False)
                            i_ = e.matmul(out=o, lhsT=Pm[g][0][:, a * 128:(a + 1) * 128], rhs=Vm[0][:, g, :],
                                          start=(len(js) == 0), stop=True)
                        return i_
                    S.op("pe", pv, reads=[Pw[g][ji][1] for ji in range(len(js))] + [Pm[g][1]] + [Br[j] for j in js] + [Br[0]],
                         writes=[k.PB[bo]])
                    o4 = k.ps[:, bo * 512:bo * 512 + 260].rearrange("p (a e) -> p a e", a=4)
                    S.op("dve", lambda e, g=g, o4=o4: e.tensor_tensor(out=den[0][:], in0=o4[:, :, 64], in1=esk[0][:, 4 * g:4 * g + 4],
                                                                      op=ALU.add), reads=[k.PB[bo], esk[1]], writes=[den[1]])
                    S.op("dve", lambda e: e.reciprocal(out=den[0][:], in_=den[0][:]), reads=[den[1]], writes=[den[1]])
                    S.op("dve", lambda e, g=g, o4=o4, yb=yb: e.tensor_tensor(out=yb[0][:, 4 * g:4 * g + 4, :], in0=o4[:, :, 0:64],
                                                                             in1=den[0][:].unsqueeze(2).broadcast_to([128, 4, 64]),
                                                                             op=ALU.mult), reads=[k.PB[bo], den[1]], writes=[yb[1]])
                S.dma("act", d["scr_yb"][i * 128:(i + 1) * 128, :], yb[0][:].rearrange("p h e -> p (h e)"), reads=[yb[1]])
            S.barrier()
    return uT_all, uTr, BETA, G


def tok_groups():
    g = [(i * 512, 512) for i in range(LP // 512)]
    if LP % 512:
        g.append((LP // 512 * 512, LP % 512))
    return g


def layer_ab_conv(k, uT_all, uTr):
    S = k.S
    d = k.d
    with contextlib.ExitStack() as st:
        wf = _sb(k, st, "v_wf", [128, 8, 1536], BF16)
        cw = _sb(k, st, "v_cw", [128, 12, 5], F32)
        raws = [_sb(k, st, f"v_raw{i}", [128, LP + 4], F32) for i in range(2)]
        acc = _sb(k, st, "v_acc", [128, LP], F32)
        acts = [_sb(k, st, f"v_act{i}", [128, LP], BF16) for i in range(2)]
        stg = _sb(k, st, "v_stg", [128, NT, 512], BF16)
        load_weight_bf16(k, wf, d["w_in_ab"][:, 0:1536], 8, 1536)
        for j in range(5):
            S.dma("sp", cw[0][:, :, j], d["conv_w_a"][j, :].rearrange("(c p) -> p c", p=128), writes=[cw[1]],
                  allow_slow_non_contiguous=True)
        for r in raws:
            S.op("pool", lambda e, r=r: e.memset(r[0][:], 0.0), writes=[r[1]])
        def proj(cc):
            rw = raws[cc % 2]
            for gi, (q0, qn_) in enumerate(tok_groups()):
                b = gi % 2

                def mm(e, q0=q0, qn_=qn_, b=b):
                    for c in range(8):
                        i = e.matmul(out=bank(k, b, qn_), lhsT=wf[0][:, c, cc * 128:(cc + 1) * 128], rhs=uT_all[0][:, c, q0:q0 + qn_],
                                     start=(c == 0), stop=(c == 7))
                    return i
                S.op("pe", mm, reads=[wf[1]] + [uTr[t] for t in range(q0 // 128, (q0 + qn_) // 128)], writes=[k.PB[b]])
                S.op("act", lambda e, q0=q0, qn_=qn_, b=b: e.activation(out=rw[0][:, 2 + q0:2 + q0 + qn_], in_=bank(k, b, qn_), func=AF.Copy),
                     reads=[k.PB[b]], writes=[rw[1]])
        def rest(cc):
            rw = raws[cc % 2]
            act = acts[cc % 2]
            S.op("dve", lambda e: e.tensor_scalar(out=acc[0][:], in0=rw[0][:, 0:LP], scalar1=cw[0][:, cc, 0:1], scalar2=None, op0=ALU.mult),
                 reads=[rw[1], cw[1]], writes=[acc[1]])
            for j in range(1, 5):
                S.op("dve", lambda e, j=j: e.scalar_tensor_tensor(out=acc[0][:], in0=rw[0][:, j:j + LP], scalar=cw[0][:, cc, j:j + 1],
                                                                  in1=acc[0][:], op0=ALU.mult, op1=ALU.add),
                     reads=[rw[1], cw[1], acc[1]], writes=[acc[1]])
            S.op("act", lambda e: e.activation(out=act[0][:], in_=acc[0][:], func=AF.Silu), reads=[acc[1]], writes=[act[1]])
            for t0 in range(0, NT, 8):
                nt_ = min(8, NT - t0)
                b = 2 + (t0 // 8) % 2
                pT = bank_bf(k, b)

                def tr(e, t0=t0, nt_=nt_, pT=pT):
                    for tt in range(nt_):
                        i = e.transpose(out=pT[:, tt * 128:(tt + 1) * 128], in_=act[0][:, (t0 + tt) * 128:(t0 + tt + 1) * 128],
                                        identity=k.ident[0][:])
                    return i
                S.op("pe", tr, reads=[act[1], k.ident[1]], writes=[k.PB[b]])
                S.op("act" if (t0 // 8) % 2 else "dve",
                     (lambda e, t0=t0, nt_=nt_, pT=pT: e.activation(out=stg[0][:, t0:t0 + nt_, (cc % 4) * 128:(cc % 4 + 1) * 128],
                                                                    in_=pT[:, 0:nt_ * 128].rearrange("p (t n) -> p t n", t=nt_), func=AF.Copy))
                     if (t0 // 8) % 2 else
                     (lambda e, t0=t0, nt_=nt_, pT=pT: e.tensor_copy(out=stg[0][:, t0:t0 + nt_, (cc % 4) * 128:(cc % 4 + 1) * 128],
                                                                     in_=pT[:, 0:nt_ * 128].rearrange("p (t n) -> p t n", t=nt_))),
                     reads=[k.PB[b]], writes=[stg[1]])
            if cc % 4 == 3:
                qi = cc // 4
                for t0 in range(0, NT, 8):
                    nt_ = min(8, NT - t0)
                    S.dma("sp", d["scr_a"][t0 * 128:(t0 + nt_) * 128, qi * 512:(qi + 1) * 512].rearrange("(t p) c -> p t c", p=128),
                          stg[0][:, t0:t0 + nt_, :], reads=[stg[1]])

        proj(0)
        for cc in range(12):
            if cc + 1 < 12:
                proj(cc + 1)
            rest(cc)
        S.barrier()


def layer_ab_delta(k, BETA, G):
    S = k.S
    d = k.d
    bctr = [0, 0]
    cur_dd = [0]

    def nb():
        dd = cur_dd[0]
        b = 4 * dd + bctr[dd] % 4
        bctr[dd] += 1
        return b

    with contextlib.ExitStack() as st:
        tri = _sb(k, st, "d_tri", [128, 4, 128], F32)
        identf = _sb(k, st, "d_identf", [128, 128], F32)
        onesf = _sb(k, st, "d_onesf", [128, 128], F32)
        S.dma("sp", tri[0][:], d["tri"].rearrange("m p n -> p m n"), writes=[tri[1]])
        S.op("dve", lambda e: e.tensor_tensor(out=identf[0][:], in0=tri[0][:, 0, :], in1=tri[0][:, 1, :], op=ALU.mult),
             reads=[tri[1]], writes=[identf[1]])
        S.op("pool", lambda e: e.memset(onesf[0][:], 1.0), writes=[onesf[1]])
        W = []
        for dd in range(2):
            w = {}
            def mk(name, shape, dt, dd=dd, w=w):
                w[name] = _sb(k, st, f"d_{name}{dd}", shape, dt)
            mk("At", [128, 1536], BF16)
            mk("sq", [128, 8, 128], F32)
            mk("ssq", [128, 8], F32)
            mk("ex12", [128, 12], F32)
            for nm in ("kn", "qn", "kg", "kd", "qd", "qdT", "qkT", "Rb", "wT", "vnew", "Sb", "Rb1", "Qb0", "Qb1", "QTb0", "QTb1"):
                mk(nm, [128, 4, 128], BF16)
            mk("kTq", [128, 8, 128], BF16)
            for nm in ("gSL", "dec", "decS", "decI", "tmp", "Bp", "Q0", "Q1", "QT1", "R0", "R1", "ub", "S32", "osb"):
                mk(nm, [128, 4, 128], F32)
            S.op("pool", lambda e, w=w: e.memset(w["S32"][0][:], 0.0), writes=[w["S32"][1]])
            S.op("pool", lambda e, w=w: e.memset(w["Sb"][0][:], 0.0), writes=[w["Sb"][1]])
            W.append(w)

        def b3(ap2):
            return ap2.unsqueeze(2).broadcast_to([128, 4, 128])

        def m3(ap2):
            return ap2.unsqueeze(1).broadcast_to([128, 4, 128])

        def bk4(b):
            return bank(k, b).rearrange("p (h n) -> p h n", h=4)

        class Rec:
            def __init__(self):
                self.ops = []

            def op(self, *a, **kw):
                self.ops.append(lambda: k.S.op(*a, **kw))

            def dma(self, *a, **kw):
                self.ops.append(lambda: k.S.dma(*a, **kw))

        def delta_tile(dd, n, S):
            cur_dd[0] = dd
            w = W[dd]
            U = tri[0][:, 0, :] if dd == 0 else tri[0][:, 1, :]
            SL = tri[0][:, 3, :] if dd == 0 else tri[0][:, 2, :]
            MincT = U
            MstrT = tri[0][:, 2, :] if dd == 0 else tri[0][:, 3, :]
            g4 = G[0][:, n, 4 * dd:4 * dd + 4]
            b4 = BETA[0][:, n, 4 * dd:4 * dd + 4]
            At, sq, ssq, ex12 = w["At"], w["sq"], w["ssq"], w["ex12"]
            S.dma("sp", At[0][:], d["scr_a"][n * 128:(n + 1) * 128, :], writes=[At[1]])
            A3 = At[0][:].rearrange("p (h n) -> p h n", h=12)
            S.op("act", lambda e: e.activation(out=sq[0][:], in_=A3[:, 0:8, :], func=AF.Square), reads=[At[1]], writes=[sq[1]])
            S.op("dve", lambda e: e.tensor_reduce(out=ssq[0][:], in_=sq[0][:], axis=AX.X, op=ALU.add), reads=[sq[1]], writes=[ssq[1]])
            S.op("act", lambda e: e.activation(out=ssq[0][:], in_=ssq[0][:], func=AF.Ln, bias=EPS), reads=[ssq[1]], writes=[ssq[1]])
            S.op("act", lambda e: e.activation(out=ssq[0][:], in_=ssq[0][:], func=AF.Exp, scale=-0.5), reads=[ssq[1]], writes=[ssq[1]])
            S.op("dve", lambda e: e.tensor_scalar(out=ssq[0][:, 0:4], in0=ssq[0][:, 0:4], scalar1=128.0 ** -0.5, scalar2=None, op0=ALU.mult),
                 reads=[ssq[1]], writes=[ssq[1]])
            bg = nb()

            def mmg(e):
                e.matmul(out=k.ps[:, bg * 512:bg * 512 + 4], lhsT=U, rhs=g4, start=True, stop=True)
                return e.matmul(out=k.ps[:, bg * 512 + 4:bg * 512 + 8], lhsT=onesf[0][:], rhs=g4, start=True, stop=True)
            S.op("pe", mmg, reads=[tri[1], onesf[1], G[1]], writes=[k.PB[bg]])
            S.op("act", lambda e: e.activation(out=ex12[0][:, 0:8], in_=k.ps[:, bg * 512:bg * 512 + 8], func=AF.Copy),
                 reads=[k.PB[bg]], writes=[ex12[1]])
            S.op("dve", lambda e: e.tensor_tensor(out=ex12[0][:, 8:12], in0=ex12[0][:, 4:8], in1=ex12[0][:, 0:4], op=ALU.subtract),
                 reads=[ex12[1]], writes=[ex12[1]])
            S.op("act", lambda e: e.activation(out=ex12[0][:], in_=ex12[0][:], func=AF.Exp), reads=[ex12[1]], writes=[ex12[1]])
            eg, glast, ekd = ex12[0][:, 0:4], ex12[0][:, 4:8], ex12[0][:, 8:12]
            kn, qn, kg, kd, qd = w["kn"], w["qn"], w["kg"], w["kd"], w["qd"]
            S.op("dve", lambda e: e.tensor_tensor(out=qn[0][:], in0=A3[:, 0:4, :], in1=b3(ssq[0][:, 0:4]), op=ALU.mult),
                 reads=[At[1], ssq[1]], writes=[qn[1]])
            S.op("pool", lambda e: e.tensor_tensor(out=kn[0][:], in0=A3[:, 4:8, :], in1=b3(ssq[0][:, 4:8]), op=ALU.mult),
                 reads=[At[1], ssq[1]], writes=[kn[1]])
            S.op("dve", lambda e: e.tensor_tensor(out=kg[0][:], in0=kn[0][:], in1=b3(eg), op=ALU.mult), reads=[kn[1], ex12[1]], writes=[kg[1]])
            S.op("pool", lambda e: e.tensor_tensor(out=kd[0][:], in0=kn[0][:], in1=b3(ekd), op=ALU.mult), reads=[kn[1], ex12[1]], writes=[kd[1]])
            S.op("dve", lambda e: e.tensor_tensor(out=qd[0][:], in0=qn[0][:], in1=b3(eg), op=ALU.mult), reads=[qn[1], ex12[1]], writes=[qd[1]])
            bt1, bt2 = nb(), nb()
            p1, p2 = bank_bf(k, bt1), bank_bf(k, bt2)

            def tr1(e):
                for h in range(4):
                    e.transpose(out=p1[:, h * 128:(h + 1) * 128], in_=kn[0][:, h, :], identity=k.ident[0][:])
                for h in range(4):
                    i = e.transpose(out=p1[:, (4 + h) * 128:(5 + h) * 128], in_=qn[0][:, h, :], identity=k.ident[0][:])
                return i
            S.op("pe", tr1, reads=[kn[1], qn[1], k.ident[1]], writes=[k.PB[bt1]])

            def tr2(e):
                for h in range(4):
                    i = e.transpose(out=p2[:, h * 128:(h + 1) * 128], in_=qd[0][:, h, :], identity=k.ident[0][:])
                return i
            S.op("pe", tr2, reads=[qd[1], k.ident[1]], writes=[k.PB[bt2]])
            kTq, qdT = w["kTq"], w["qdT"]
            S.op("act", lambda e: e.activation(out=kTq[0][:], in_=p1.rearrange("p (h n) -> p h n", h=8), func=AF.Copy),
                 reads=[k.PB[bt1]], writes=[kTq[1]])
            S.op("act", lambda e: e.activation(out=qdT[0][:], in_=p2[:, 0:512].rearrange("p (h n) -> p h n", h=4), func=AF.Copy),
                 reads=[k.PB[bt2]], writes=[qdT[1]])
            gSL, dec, decS, decI = w["gSL"], w["dec"], w["decS"], w["decI"]
            S.op("pool", lambda e: e.tensor_tensor(out=gSL[0][:], in0=m3(SL), in1=b3(g4), op=ALU.mult), reads=[tri[1], G[1]], writes=[gSL[1]])
            bd = nb()

            def mmd(e):
                for h in range(4):
                    i = e.matmul(out=bank(k, bd, 128, h * 128), lhsT=gSL[0][:, h, :], rhs=U, start=True, stop=True)
                return i
            S.op("pe", mmd, reads=[gSL[1], tri[1]], writes=[k.PB[bd]])
            S.op("act", lambda e: e.activation(out=dec[0][:], in_=bk4(bd), func=AF.Exp), reads=[k.PB[bd]], writes=[dec[1]])
            S.op("dve", lambda e: e.tensor_tensor(out=decS[0][:], in0=dec[0][:], in1=m3(MstrT), op=ALU.mult), reads=[dec[1], tri[1]], writes=[decS[1]])
            S.op("pool", lambda e: e.tensor_tensor(out=decI[0][:], in0=dec[0][:], in1=m3(MincT), op=ALU.mult), reads=[dec[1], tri[1]], writes=[decI[1]])
            bkk, bkq = nb(), nb()

            def mmk(e):
                for h in range(4):
                    i = e.matmul(out=bank(k, bkk, 128, h * 128), lhsT=kTq[0][:, h, :], rhs=kTq[0][:, h, :], start=True, stop=True)
                return i
            S.op("pe", mmk, reads=[kTq[1]], writes=[k.PB[bkk]])

            def mmq(e):
                for h in range(4):
                    i = e.matmul(out=bank(k, bkq, 128, h * 128), lhsT=kTq[0][:, h, :], rhs=kTq[0][:, 4 + h, :], start=True, stop=True)
                return i
            S.op("pe", mmq, reads=[kTq[1]], writes=[k.PB[bkq]])
            tmp, Bp, qkT = w["tmp"], w["Bp"], w["qkT"]
            S.op("dve", lambda e: e.tensor_tensor(out=tmp[0][:], in0=bk4(bkk), in1=decS[0][:], op=ALU.mult), reads=[k.PB[bkk], decS[1]], writes=[tmp[1]])
            S.op("pool", lambda e: e.tensor_tensor(out=Bp[0][:], in0=tmp[0][:], in1=b3(b4), op=ALU.mult), reads=[tmp[1], BETA[1]], writes=[Bp[1]])
            S.op("dve", lambda e: e.tensor_tensor(out=qkT[0][:], in0=bk4(bkq), in1=decI[0][:], op=ALU.mult), reads=[k.PB[bkq], decI[1]], writes=[qkT[1]])
            Q = [w["Q0"], w["Q1"]]
            QT = [Bp, w["QT1"]]
            R = [w["R0"], w["R1"]]
            ba = nb()

            def tra(e):
                for h in range(4):
                    i = e.transpose(out=bank(k, ba, 128, h * 128), in_=Bp[0][:, h, :], identity=identf[0][:])
                return i
            S.op("pe", tra, reads=[Bp[1], identf[1]], writes=[k.PB[ba]])
            S.op("act", lambda e: e.activation(out=Q[0][0][:], in_=bk4(ba), func=AF.Copy), reads=[k.PB[ba]], writes=[Q[0][1]])
            S.op("pool", lambda e: e.tensor_tensor(out=R[0][0][:], in0=m3(identf[0][:]), in1=Bp[0][:], op=ALU.subtract),
                 reads=[identf[1], Bp[1]], writes=[R[0][1]])
            Qb = [w["Qb0"], w["Qb1"]]
            QTb = [w["QTb0"], w["QTb1"]]
            Rbb = [w["Rb"], w["Rb1"]]
            qc, qtc, rc = Q[0], QT[0], R[0]
            f32_q = [Q[1], Q[0]]
            f32_qt = [w["QT1"], w["tmp"]]
            f32_r = [R[1], R[0]]
            for lev in range(1, 7):
                lowp = lev >= 4
                b1 = nb()

                def mq(e, qc=qc, qtc=qtc, b1=b1):
                    for h in range(4):
                        i = e.matmul(out=bank(k, b1, 128, h * 128), lhsT=qtc[0][:, h, :], rhs=qc[0][:, h, :], start=True, stop=True)
                    return i
                S.op("pe", mq, reads=[qtc[1], qc[1]], writes=[k.PB[b1]])
                if lev < 6:
                    b2 = nb()

                    def mqt(e, qc=qc, qtc=qtc, b2=b2):
                        for h in range(4):
                            i = e.matmul(out=bank(k, b2, 128, h * 128), lhsT=qc[0][:, h, :], rhs=qtc[0][:, h, :], start=True, stop=True)
                        return i
                    S.op("pe", mqt, reads=[qtc[1], qc[1]], writes=[k.PB[b2]])
                qn_ = Qb[lev % 2] if lowp else f32_q[(lev - 1) % 2]
                S.op("act", lambda e, qn_=qn_, b1=b1: e.activation(out=qn_[0][:], in_=bk4(b1), func=AF.Copy), reads=[k.PB[b1]], writes=[qn_[1]])
                q_next = qn_
                if lev == 3:
                    q_next = Qb[1]
                    S.op("act", lambda e, q_next=q_next, b1=b1: e.activation(out=q_next[0][:], in_=bk4(b1), func=AF.Copy),
                         reads=[k.PB[b1]], writes=[q_next[1]])
                if lev < 6:
                    qt_new = QTb[lev % 2] if lev >= 3 else f32_qt[(lev - 1) % 2]
                    S.op("dve", lambda e, qt_new=qt_new, b2=b2: e.tensor_copy(out=qt_new[0][:], in_=bk4(b2)), reads=[k.PB[b2]], writes=[qt_new[1]])
                    qtc = qt_new
                b3_ = nb()

                def mr(e, qn_=qn_, rc=rc, b3_=b3_):
                    for h in range(4):
                        i = e.matmul(out=bank(k, b3_, 128, h * 128), lhsT=qn_[0][:, h, :], rhs=rc[0][:, h, :], start=True, stop=True)
                    return i
                S.op("pe", mr, reads=[qn_[1], rc[1]], writes=[k.PB[b3_]])
                rn = Rbb[lev % 2] if lev >= 3 else f32_r[(lev - 1) % 2]
                S.op("dve", lambda e, rc=rc, rn=rn, b3_=b3_: e.tensor_tensor(out=rn[0][:], in0=bk4(b3_), in1=rc[0][:], op=ALU.add),
                     reads=[k.PB[b3_], rc[1]], writes=[rn[1]])
                rc = rn
                qc = q_next
            Rb = rc
            ub_, wT = w["ub"], w["wT"]
            bu, bw = nb(), nb()

            def mu(e):
                for h in range(4):
                    i = e.matmul(out=bank(k, bu, 128, h * 128), lhsT=Rb[0][:, h, :], rhs=A3[:, 8 + h, :], start=True, stop=True)
                return i
            S.op("pe", mu, reads=[Rb[1], At[1]], writes=[k.PB[bu]])

            def mw(e):
                for h in range(4):
                    i = e.matmul(out=bank(k, bw, 128, h * 128), lhsT=kg[0][:, h, :], rhs=Rb[0][:, h, :], start=True, stop=True)
                return i
            S.op("pe", mw, reads=[Rb[1], kg[1]], writes=[k.PB[bw]])
            S.op("dve", lambda e: e.tensor_tensor(out=ub_[0][:], in0=bk4(bu), in1=b3(b4), op=ALU.mult), reads=[k.PB[bu], BETA[1]], writes=[ub_[1]])
            S.op("act", lambda e: e.activation(out=wT[0][:], in_=bk4(bw), func=AF.Copy), reads=[k.PB[bw]], writes=[wT[1]])
            S32, Sb, vnew, osb = w["S32"], w["Sb"], w["vnew"], w["osb"]
            tmp2 = w["dec"]
            b1 = nb()

            def m1(e):
                for h in range(4):
                    i = e.matmul(out=bank(k, b1, 128, h * 128), lhsT=wT[0][:, h, :], rhs=Sb[0][:, h, :], start=True, stop=True)
                return i
            S.op("pe", m1, reads=[wT[1], Sb[1]], writes=[k.PB[b1]])
            S.op("dve", lambda e: e.tensor_tensor(out=tmp2[0][:], in0=bk4(b1), in1=b3(b4), op=ALU.mult), reads=[k.PB[b1], BETA[1]], writes=[tmp2[1]])
            S.op("dve", lambda e: e.tensor_tensor(out=vnew[0][:], in0=ub_[0][:], in1=tmp2[0][:], op=ALU.subtract),
                 reads=[ub_[1], tmp2[1]], writes=[vnew[1]])
            b2, b3b = nb(), nb()

            def m2(e):
                for h in range(4):
                    e.matmul(out=bank(k, b2, 128, h * 128), lhsT=qdT[0][:, h, :], rhs=Sb[0][:, h, :], start=True, stop=False)
                    i = e.matmul(out=bank(k, b2, 128, h * 128), lhsT=qkT[0][:, h, :], rhs=vnew[0][:, h, :], start=False, stop=True)
                return i
            S.op("pe", m2, reads=[qdT[1], Sb[1], qkT[1], vnew[1]], writes=[k.PB[b2]])

            def m3_(e):
                for h in range(4):
                    i = e.matmul(out=bank(k, b3b, 128, h * 128), lhsT=kd[0][:, h, :], rhs=vnew[0][:, h, :], start=True, stop=True)
                return i
            S.op("pe", m3_, reads=[kd[1], vnew[1]], writes=[k.PB[b3b]])
            S.op("act", lambda e: e.activation(out=osb[0][:], in_=bk4(b2), func=AF.Copy), reads=[k.PB[b2]], writes=[osb[1]])
            S.dma("act", d["scr_o"][dd, n * 128:(n + 1) * 128, :], osb[0][:].rearrange("p h n -> p (h n)"), reads=[osb[1]])
            S.op("dve", lambda e: e.tensor_tensor(out=tmp2[0][:], in0=S32[0][:], in1=b3(glast), op=ALU.mult),
                 reads=[S32[1], ex12[1]], writes=[tmp2[1]])
            S.op("dve", lambda e: e.tensor_tensor(out=S32[0][:], in0=bk4(b3b), in1=tmp2[0][:], op=ALU.add),
                 reads=[k.PB[b3b], tmp2[1]], writes=[S32[1]])
            S.op("act", lambda e: e.activation(out=Sb[0][:], in_=S32[0][:], func=AF.Copy), reads=[S32[1]], writes=[Sb[1]])

        jobs = getattr(k, "jobs", [])
        for step in range(NT):
            ra, rb = Rec(), Rec()
            njob = (len(jobs) + (NT - step) - 1) // (NT - step) if jobs else 0
            for _ in range(njob):
                ra.ops.append(jobs.pop(0))
            delta_tile(0, step, ra)
            delta_tile(1, NT - 1 - step, rb)
            for i in range(max(len(ra.ops), len(rb.ops))):
                if i < len(ra.ops):
                    ra.ops[i]()
                if i < len(rb.ops):
                    rb.ops[i]()
        S.barrier()


def layer_ab_out(k, src, dst):
    S = k.S
    d = k.d
    with contextlib.ExitStack() as st:
        wout = _sb(k, st, "o_wout", [128, 8, D], BF16)
        GO = _sb(k, st, "o_GO", [128, 128], F32)
        ofs = [_sb(k, st, f"o_of{i}", [128, 4, 128], F32) for i in range(2)]
        obs = [_sb(k, st, f"o_ob{i}", [128, 4, 128], F32) for i in range(2)]
        szs = [_sb(k, st, f"o_sz{i}", [128, 4, 128], BF16) for i in range(2)]
        mixs = [_sb(k, st, f"o_mix{i}", [128, D], BF16) for i in range(2)]
        hts = [_sb(k, st, f"o_ht{i}", [128, D], F32) for i in range(2)]
        hos = [_sb(k, st, f"o_ho{i}", [128, D], F32) for i in range(2)]
        sqs = [_sb(k, st, f"o_sq{i}", [128, 4, 128], F32) for i in range(2)]
        ssqs = [_sb(k, st, f"o_ssq{i}", [128, 4], F32) for i in range(2)]
        mixTs = [_sb(k, st, f"o_mixT{i}", [128, 8, 128], BF16) for i in range(2)]
        load_weight_bf16(k, wout, d["w_out_ab"], 8, D)
        S.dma("sp", GO[0][:], d["a_out_norm_g"].partition_broadcast(128), writes=[GO[1]])
        def body(S, t):
            of, ob, sz, mix, ht, ho = ofs[t % 2], obs[t % 2], szs[t % 2], mixs[t % 2], hts[t % 2], hos[t % 2]
            sq, ssq, mixT = sqs[t % 2], ssqs[t % 2], mixTs[t % 2]
            tbk = 0 if t % 2 == 0 else 5
            rows = slice(t * 128, (t + 1) * 128)
            S.dma("sp", of[0][:].rearrange("p h n -> p (h n)"), d["scr_o"][0, rows, :], writes=[of[1]])
            S.dma("sp", ob[0][:].rearrange("p h n -> p (h n)"), d["scr_o"][1, rows, :], writes=[ob[1]])
            S.dma("sp", sz[0][:].rearrange("p h n -> p (h n)"), d["scr_z"][rows, :], writes=[sz[1]])
            S.dma("sp", mix[0][:, 512:1024], d["scr_yb"][rows, :], writes=[mix[1]])
            S.dma("sp", ht[0][:], src(t), writes=[ht[1]])
            S.op("dve", lambda e: e.tensor_tensor(out=of[0][:], in0=of[0][:], in1=ob[0][:], op=ALU.add), reads=[of[1], ob[1]], writes=[of[1]])
            S.op("act", lambda e: e.activation(out=sq[0][:], in_=of[0][:], func=AF.Square), reads=[of[1]], writes=[sq[1]])
            S.op("dve", lambda e: e.tensor_reduce(out=ssq[0][:], in_=sq[0][:], axis=AX.X, op=ALU.add), reads=[sq[1]], writes=[ssq[1]])
            S.op("act", lambda e: e.activation(out=ssq[0][:], in_=ssq[0][:], func=AF.Ln, scale=1.0 / 128, bias=EPS), reads=[ssq[1]], writes=[ssq[1]])
            S.op("act", lambda e: e.activation(out=ssq[0][:], in_=ssq[0][:], func=AF.Exp, scale=-0.5), reads=[ssq[1]], writes=[ssq[1]])
            S.op("dve", lambda e: e.tensor_tensor(out=sq[0][:], in0=of[0][:], in1=ssq[0][:].unsqueeze(2).broadcast_to([128, 4, 128]), op=ALU.mult),
                 reads=[of[1], ssq[1]], writes=[sq[1]])
            S.op("pool", lambda e: e.tensor_tensor(out=sq[0][:], in0=sq[0][:], in1=GO[0][:].unsqueeze(1).broadcast_to([128, 4, 128]), op=ALU.mult),
                 reads=[sq[1], GO[1]], writes=[sq[1]])
            S.op("dve", lambda e: e.tensor_tensor(out=mix[0][:, 0:512].rearrange("p (h n) -> p h n", h=4), in0=sq[0][:], in1=sz[0][:], op=ALU.mult),
                 reads=[sq[1], sz[1]], writes=[mix[1]])
            pT = bank_bf(k, tbk)

            def tr(e):
                for c in range(8):
                    i = e.transpose(out=pT[:, c * 128:(c + 1) * 128], in_=mix[0][:, c * 128:(c + 1) * 128], identity=k.ident[0][:])
                return i
            S.op("pe", tr, reads=[mix[1], k.ident[1]], writes=[k.PB[tbk]])
            S.op("act", lambda e: e.activation(out=mixT[0][:], in_=pT.rearrange("p (c n) -> p c n", c=8), func=AF.Copy),
                 reads=[k.PB[tbk]], writes=[mixT[1]])
            for nbk in range(2):
                b = 1 + nbk + 2 * (t % 2)

                def mm(e, nbk=nbk, b=b):
                    for c in range(8):
                        i = e.matmul(out=bank(k, b), lhsT=mixT[0][:, c, :], rhs=wout[0][:, c, nbk * 512:(nbk + 1) * 512],
                                     start=(c == 0), stop=(c == 7))
                    return i
                S.op("pe", mm, reads=[mixT[1], wout[1]], writes=[k.PB[b]])
                S.op("dve", lambda e, nbk=nbk, b=b: e.tensor_tensor(out=ho[0][:, nbk * 512:(nbk + 1) * 512], in0=bank(k, b),
                                                                    in1=ht[0][:, nbk * 512:(nbk + 1) * 512], op=ALU.add),
                     reads=[k.PB[b], ht[1]], writes=[ho[1]])
            if t == 0:
                S.op("dve", lambda e: e.tensor_scalar(out=ho[0][:], in0=ho[0][:], scalar1=k.padmask[0][:, 0:1], scalar2=None, op0=ALU.mult),
                     reads=[ho[1], k.padmask[1]], writes=[ho[1]])
            S.dma("act", dst(t), ho[0][:], reads=[ho[1]])
        emit_skewed(k, [(lambda S, t=t: body(S, t)) for t in range(NT)])
        S.barrier()
```

```python
import contextlib
import numpy as np
import concourse.bass as bass
import concourse.mybir as mybir
from concourse.bass_utils import run_bass_kernel_spmd

F32 = mybir.dt.float32
BF16 = mybir.dt.bfloat16
AF = mybir.ActivationFunctionType
ALU = mybir.AluOpType
AX = mybir.AxisListType

D = 1024
SEQ = 4096
NMETA = 16
FP = 112
LP = 4224
NT = 33
HSMUL = 1
DFF = 4096
EPS = 1e-6
IN_AB = 2832
IN_C = 1536


class Res:
    __slots__ = ("name", "w", "r")

    def __init__(self, name):
        self.name = name
        self.w = None
        self.r = []


class Sched:
    NSLOT = 12

    def __init__(self, nc, stack):
        self.nc = nc
        self.eng = {"pe": nc.tensor, "act": nc.scalar, "dve": nc.vector,
                    "pool": nc.gpsimd, "sp": nc.sync}
        self.count = {e: 0 for e in self.eng}
        self.waited = {e: {} for e in self.eng}
        self.sem = {}
        for e in self.eng:
            self.sem[e] = stack.enter_context(nc.semaphore("s_" + e))
        self.dslot = {}
        for q in ("sp", "act", "pool"):
            for s in range(self.NSLOT):
                self.sem[("d", q, s)] = stack.enter_context(nc.semaphore(f"d_{q}_{s}"))
                self.dslot[(q, s)] = 0
        self.dnext = {q: 0 for q in ("sp", "act", "pool")}
        self.n_ops = 0

    def _deps(self, reads, writes):
        deps = []
        for r in reads:
            if r.w is not None:
                deps.append(r.w)
        for w in writes:
            if w.w is not None:
                deps.append(w.w)
            deps.extend(w.r)
        return deps

    def _waits(self, e, deps):
        out = []
        wd = self.waited[e]
        best = {}
        for (k, v) in deps:
            if wd.get(k, 0) >= v or (e == "pe" and k == "pe"):
                continue
            if best.get(k, 0) < v:
                best[k] = v
        for k, v in best.items():
            wd[k] = v
            out.append((k, v))
        return out

    def _mark(self, ev, reads, writes):
        for r in reads:
            r.r.append(ev)
        for w in writes:
            w.w = ev
            w.r = []

    def _run(self, e, waits, fn, sig):
        eng = self.eng[e]
        for (k, v) in waits:
            eng.wait_ge(self.sem[k], v)
        if fn is None:
            return
        ins = fn(eng)
        ins.then_inc(self.sem[sig[0]], sig[1])

    def op(self, e, fn, reads=(), writes=()):
        waits = self._waits(e, self._deps(reads, writes))
        self.count[e] += 1
        ev = (e, self.count[e])
        self._run(e, waits, fn, (e, 1))
        self._mark(ev, reads, writes)
        self.n_ops += 1
        return ev

    def dma(self, q, out, in_, reads=(), writes=(), **kw):
        s = self.dnext[q]
        self.dnext[q] = (s + 1) % self.NSLOT
        key = ("d", q, s)
        deps = self._deps(reads, writes)
        prev = self.dslot[(q, s)]
        if prev:
            deps.append((key, prev))
        waits = self._waits(q, deps)
        val = prev + 16
        self.dslot[(q, s)] = val
        ev = (key, val)
        self._run(q, waits, lambda eng: eng.dma_start(out=out, in_=in_, **kw), (key, 16))
        self._mark(ev, reads, writes)
        self.n_ops += 1
        return ev

    def _all_events(self):
        deps = []
        for (q, s), v in self.dslot.items():
            if v:
                deps.append((("d", q, s), v))
        for e in ("pe", "act", "dve", "pool"):
            if self.count[e]:
                deps.append((e, self.count[e]))
        return deps

    def barrier(self):
        deps = self._all_events()
        for e in self.eng:
            self._run(e, self._waits(e, deps), None, None)

    def finish(self):
        self._run("sp", self._waits("sp", self._all_events()), None, None)


class K:
    pass


class Rec:
    def __init__(self, k):
        self.k = k
        self.ops = []

    def op(self, *a, **kw):
        self.ops.append(lambda: self.k.S.op(*a, **kw))

    def dma(self, *a, **kw):
        self.ops.append(lambda: self.k.S.dma(*a, **kw))

    def raw(self, fn):
        self.ops.append(fn)


def emit_lockstep(k, bodies, d=2):
    for g0 in range(0, len(bodies), d):
        recs = []
        for b in bodies[g0:g0 + d]:
            r = Rec(k)
            b(r)
            recs.append(r.ops)
        for p in range(max(len(o) for o in recs)):
            for ops in recs:
                if p < len(ops):
                    ops[p]()


def emit_skewed(k, bodies, depth=2):
    recs = []
    for b in bodies:
        r = Rec(k)
        b(r)
        recs.append(r.ops)
    if not recs:
        return
    L = max(len(o) for o in recs)
    step = max(1, (L + depth - 1) // depth)
    items = []
    for t, ops in enumerate(recs):
        for p, f in enumerate(ops):
            items.append((t * step + p, t, f))
    items.sort(key=lambda x: (x[0], x[1]))
    for it in items:
        it[2]()


def _sb(k, st, name, shape, dt):
    k.uid = getattr(k, "uid", 0) + 1
    t = st.enter_context(k.nc.sbuf_tensor(f"sb{k.uid}_{name}", shape, dt))
    return t, Res(name)


def bank(k, b, n=512, off=0):
    return k.ps[:, b * 512 + off:b * 512 + off + n]


def bank_bf(k, b):
    return k.psb[:, b * 1024:(b + 1) * 1024]


def load_weight_bf16(k, dst, src, kchunks, ncols):
    S = k.S
    step = min(ncols, 2048)
    for c in range(kchunks):
        for n0 in range(0, ncols, step):
            n1 = min(ncols, n0 + step)
            S.dma("pool", dst[0][:, c, n0:n1], src[c * 128:(c + 1) * 128, n0:n1], writes=[dst[1]])


def rms_to_uT(k, ht, gt, ub, stat, uT_ap, uT_res, tbank, pad0=False, S=None):
    S = S or k.S
    ss, rs = stat
    S.op("act", lambda e: e.activation(out=ub[0][:], in_=ht[0][:], func=AF.Square, accum_out=ss[0][:]),
         reads=[ht[1]], writes=[ub[1], ss[1]])
    S.op("act", lambda e: e.activation(out=rs[0][:], in_=ss[0][:], func=AF.Ln, scale=1.0 / D, bias=EPS),
         reads=[ss[1]], writes=[rs[1]])
    S.op("act", lambda e: e.activation(out=rs[0][:], in_=rs[0][:], func=AF.Exp, scale=-0.5), reads=[rs[1]], writes=[rs[1]])
    S.op("dve", lambda e: e.scalar_tensor_tensor(out=ub[0][:], in0=ht[0][:], scalar=rs[0][:, 0:1], in1=gt[0][:],
                                                 op0=ALU.mult, op1=ALU.mult),
         reads=[ht[1], rs[1], gt[1]], writes=[ub[1]])
    pT = bank_bf(k, tbank)

    def tr(e):
        for c in range(8):
            i = e.transpose(out=pT[:, c * 128:(c + 1) * 128], in_=ub[0][:, c * 128:(c + 1) * 128],
                            identity=k.ident[0][:])
        return i
    S.op("pe", tr, reads=[ub[1], k.ident[1]], writes=[k.PB[tbank]])
    S.op("act", lambda e: e.activation(out=uT_ap, in_=pT.rearrange("p (c n) -> p c n", c=8), func=AF.Copy),
         reads=[k.PB[tbank]], writes=[uT_res])


def convert_jobs(k):
    S = k.S
    d = k.d
    k.wres = {(l, i): Res(f"scrw{l}{i}") for l in range(2) for i in range(2)}
    jobs = []
    for l in range(2):
        for c in range(8):
            for n0 in range(0, DFF, 2048):
                jobs.append(lambda l=l, c=c, n0=n0: S.dma("pool", d["scr_w1"][l, c * 128:(c + 1) * 128, n0:n0 + 2048],
                                                         d["w_ff1"][l, c * 128:(c + 1) * 128, n0:n0 + 2048], writes=[k.wres[(l, 0)]]))
        for f in range(0, 32, 2):
            jobs.append(lambda l=l, f=f: S.dma("pool", d["scr_w2"][l, f * 128:(f + 2) * 128, :].rearrange("(a p) n -> p a n", p=128),
                                               d["w_ff2"][l, f * 128:(f + 2) * 128, :].rearrange("(a p) n -> p a n", p=128),
                                               writes=[k.wres[(l, 1)]]))
    return jobs


def mlp_phase(k, src, g_row, w1d, w2d, out_fn, tiles, l=None):
    S = k.S
    nc = k.nc
    with contextlib.ExitStack() as st:
        w1 = _sb(k, st, "m_w1", [128, 8, DFF], BF16)
        w2 = _sb(k, st, "m_w2", [128, 32, D], BF16)
        gt = _sb(k, st, "m_gt", [128, D], F32)
        hts = [[_sb(k, st, f"m_ht{a}{b}", [128, D], F32) for b in range(2)] for a in range(2)]
        hos = [_sb(k, st, f"m_ho{b}", [128, D], F32) for b in range(2)]
        ubs = [_sb(k, st, f"m_ub{j}", [128, D], BF16) for j in range(2)]
        stats = [(_sb(k, st, f"m_ss{j}", [128, 1], F32), _sb(k, st, f"m_rs{j}", [128, 1], F32)) for j in range(2)]
        uTs = [_sb(k, st, f"m_uT{a}", [128, 8, 256], BF16) for a in range(2)]
        aT = _sb(k, st, "m_aT", [128, 32, 256], BF16)
        aTr = [Res(f"m_aT{f}") for f in range(32)]
        rr = [_sb(k, st, f"m_r{i}", [128, 512], F32) for i in range(3)]
        stat = (_sb(k, st, "m_ss", [128, 1], F32), _sb(k, st, "m_rs", [128, 1], F32))
        S.dma("sp", gt[0][:], g_row.partition_broadcast(128), writes=[gt[1]])
        if l is not None and getattr(k, "wres", None):
            for c in range(8):
                S.dma("sp" if c % 2 == 0 else "act", w1[0][:, c, :], k.d["scr_w1"][l, c * 128:(c + 1) * 128, :],
                      reads=[k.wres[(l, 0)]], writes=[w1[1]])
            for f in range(0, 32, 4):
                S.dma("sp" if (f // 4) % 2 == 0 else "act", w2[0][:, f:f + 4, :],
                      k.d["scr_w2"][l, f * 128:(f + 4) * 128, :].rearrange("(a p) n -> p a n", p=128),
                      reads=[k.wres[(l, 1)]], writes=[w2[1]])
        else:
            load_weight_bf16(k, w1, w1d, 8, DFF)
            load_weight_bf16(k, w2, w2d, 32, D)
        groups = [tiles[i:i + 2] for i in range(0, len(tiles), 2)]
        hb = [Res(f"m_hb{i}") for i in range(4)]

        def prep_a(gi):
            grp = groups[gi]
            a = gi % 2
            for j, t in enumerate(grp):
                ht = hts[a][j]
                ub = ubs[j]
                ss, rs = stats[j]
                S.dma("sp", ht[0][:], src(t), writes=[ht[1]])
                S.op("act", lambda e, ub=ub, ht=ht, ss=ss: e.activation(out=ub[0][:], in_=ht[0][:], func=AF.Square, accum_out=ss[0][:]),
                     reads=[ht[1]], writes=[ub[1], ss[1]])
                S.op("act", lambda e, ss=ss, rs=rs: e.activation(out=rs[0][:], in_=ss[0][:], func=AF.Ln, scale=1.0 / D, bias=EPS),
                     reads=[ss[1]], writes=[rs[1]])
                S.op("act", lambda e, rs=rs: e.activation(out=rs[0][:], in_=rs[0][:], func=AF.Exp, scale=-0.5), reads=[rs[1]], writes=[rs[1]])
                S.op("dve", lambda e, ub=ub, ht=ht, rs=rs: e.scalar_tensor_tensor(out=ub[0][:], in0=ht[0][:], scalar=rs[0][:, 0:1], in1=gt[0][:],
                                                                                 op0=ALU.mult, op1=ALU.mult),
                     reads=[ht[1], rs[1], gt[1]], writes=[ub[1]])

        def prep_b(gi):
            grp = groups[gi]
            a = gi % 2
            for j, t in enumerate(grp):
                ub = ubs[j]
                pT = bank_bf(k, 0)

                def tr(e, ub=ub, pT=pT):
                    for c in range(8):
                        i = e.transpose(out=pT[:, c * 128:(c + 1) * 128], in_=ub[0][:, c * 128:(c + 1) * 128], identity=k.ident[0][:])
                    return i
                S.op("pe", tr, reads=[ub[1], k.ident[1]], writes=[k.PB[0]])
                S.op("act", lambda e, j=j, pT=pT: e.activation(out=uTs[a][0][:, :, j * 128:(j + 1) * 128],
                                                               in_=pT.rearrange("p (c n) -> p c n", c=8), func=AF.Copy),
                     reads=[k.PB[0]], writes=[uTs[a][1]])

        def ff1(gi):
            grp = groups[gi]
            a = gi % 2
            n = 128 * len(grp)
            for fp in range(16):
                b = 1 + fp % 2
                pa = bank(k, b).rearrange("p (two n) -> p two n", two=2)[:, :, 0:n]

                def mm(e, fp=fp, b=b):
                    for ff in range(2):
                        f = 2 * fp + ff
                        for c in range(8):
                            i = e.matmul(out=bank(k, b, n, ff * 256), lhsT=w1[0][:, c, f * 128:(f + 1) * 128],
                                         rhs=uTs[a][0][:, c, 0:n], start=(c == 0), stop=(c == 7))
                    return i
                S.op("pe", mm, reads=[w1[1], uTs[a][1]], writes=[k.PB[b]])
                r = rr[fp % 3]
                rv = r[0][:].rearrange("p (two n) -> p two n", two=2)[:, :, 0:n]
                S.op("act", lambda e, pa=pa, rv=rv: e.activation(out=rv, in_=pa, func=AF.Relu),
                     reads=[k.PB[b]], writes=[r[1]])
                S.op("pool", lambda e, rv=rv, fp=fp: e.tensor_tensor(out=aT[0][:, 2 * fp:2 * fp + 2, 0:n], in0=rv, in1=rv,
                                                                   op=ALU.mult),
                     reads=[r[1]], writes=[aTr[2 * fp], aTr[2 * fp + 1]])

        def ff2(gi):
            grp = groups[gi]
            a = gi % 2
            for j, t in enumerate(grp):
                for nb in range(2):
                    b = 3 + 2 * j + nb

                    def mm(e, j=j, nb=nb, b=b):
                        for f in range(32):
                            i = e.matmul(out=bank(k, b), lhsT=aT[0][:, f, j * 128:(j + 1) * 128],
                                         rhs=w2[0][:, f, nb * 512:(nb + 1) * 512], start=(f == 0), stop=(f == 31))
                        return i
                    S.op("pe", mm, reads=[w2[1]] + aTr, writes=[k.PB[b]])
                    S.op("dve", lambda e, j=j, nb=nb, b=b: e.tensor_tensor(
                        out=hos[j][0][:, nb * 512:(nb + 1) * 512], in0=bank(k, b),
                        in1=hts[a][j][0][:, nb * 512:(nb + 1) * 512], op=ALU.add),
                        reads=[k.PB[b], hts[a][j][1]], writes=[hos[j][1]])
                S.dma("act", out_fn(t), hos[j][0][:], reads=[hos[j][1]])

        prep_a(0)
        prep_b(0)
        for gi in range(len(groups)):
            if gi + 1 < len(groups):
                prep_a(gi + 1)
            ff1(gi)
            if gi + 1 < len(groups):
                prep_b(gi + 1)
            ff2(gi)
        S.barrier()


def layer_c(k, src, dst, l):
    S = k.S
    d = k.d
    with contextlib.ExitStack() as st:
        QT = _sb(k, st, "c_QT", [128, 8, LP], BF16)
        KT = _sb(k, st, "c_KT", [128, 2, LP], BF16)
        V = _sb(k, st, "c_V", [128, NT, 256], BF16)
        QTr = [Res(f"c_QT{t}") for t in range(NT)]
        with contextlib.ExitStack() as s1:
            wqkv = _sb(k, s1, "c_wqkv", [128, 8, IN_C], BF16)
            gt = _sb(k, s1, "c_gt", [128, D], F32)
            GQK = _sb(k, s1, "c_GQK", [128, 10, 128], F32)
            hts = [_sb(k, s1, f"c_ht{i}", [128, D], F32) for i in range(2)]
            css = [_sb(k, s1, f"c_cs{i}", [128, 128], F32) for i in range(2)]
            B2 = []
            for i in range(2):
                B2.append(dict(
                    ub=_sb(k, s1, f"c_ub{i}", [128, D], BF16), uT=_sb(k, s1, f"c_uT{i}", [128, 8, 128], BF16),
                    qkv=_sb(k, s1, f"c_qkv{i}", [128, IN_C], F32), sq=_sb(k, s1, f"c_sq{i}", [128, 10, 128], F32),
                    t1=_sb(k, s1, f"c_t1{i}", [128, 10, 64], F32), t2=_sb(k, s1, f"c_t2{i}", [128, 10, 64], F32),
                    qr=_sb(k, s1, f"c_qr{i}", [128, 10, 64, 2], BF16), ssq=_sb(k, s1, f"c_ssq{i}", [128, 10], F32),
                    stat=(_sb(k, s1, f"c_ss{i}", [128, 1], F32), _sb(k, s1, f"c_rs{i}", [128, 1], F32))))
            load_weight_bf16(k, wqkv, d["w_qkv_c"], 8, IN_C)
            S.dma("sp", gt[0][:], d["attn_norm_g"][l:l + 1, :].partition_broadcast(128), writes=[gt[1]])
            S.dma("sp", GQK[0][:, 0:8, :], d["c_q_norm_g"].partition_broadcast(128).unsqueeze(1).broadcast_to([128, 8, 128]),
                  writes=[GQK[1]])
            S.dma("sp", GQK[0][:, 8:10, :], d["c_k_norm_g"].partition_broadcast(128).unsqueeze(1).broadcast_to([128, 2, 128]),
                  writes=[GQK[1]])
            S.op("act", lambda e: e.mul(out=GQK[0][:, 0:8, :], in_=GQK[0][:, 0:8, :], mul=128.0 ** -0.5),
                 reads=[GQK[1]], writes=[GQK[1]])
            def body(S, t):
                ht = hts[t % 2]
                cs = css[t % 2]
                bb = B2[t % 2]
                ub, uT, qkv, sq, t1, t2, qr, ssq, stat = (bb[n] for n in ("ub", "uT", "qkv", "sq", "t1", "t2", "qr", "ssq", "stat"))
                qn = sq
                tb = 0 if t % 2 == 0 else 6
                qb_ = 4 if t % 2 == 0 else 7
                S.dma("sp", ht[0][:], src(t), writes=[ht[1]])
                S.dma("sp", cs[0][:], d["rope"][t * 128:(t + 1) * 128, :], writes=[cs[1]])
                rms_to_uT(k, ht, gt, ub, stat, uT[0][:], uT[1], tb, S=S)
                for nb in range(3):
                    def mm(e, nb=nb):
                        for c in range(8):
                            i = e.matmul(out=bank(k, 1 + nb), lhsT=uT[0][:, c, :], rhs=wqkv[0][:, c, nb * 512:(nb + 1) * 512],
                                         start=(c == 0), stop=(c == 7))
                        return i
                    S.op("pe", mm, reads=[uT[1], wqkv[1]], writes=[k.PB[1 + nb]])
                S.op("act", lambda e: e.activation(out=qkv[0][:], in_=k.ps[:, 512:2048], func=AF.Copy),
                     reads=[k.PB[1], k.PB[2], k.PB[3]], writes=[qkv[1]])
                if t == 0:
                    S.op("dve", lambda e: e.tensor_scalar(out=V[0][:, t, :], in0=qkv[0][:, 1280:1536], scalar1=k.padmask[0][:, 0:1],
                                                          scalar2=None, op0=ALU.mult),
                         reads=[qkv[1], k.padmask[1]], writes=[QTr[t]])
                else:
                    S.op("pool", lambda e, t=t: e.tensor_copy(out=V[0][:, t, :], in_=qkv[0][:, 1280:1536]),
                         reads=[qkv[1]], writes=[QTr[t]])
                qk3 = qkv[0][:, 0:1280].rearrange("p (h d) -> p h d", h=10)
                S.op("act", lambda e: e.activation(out=sq[0][:], in_=qk3, func=AF.Square), reads=[qkv[1]], writes=[sq[1]])
                S.op("dve", lambda e: e.tensor_reduce(out=ssq[0][:], in_=sq[0][:], axis=AX.X, op=ALU.add),
                     reads=[sq[1]], writes=[ssq[1]])
                S.op("act", lambda e: e.activation(out=ssq[0][:], in_=ssq[0][:], func=AF.Ln, scale=1.0 / 128, bias=EPS),
                     reads=[ssq[1]], writes=[ssq[1]])
                S.op("act", lambda e: e.activation(out=ssq[0][:], in_=ssq[0][:], func=AF.Exp, scale=-0.5), reads=[ssq[1]], writes=[ssq[1]])
                rb = ssq[0][:].unsqueeze(2).broadcast_to([128, 10, 128])
                S.op("dve", lambda e: e.tensor_tensor(out=qn[0][:], in0=qk3, in1=rb, op=ALU.mult),
                     reads=[qkv[1], ssq[1]], writes=[qn[1]])
                S.op("pool", lambda e: e.tensor_tensor(out=qn[0][:], in0=qn[0][:], in1=GQK[0][:], op=ALU.mult),
                     reads=[qn[1], GQK[1]], writes=[qn[1]])
                q4 = qn[0][:].rearrange("p h (i two) -> p h i two", two=2)
                x0 = q4[:, :, :, 0]
                x1 = q4[:, :, :, 1]
                cosb = cs[0][:, 0:64].unsqueeze(1).broadcast_to([128, 10, 64])
                sinb = cs[0][:, 64:128].unsqueeze(1).broadcast_to([128, 10, 64])
                S.op("dve", lambda e: e.tensor_tensor(out=t1[0][:], in0=x0, in1=cosb, op=ALU.mult),
                     reads=[qn[1], cs[1]], writes=[t1[1]])
                S.op("pool", lambda e: e.tensor_tensor(out=t2[0][:], in0=x1, in1=sinb, op=ALU.mult),
                     reads=[qn[1], cs[1]], writes=[t2[1]])
                S.op("dve", lambda e: e.tensor_tensor(out=qr[0][:, :, :, 0], in0=t1[0][:], in1=t2[0][:], op=ALU.subtract),
                     reads=[t1[1], t2[1]], writes=[qr[1]])
                S.op("dve", lambda e: e.tensor_tensor(out=t1[0][:], in0=x0, in1=sinb, op=ALU.mult),
                     reads=[qn[1], cs[1]], writes=[t1[1]])
                S.op("pool", lambda e: e.tensor_tensor(out=t2[0][:], in0=x1, in1=cosb, op=ALU.mult),
                     reads=[qn[1], cs[1]], writes=[t2[1]])
                S.op("dve", lambda e: e.tensor_tensor(out=qr[0][:, :, :, 1], in0=t1[0][:], in1=t2[0][:], op=ALU.add),
                     reads=[t1[1], t2[1]], writes=[qr[1]])
                qrf = qr[0][:].rearrange("p h i two -> p (h i two)")
                pa = bank_bf(k, qb_)
                pb = bank_bf(k, 5)

                def tr(e):
                    for h in range(8):
                        i = e.transpose(out=pa[:, h * 128:(h + 1) * 128], in_=qrf[:, h * 128:(h + 1) * 128], identity=k.ident[0][:])
                    return i
                S.op("pe", tr, reads=[qr[1], k.ident[1]], writes=[k.PB[qb_]])

                def trk(e):
                    for h in range(8, 10):
                        i = e.transpose(out=pb[:, (h - 8) * 128:(h - 7) * 128], in_=qrf[:, h * 128:(h + 1) * 128], identity=k.ident[0][:])
                    return i
                S.op("pe", trk, reads=[qr[1], k.ident[1]], writes=[k.PB[5]])
                S.op("act", lambda e, t=t: e.activation(out=QT[0][:, :, t * 128:(t + 1) * 128],
                                                        in_=pa.rearrange("p (h n) -> p h n", h=8), func=AF.Copy),
                     reads=[k.PB[qb_]], writes=[QTr[t]])
                S.op("dve", lambda e, t=t: e.tensor_copy(out=KT[0][:, :, t * 128:(t + 1) * 128],
                                                         in_=pb[:, 0:256].rearrange("p (h n) -> p h n", h=2)),
                     reads=[k.PB[5]], writes=[QTr[t]])
            emit_skewed(k, [(lambda S, t=t: body(S, t)) for t in range(NT)])
            S.barrier()
        with contextlib.ExitStack() as s2:
            wout = _sb(k, s2, "c_wout", [128, 8, D], BF16)
            OT = _sb(k, s2, "c_OT", [128, 8, 512], BF16)
            OTr = [Res(f"c_OT{h}") for h in range(8)]
            Pt = [_sb(k, s2, f"c_P{i}", [128, 2, 512], BF16) for i in range(4)]
            accs = [[_sb(k, s2, f"c_acc{i}{j}", [128, 2, 512], F32) for j in range(2)] for i in range(2)]
            onesf = _sb(k, s2, "c_onesf", [128, 128], F32)
            rden = [_sb(k, s2, f"c_rden{i}", [128, 512], F32) for i in range(2)]
            hts = [_sb(k, s2, f"c_h2{i}", [128, D], F32) for i in range(2)]
            hos = [_sb(k, s2, f"c_ho{i}", [128, D], F32) for i in range(2)]
            load_weight_bf16(k, wout, d["w_out_c"], 8, D)
            S.op("pool", lambda e: e.memset(onesf[0][:], 1.0), writes=[onesf[1]])
            nq = LP - 128
            groups = [(128 + g * 512, 512) for g in range(nq // 512)] + ([(128 + nq // 512 * 512, nq % 512)] if nq % 512 else [])
            pairs = [tuple(range(j, min(j + 2, NT))) for j in range(0, NT, 2)]
            SBP = [(0, 1), (6, 7)]
            combo = 0
            pcount = 0
            tcnt = [0]
            pending = []
            for (q0, qn_) in groups:
                for h in range(8):
                    kv = h // 4
                    bo = 2 + 2 * (combo % 2)
                    bd = bo + 1
                    acc = accs[combo % 2]
                    combo += 1
                    qsl = QT[0][:, h, q0:q0 + qn_]
                    qres = [QTr[tt] for tt in range(q0 // 128, (q0 + qn_) // 128)]

                    def smm(pi, kv=kv, qsl=qsl, qres=qres):
                        pr = pairs[pi]
                        bp = SBP[pi % 2]

                        def f(e):
                            for idx, j in enumerate(pr):
                                i = e.matmul(out=bank(k, bp[idx], qn_), lhsT=KT[0][:, kv, j * 128:(j + 1) * 128], rhs=qsl,
                                             start=True, stop=True)
                            return i
                        S.op("pe", f, reads=[QTr[j] for j in pr] + qres, writes=[k.PB[bp[idx]] for idx in range(len(pr))])
                    smm(0)
                    used = [False, False]
                    pe_started = [False]
                    for pi, pr in enumerate(pairs):
                        n = len(pr)
                        bp = SBP[pi % 2]
                        P = Pt[pcount % 4]
                        pcount += 1
                        sv = k.ps[:, bp[0] * 512:(bp[0] + 2) * 512].rearrange("p (two n) -> p two n", two=2)[:, 0:n, 0:qn_]
                        pv_ = P[0][:, 0:n, 0:qn_]
                        S.op("act", lambda e, sv=sv, pv_=pv_: e.activation(out=pv_, in_=sv, func=AF.Exp),
                             reads=[k.PB[bp[idx]] for idx in range(n)], writes=[P[1]])
                        if pi + 1 < len(pairs):
                            smm(pi + 1)
                        if pi == min(2, len(pairs) - 1):
                            while pending:
                                pending.pop(0)()

                        def pvf(e, P=P, pr=pr):
                            for idx, j in enumerate(pr):
                                i = e.matmul(out=bank(k, bo, qn_), lhsT=V[0][:, j, kv * 128:(kv + 1) * 128], rhs=P[0][:, idx, 0:qn_],
                                             start=(j == 0), stop=(j == NT - 1))
                            return i
                        S.op("pe", pvf, reads=[P[1]] + [QTr[j] for j in pr], writes=[k.PB[bo]])
                        mode = pi % 3
                        if mode == 0:
                            def df(e, P=P, pr=pr):
                                for idx, j in enumerate(pr):
                                    om = k.ones0 if j == 0 else k.ones
                                    i = e.matmul(out=bank(k, bd, qn_), lhsT=om[0][:], rhs=P[0][:, idx, 0:qn_],
                                                 start=(not pe_started[0]), stop=False)
                                    pe_started[0] = True
                                return i
                            S.op("pe", df, reads=[P[1], k.ones[1], k.ones0[1]], writes=[k.PB[bd]])
                        else:
                            ai = mode - 1
                            ac = acc[ai]
                            eng = "dve" if ai == 0 else "pool"
                            if not used[ai]:
                                used[ai] = True
                                if n < 2:
                                    S.op(eng, lambda e, ac=ac: e.memset(ac[0][:], 0.0), writes=[ac[1]])
                                S.op(eng, lambda e, ac=ac, pv_=pv_: e.tensor_copy(out=ac[0][:, 0:n, 0:qn_], in_=pv_),
                                     reads=[P[1]], writes=[ac[1]])
                                if n == 2 and qn_ < 512:
                                    pass
                            else:
                                S.op(eng, lambda e, ac=ac, pv_=pv_: e.tensor_tensor(out=ac[0][:, 0:n, 0:qn_], in0=ac[0][:, 0:n, 0:qn_],
                                                                                   in1=pv_, op=ALU.add),
                                     reads=[P[1], ac[1]], writes=[ac[1]])
                    def make_tail(acc=acc, used=list(used), bo=bo, bd=bd, rd=rden[combo % 2], h=h, qn_=qn_):
                        def tail():
                            srcs = [a_ for ai, a_ in enumerate(acc) if used[ai]]
                            if len(srcs) == 2:
                                S.op("pool", lambda e: e.tensor_tensor(out=acc[0][0][:, :, 0:qn_], in0=acc[0][0][:, :, 0:qn_],
                                                                       in1=acc[1][0][:, :, 0:qn_], op=ALU.add),
                                     reads=[acc[0][1], acc[1][1]], writes=[acc[0][1]])
                            if srcs:
                                def ff(e):
                                    e.matmul(out=bank(k, bd, qn_), lhsT=onesf[0][:], rhs=srcs[0][0][:, 0, 0:qn_], start=False, stop=False)
                                    return e.matmul(out=bank(k, bd, qn_), lhsT=onesf[0][:], rhs=srcs[0][0][:, 1, 0:qn_], start=False, stop=True)
                                S.op("pe", ff, reads=[onesf[1], srcs[0][1]], writes=[k.PB[bd]])
                            S.op("dve", lambda e: e.reciprocal(out=rd[0][:, 0:qn_], in_=bank(k, bd, qn_)),
                                 reads=[k.PB[bd]], writes=[rd[1]])
                            S.op("dve", lambda e: e.tensor_tensor(out=OT[0][:, h, 0:qn_], in0=bank(k, bo, qn_),
                                                                  in1=rd[0][:, 0:qn_], op=ALU.mult),
                                 reads=[k.PB[bo], rd[1]], writes=[OTr[h]])
                        return tail
                    pending.append(make_tail())
                def make_outproj(q0=q0, qn_=qn_, bpar=combo % 2):
                    def outproj():
                        for tt in range(qn_ // 128):
                            t = q0 // 128 + tt
                            ht = hts[tcnt[0] % 2]
                            ho = hos[tcnt[0] % 2]
                            tcnt[0] += 1
                            S.dma("sp", ht[0][:], src(t), writes=[ht[1]])
                            for nb in range(2):
                                b = 3 + 2 * (1 - bpar)

                                def mm(e, tt=tt, nb=nb, b=b):
                                    for h in range(8):
                                        i = e.matmul(out=bank(k, b), lhsT=OT[0][:, h, tt * 128:(tt + 1) * 128],
                                                     rhs=wout[0][:, h, nb * 512:(nb + 1) * 512], start=(h == 0), stop=(h == 7))
                                    return i
                                S.op("pe", mm, reads=OTr + [wout[1]], writes=[k.PB[b]])
                                S.op("dve", lambda e, nb=nb, b=b, ht=ht, ho=ho: e.tensor_tensor(
                                    out=ho[0][:, nb * 512:(nb + 1) * 512], in0=bank(k, b), in1=ht[0][:, nb * 512:(nb + 1) * 512],
                                    op=ALU.add), reads=[k.PB[b], ht[1]], writes=[ho[1]])
                            S.dma("act", dst(t), ho[0][:], reads=[ho[1]])
                    return outproj
                pending.append(make_outproj())
            while pending:
                pending.pop(0)()
            S.barrier()


def input_specs():
  return [
    ("x", [SEQ, D]), ("meta_tokens", [NMETA, D]), ("attn_norm_g", [2, D]), ("mlp_norm_g", [2, D]),
    ("w_in_ab", [D, IN_AB]), ("conv_w_a", [5, 1536]), ("a_log", [2, 4]), ("dt_bias", [2, 4]),
    ("a_out_norm_g", [128]), ("b_q_norm_g", [64]), ("b_k_norm_g", [64]), ("b_sink", [8]),
    ("w_out_ab", [D, D]), ("w_qkv_c", [D, IN_C]), ("c_q_norm_g", [128]), ("c_k_norm_g", [128]),
    ("w_out_c", [D, D]), ("w_ff1", [2, D, DFF]), ("w_ff2", [2, DFF, D]),
    ("rope", [LP, 128]), ("padmask", [128, 1]), ("etab", [3, 2, 128, 512]), ("tri", [4, 128, 128]),
]


def build_nc(layers=(0, 1), dbg="cm", mlp_tiles=None, dbg0="vdom"):
    nc = bass.Bass("TRN2", target_bir_lowering=False)
    k = K()
    k.nc = nc
    d = {}
    for name, shape in input_specs():
        d[name] = nc.dram_tensor(name, shape, F32, kind="ExternalInput").ap()
    out = nc.dram_tensor("out", [SEQ, D], F32, kind="ExternalOutput").ap()
    d["scr_z"] = nc.dram_tensor("scr_z", [LP, 512], BF16, kind="ExternalOutput").ap()
    d["scr_yb"] = nc.dram_tensor("scr_yb", [LP, 512], BF16, kind="ExternalOutput").ap()
    d["scr_a"] = nc.dram_tensor("scr_a", [LP, 1536], BF16, kind="ExternalOutput").ap()
    d["scr_o"] = nc.dram_tensor("scr_o", [2, LP, 512], F32, kind="ExternalOutput").ap()
    d["scr_w1"] = nc.dram_tensor("scr_w1", [2, D, DFF], BF16, kind="ExternalOutput").ap()
    d["scr_w2"] = nc.dram_tensor("scr_w2", [2, DFF, D], BF16, kind="ExternalOutput").ap()
    H0s = nc.dram_tensor("H0s", [128, D], F32, kind="Internal").ap()
    k.d = d
    with contextlib.ExitStack() as st:
        S = Sched(nc, st)
        k.S = S
        k.ps = st.enter_context(nc.psum_tensor("ps", [128, 4096], F32))
        k.psb = k.ps.bitcast(BF16)
        k.PB = [Res(f"PB{i}") for i in range(8)]
        k.ident = _sb(k, st, "ident", [128, 128], BF16)
        k.ones = _sb(k, st, "ones", [128, 128], BF16)
        k.ones0 = _sb(k, st, "ones0", [128, 128], BF16)
        k.padmask = _sb(k, st, "padmask", [128, 1], F32)
        zt = _sb(k, st, "zt", [128, D], F32)
        S.dma("sp", k.padmask[0][:], d["padmask"], writes=[k.padmask[1]])
        S.op("pool", lambda e: e.memset(k.ident[0][:], 0.0), writes=[k.ident[1]])
        S.op("pool", lambda e: e.affine_select(out=k.ident[0][:], in_=k.ident[0][:], pattern=[[-1, 128]],
                                               compare_op=ALU.not_equal, fill=1.0, base=0, channel_multiplier=1),
             reads=[k.ident[1]], writes=[k.ident[1]])
        S.op("pool", lambda e: e.memset(k.ones[0][:], 1.0), writes=[k.ones[1]])
        S.op("dve", lambda e: e.tensor_scalar(out=k.ones0[0][:], in0=k.ones[0][:], scalar1=k.padmask[0][:, 0:1], scalar2=None,
                                              op0=ALU.mult), reads=[k.ones[1], k.padmask[1]], writes=[k.ones0[1]])
        S.op("pool", lambda e: e.memset(zt[0][:], 0.0), writes=[zt[1]])
        S.dma("sp", H0s[0:FP, :], zt[0][0:FP, :], reads=[zt[1]])
        S.dma("sp", H0s[FP:128, :], d["meta_tokens"])
        S.barrier()

        def xin(t):
            return H0s if t == 0 else d["x"][(t - 1) * 128:t * 128, :]

        def hbuf(t):
            return H0s if t == 0 else out[(t - 1) * 128:t * 128, :]
        cur = xin
        if 0 in layers:
            k.jobs = convert_jobs(k)
            with contextlib.ExitStack() as sl:
                uT_all, uTr, BETA, G = layer_ab_front(k, cur, 0, sl)
                if "v" in dbg0:
                    layer_ab_conv(k, uT_all, uTr)
                if "d" in dbg0:
                    layer_ab_delta(k, BETA, G)
            if "o" in dbg0:
                layer_ab_out(k, cur, hbuf)
            if "m" in dbg0:
                mlp_phase(k, hbuf, d["mlp_norm_g"][0:1, :], d["w_ff1"][0], d["w_ff2"][0], hbuf, list(range(NT)), l=0)
            cur = hbuf
        if 1 in layers:
            layer_c(k, cur, hbuf, 1)
            mlp_phase(k, hbuf, d["mlp_norm_g"][1:2, :], d["w_ff1"][1], d["w_ff2"][1], hbuf, mlp_tiles or list(range(1, NT)), l=1)
        S.finish()
    return nc


def rope_table():
    rows = SEQ // 64
    row = np.repeat(np.arange(rows), 64)
    col = np.tile(np.arange(64), rows)
    meta = np.arange(NMETA) - NMETA
    row = np.concatenate([meta, row]).astype(np.float32)
    col = np.concatenate([meta, col]).astype(np.float32)
    freqs = (np.float32(10000.0) ** (-np.arange(0, 64, 2, dtype=np.float32) / np.float32(64))).astype(np.float32)
    ang = np.concatenate([row[:, None] * freqs, col[:, None] * freqs], axis=-1).astype(np.float32)
    tab = np.zeros((LP, 128), np.float32)
    tab[:FP, :64] = 1.0
    tab[FP:, :64] = np.cos(ang)
    tab[FP:, 64:] = np.sin(ang)
    return tab


def const_inputs():
    pm = np.ones((128, 1), np.float32)
    pm[:FP] = 0.0
    r = np.arange(128)[:, None]
    c = np.arange(128)[None, :]
    et = np.zeros((3, 2, 128, 512), np.float32)
    for off in (-1, 0, 1):
        dist = np.abs(128 * off + r - c).astype(np.float32)
        for g in range(2):
            for a in range(4):
                h = 4 * g + a
                slope = np.float32(2.0) ** np.float32(-8.0 * (h + 1.0) / 8.0)
                et[off + 1, g, :, a * 128:(a + 1) * 128] = np.where(dist <= 128, np.exp(-slope * dist), 0.0)
    tri = np.stack([(r <= c), (r >= c), (r < c), (r > c)]).astype(np.float32)
    return {"rope": rope_table(), "padmask": pm, "etab": et, "tri": tri}


def make_in_maps(inputs, ncores=8):
    c = const_inputs()
    sq = lambda a: np.ascontiguousarray(np.asarray(a, dtype=np.float32))
    shared = {
        "meta_tokens": sq(inputs["meta_tokens"]), "attn_norm_g": sq(inputs["attn_norm_g"]),
        "mlp_norm_g": sq(inputs["mlp_norm_g"]), "w_in_ab": sq(inputs["w_in_ab"][0]),
        "conv_w_a": sq(inputs["conv_w_a"][0]), "a_log": sq(inputs["a_log"][0]), "dt_bias": sq(inputs["dt_bias"][0]),
        "a_out_norm_g": sq(inputs["a_out_norm_g"][0]), "b_q_norm_g": sq(inputs["b_q_norm_g"][0]),
        "b_k_norm_g": sq(inputs["b_k_norm_g"][0]), "b_sink": sq(inputs["b_sink"][0]),
        "w_out_ab": sq(inputs["w_out_ab"][0]), "w_qkv_c": sq(inputs["w_qkv_c"][0]),
        "c_q_norm_g": sq(inputs["c_q_norm_g"][0]), "c_k_norm_g": sq(inputs["c_k_norm_g"][0]),
        "w_out_c": sq(inputs["w_out_c"][0]), "w_ff1": sq(inputs["w_ff1"]), "w_ff2": sq(inputs["w_ff2"]),
    }
    shared.update(c)
    maps = []
    for b in range(ncores):
        m = dict(shared)
        m["x"] = sq(inputs["x"][b])
        maps.append(m)
    return maps


def kernel(**inputs):
    nc = build_nc()
    in_maps = make_in_maps(inputs, 8)
    res = run_bass_kernel_spmd(nc, in_maps, core_ids=list(range(8)))
    return np.stack([np.asarray(r["out"], dtype=np.float32) for r in res.results], axis=0)


NTOK = 1296


def layer_ab_front(k, src, l, st):
    S = k.S
    d = k.d
    uT_all = _sb(k, st, "a_uTall", [128, 8, LP], BF16)
    uTr = [Res(f"a_uT{t}") for t in range(NT)]
    BETA = _sb(k, st, "a_BETA", [128, NT, 8], F32)
    G = _sb(k, st, "a_G", [128, NT, 8], F32)
    with contextlib.ExitStack() as sB:
        qT_all = _sb(k, sB, "b_qT", [128, NT, 512], BF16)
        kT_all = _sb(k, sB, "b_kT", [128, LP], BF16)
        Vaug = _sb(k, sB, "b_Vaug", [128, NT, 2, 65], BF16)
        Vm = _sb(k, sB, "b_Vm", [16, 2, 65], BF16)
        Br = [Res(f"b_t{t}") for t in range(NT)]
        with contextlib.ExitStack() as s1:
            wt = _sb(k, s1, "a_wtok", [128, 8, NTOK], BF16)
            gt = _sb(k, s1, "a_gt", [128, D], F32)
            GB = _sb(k, s1, "a_GB", [128, 10, 64], F32)
            dtb = _sb(k, s1, "a_dtb", [128, 8], F32)
            negA = _sb(k, s1, "a_negA", [128, 8], F32)
            hts = [_sb(k, s1, f"a_ht{i}", [128, D], F32) for i in range(2)]
            szs = [_sb(k, s1, f"a_sz{i}", [128, 512], BF16) for i in range(2)]
            A2 = []
            for i in range(2):
                A2.append(dict(
                    ub=_sb(k, s1, f"a_ub{i}", [128, D], BF16), pj=_sb(k, s1, f"a_pj{i}", [128, NTOK], F32),
                    sq=_sb(k, s1, f"a_sq{i}", [128, 10, 64], F32), qkn=_sb(k, s1, f"a_qkn{i}", [128, 10, 64], BF16),
                    ssq=_sb(k, s1, f"a_ssq{i}", [128, 10], F32), ga=_sb(k, s1, f"a_ga{i}", [128, 8], F32),
                    ez=_sb(k, s1, f"a_ez{i}", [128, 512], F32),
                    stat=(_sb(k, s1, f"a_ss{i}", [128, 1], F32), _sb(k, s1, f"a_rs{i}", [128, 1], F32))))
            win = d["w_in_ab"]
            for c in range(8):
                rows = slice(c * 128, (c + 1) * 128)
                S.dma("pool", wt[0][:, c, 0:528], win[rows, 1536:2064], writes=[wt[1]])
                for par in range(2):
                    S.dma("pool", wt[0][:, c, 528:1040].rearrange("p (a par e) -> p a par e", a=4, par=2)[:, :, par, :],
                          win[rows, 2064 + par * 256:2064 + (par + 1) * 256].rearrange("p (a e) -> p a e", a=4), writes=[wt[1]])
                S.dma("pool", wt[0][:, c, 1040:1296], win[rows, 2576:2832], writes=[wt[1]])
            S.dma("sp", gt[0][:], d["attn_norm_g"][l:l + 1, :].partition_broadcast(128), writes=[gt[1]])
            S.dma("sp", GB[0][:, 0:8, :], d["b_q_norm_g"].partition_broadcast(128).unsqueeze(1).broadcast_to([128, 8, 64]),
                  writes=[GB[1]])
            S.dma("sp", GB[0][:, 8:10, :], d["b_k_norm_g"].partition_broadcast(128).unsqueeze(1).broadcast_to([128, 2, 64]),
                  writes=[GB[1]])
            S.op("act", lambda e: e.mul(out=GB[0][:, 0:8, :], in_=GB[0][:, 0:8, :], mul=64.0 ** -0.5),
                 reads=[GB[1]], writes=[GB[1]])
            S.dma("sp", dtb[0][:], d["dt_bias"].rearrange("a b -> (a b)").partition_broadcast(128), writes=[dtb[1]])
            S.dma("sp", negA[0][:], d["a_log"].rearrange("a b -> (a b)").partition_broadcast(128), writes=[negA[1]])
            S.op("act", lambda e: e.activation(out=negA[0][:], in_=negA[0][:], func=AF.Exp), reads=[negA[1]], writes=[negA[1]])
            S.op("act", lambda e: e.mul(out=negA[0][:], in_=negA[0][:], mul=-1.0), reads=[negA[1]], writes=[negA[1]])
            S.op("pool", lambda e: e.memset(Vaug[0][:], 1.0), writes=Br)
            S.op("pool", lambda e: e.memset(Vm[0][:], 1.0), writes=[Br[0]])
            jobs = getattr(k, "jobs", [])
            per = 1 if jobs else 0

            def body(S, t):
                for _ in range(per):
                    if jobs:
                        S.raw(jobs.pop(0))
                ht = hts[t % 2]
                ab_ = A2[t % 2]
                ub, pj, sq, qkn, ssq, ga, stat = (ab_[n] for n in ("ub", "pj", "sq", "qkn", "ssq", "ga", "stat"))
                tb = 0 if t % 2 == 0 else 6
                qb_ = 5 if t % 2 == 0 else 7
                S.dma("sp", ht[0][:], src(t), writes=[ht[1]])
                rms_to_uT(k, ht, gt, ub, stat, uT_all[0][:, :, t * 128:(t + 1) * 128], uTr[t], tb, S=S)
                for nb, (c0, c1) in enumerate([(0, 512), (512, 1024), (1024, NTOK)]):
                    def mm(e, nb=nb, c0=c0, c1=c1):
                        for c in range(8):
                            i = e.matmul(out=bank(k, 1 + nb, c1 - c0), lhsT=uT_all[0][:, c, t * 128:(t + 1) * 128],
                                         rhs=wt[0][:, c, c0:c1], start=(c == 0), stop=(c == 7))
                        return i
                    S.op("pe", mm, reads=[uTr[t], wt[1]], writes=[k.PB[1 + nb]])
                S.op("act", lambda e: e.activation(out=pj[0][:], in_=k.ps[:, 512:512 + NTOK], func=AF.Copy),
                     reads=[k.PB[1], k.PB[2], k.PB[3]], writes=[pj[1]])
                if t == 0:
                    def mmv(e):
                        for c in range(8):
                            i = e.matmul(out=k.ps[0:16, 2048:2176], lhsT=uT_all[0][:, c, FP:128], rhs=wt[0][:, c, 1168:1296],
                                         start=(c == 0), stop=(c == 7))
                        return i
                    S.op("pe", mmv, reads=[uTr[0], wt[1]], writes=[k.PB[4]])
                    S.op("act", lambda e: e.activation(out=Vm[0][:, :, 0:64], in_=k.ps[0:16, 2048:2176].rearrange("p (g e) -> p g e", g=2),
                                                       func=AF.Copy), reads=[k.PB[4]], writes=[Br[0]])
                sz = szs[t % 2]
                S.op("act", lambda e, sz=sz: e.activation(out=sz[0][:], in_=pj[0][:, 0:512], func=AF.Silu), reads=[pj[1]], writes=[sz[1]])
                S.dma("act", d["scr_z"][t * 128:(t + 1) * 128, :], sz[0][:], reads=[sz[1]])
                S.op("act", lambda e, t=t: e.activation(out=BETA[0][:, t, :], in_=pj[0][:, 512:520], func=AF.Exp, scale=-1.0),
                     reads=[pj[1]], writes=[BETA[1]])
                S.op("dve", lambda e, t=t: e.tensor_scalar(out=BETA[0][:, t, :], in0=BETA[0][:, t, :], scalar1=1.0, scalar2=None, op0=ALU.add),
                     reads=[BETA[1]], writes=[BETA[1]])
                S.op("dve", lambda e, t=t: e.reciprocal(out=BETA[0][:, t, :], in_=BETA[0][:, t, :]), reads=[BETA[1]], writes=[BETA[1]])
                S.op("dve", lambda e: e.tensor_tensor(out=ga[0][:], in0=pj[0][:, 520:528], in1=dtb[0][:], op=ALU.add),
                     reads=[pj[1], dtb[1]], writes=[ga[1]])
                S.op("act", lambda e: e.activation(out=ga[0][:], in_=ga[0][:], func=AF.Exp), reads=[ga[1]], writes=[ga[1]])
                S.op("act", lambda e: e.activation(out=ga[0][:], in_=ga[0][:], func=AF.Ln, bias=1.0), reads=[ga[1]], writes=[ga[1]])
                S.op("dve", lambda e, t=t: e.tensor_tensor(out=G[0][:, t, :], in0=ga[0][:], in1=negA[0][:], op=ALU.mult),
                     reads=[ga[1], negA[1]], writes=[G[1]])
                if t == 0:
                    S.op("dve", lambda e: e.tensor_scalar(out=BETA[0][:, 0, :], in0=BETA[0][:, 0, :], scalar1=k.padmask[0][:, 0:1],
                                                          scalar2=None, op0=ALU.mult), reads=[BETA[1], k.padmask[1]], writes=[BETA[1]])
                    S.op("dve", lambda e: e.tensor_scalar(out=G[0][:, 0, :], in0=G[0][:, 0, :], scalar1=k.padmask[0][:, 0:1],
                                                          scalar2=None, op0=ALU.mult), reads=[G[1], k.padmask[1]], writes=[G[1]])
                qk3 = pj[0][:, 528:1168].rearrange("p (h e) -> p h e", h=10)
                S.op("act", lambda e: e.activation(out=sq[0][:], in_=qk3, func=AF.Square), reads=[pj[1]], writes=[sq[1]])
                S.op("dve", lambda e: e.tensor_reduce(out=ssq[0][:], in_=sq[0][:], axis=AX.X, op=ALU.add), reads=[sq[1]], writes=[ssq[1]])
                S.op("act", lambda e: e.activation(out=ssq[0][:], in_=ssq[0][:], func=AF.Ln, scale=1.0 / 64, bias=EPS),
                     reads=[ssq[1]], writes=[ssq[1]])
                S.op("act", lambda e: e.activation(out=ssq[0][:], in_=ssq[0][:], func=AF.Exp, scale=-0.5), reads=[ssq[1]], writes=[ssq[1]])
                S.op("dve", lambda e: e.tensor_tensor(out=sq[0][:], in0=qk3, in1=ssq[0][:].unsqueeze(2).broadcast_to([128, 10, 64]),
                                                      op=ALU.mult), reads=[pj[1], ssq[1]], writes=[sq[1]])
                S.op("pool", lambda e: e.tensor_tensor(out=qkn[0][:], in0=sq[0][:], in1=GB[0][:], op=ALU.mult),
                     reads=[sq[1], GB[1]], writes=[qkn[1]])
                qkf = qkn[0][:].rearrange("p h e -> p (h e)")
                pa = bank_bf(k, qb_)

                def tr(e):
                    for a in range(5):
                        i = e.transpose(out=pa[:, a * 128:(a + 1) * 128], in_=qkf[:, a * 128:(a + 1) * 128], identity=k.ident[0][:])
                    return i
                S.op("pe", tr, reads=[qkn[1], k.ident[1]], writes=[k.PB[qb_]])
                S.op("act", lambda e, t=t: e.activation(out=qT_all[0][:, t, :], in_=pa[:, 0:512], func=AF.Copy),
                     reads=[k.PB[qb_]], writes=[Br[t]])
                S.op("dve", lambda e, t=t: e.tensor_copy(out=kT_all[0][:, t * 128:(t + 1) * 128], in_=pa[:, 512:640]),
                     reads=[k.PB[qb_]], writes=[Br[t]])
                S.op("pool", lambda e, t=t: e.tensor_copy(out=Vaug[0][:, t, :, 0:64],
                                                          in_=pj[0][:, 1168:1296].rearrange("p (g e) -> p g e", g=2)),
                     reads=[pj[1]], writes=[Br[t]])
            emit_skewed(k, [(lambda S, t=t: body(S, t)) for t in range(NT)])
            S.barrier()
        with contextlib.ExitStack() as s2:
            ET = _sb(k, s2, "b_E", [128, 6, 512], F32)
            esk = _sb(k, s2, "b_esk", [128, 8], F32)
            exs = [_sb(k, s2, f"b_ex{i}", [128, 512], F32) for i in range(2)]
            Pw = [[_sb(k, s2, f"b_Pw{g}{j}", [128, 512], BF16) for j in range(3)] for g in range(2)]
            Pm = [_sb(k, s2, f"b_Pm{g}", [16, 512], BF16) for g in range(2)]
            den = _sb(k, s2, "b_den", [128, 4], F32)
            ybs = [_sb(k, s2, f"b_yb{i}", [128, 8, 64], BF16) for i in range(2)]
            S.dma("sp", ET[0][:], d["etab"].rearrange("o g p n -> p (o g) n"), writes=[ET[1]])
            S.dma("sp", esk[0][:], d["b_sink"].partition_broadcast(128), writes=[esk[1]])
            S.op("act", lambda e: e.activation(out=esk[0][:], in_=esk[0][:], func=AF.Exp), reads=[esk[1]], writes=[esk[1]])
            sc = 0
            for i in range(NT):
                yb = ybs[i % 2]
                js = [j for j in (i - 1, i, i + 1) if 1 <= j <= NT - 1]
                for g in range(2):
                    prt = slice(64 * g, 64 * g + 64)
                    qs = qT_all[0][prt, i, :]
                    for ji, j in enumerate(js):
                        bs = sc % 2
                        ex = exs[sc % 2]
                        sc += 1
                        S.op("pe", lambda e, j=j, bs=bs: e.matmul(out=bank(k, bs), lhsT=kT_all[0][prt, j * 128:(j + 1) * 128], rhs=qs,
                                                                  start=True, stop=True), reads=[Br[j], Br[i]], writes=[k.PB[bs]])
                        S.op("act", lambda e, ex=ex, bs=bs: e.activation(out=ex[0][:], in_=bank(k, bs), func=AF.Exp),
                             reads=[k.PB[bs]], writes=[ex[1]])
                        S.op("dve", lambda e, ex=ex, ji=ji, j=j: e.tensor_tensor(out=Pw[g][ji][0][:], in0=ex[0][:],
                                                                                  in1=ET[0][:, (j - i + 1) * 2 + g, :], op=ALU.mult),
                             reads=[ex[1], ET[1]], writes=[Pw[g][ji][1]])
                    S.op("pe", lambda e: e.matmul(out=k.ps[0:16, 1024:1536], lhsT=kT_all[0][prt, FP:128], rhs=qs, start=True, stop=True),
                         reads=[Br[0], Br[i]], writes=[k.PB[2]])
                    S.op("act", lambda e: e.activation(out=Pm[g][0][:], in_=k.ps[0:16, 1024:1536], func=AF.Exp),
                         reads=[k.PB[2]], writes=[Pm[g][1]])
                    bo = 3 + g

                    def pv(e, g=g, bo=bo):
                        for a in range(4):
                            o = k.ps[:, bo * 512 + a * 65:bo * 512 + (a + 1) * 65]
                            for ji, j in enumerate(js):
                                e.matmul(out=o, lhsT=Pw[g][ji][0][:, a * 128:(a + 1) * 128], rhs=Vaug[0][:, j, g, :],
                                         start=(ji == 0), stop=False)
                            i_ = e.matmul(out=o, lhsT=Pm[g][0][:, a * 128:(a + 1) * 128], rhs=Vm[0][:, g, :],
                                          start=(len(js) == 0), stop=True)
                        return i_
                    S.op("pe", pv, reads=[Pw[g][ji][1] for ji in range(len(js))] + [Pm[g][1]] + [Br[j] for j in js] + [Br[0]],
                         writes=[k.PB[bo]])
                    o4 = k.ps[:, bo * 512:bo * 512 + 260].rearrange("p (a e) -> p a e", a=4)
                    S.op("dve", lambda e, g=g, o4=o4: e.tensor_tensor(out=den[0][:], in0=o4[:, :, 64], in1=esk[0][:, 4 * g:4 * g + 4],
                                                                      op=ALU.add), reads=[k.PB[bo], esk[1]], writes=[den[1]])
                    S.op("dve", lambda e: e.reciprocal(out=den[0][:], in_=den[0][:]), reads=[den[1]], writes=[den[1]])
                    S.op("dve", lambda e, g=g, o4=o4, yb=yb: e.tensor_tensor(out=yb[0][:, 4 * g:4 * g + 4, :], in0=o4[:, :, 0:64],
                                                                             in1=den[0][:].unsqueeze(2).broadcast_to([128, 4, 64]),
                                                                             op=ALU.mult), reads=[k.PB[bo], den[1]], writes=[yb[1]])
                S.dma("act", d["scr_yb"][i * 128:(i + 1) * 128, :], yb[0][:].rearrange("p h e -> p (h e)"), reads=[yb[1]])
            S.barrier()
    return uT_all, uTr, BETA, G


def tok_groups():
    g = [(i * 512, 512) for i in range(LP // 512)]
    if LP % 512:
        g.append((LP // 512 * 512, LP % 512))
    return g


def layer_ab_conv(k, uT_all, uTr):
    S = k.S
    d = k.d
    with contextlib.ExitStack() as st:
        wf = _sb(k, st, "v_wf", [128, 8, 1536], BF16)
        cw = _sb(k, st, "v_cw", [128, 12, 5], F32)
        raws = [_sb(k, st, f"v_raw{i}", [128, LP + 4], F32) for i in range(2)]
        acc = _sb(k, st, "v_acc", [128, LP], F32)
        acts = [_sb(k, st, f"v_act{i}", [128, LP], BF16) for i in range(2)]
        stg = _sb(k, st, "v_stg", [128, NT, 512], BF16)
        load_weight_bf16(k, wf, d["w_in_ab"][:, 0:1536], 8, 1536)
        for j in range(5):
            S.dma("sp", cw[0][:, :, j], d["conv_w_a"][j, :].rearrange("(c p) -> p c", p=128), writes=[cw[1]],
                  allow_slow_non_contiguous=True)
        for r in raws:
            S.op("pool", lambda e, r=r: e.memset(r[0][:], 0.0), writes=[r[1]])
        def proj(cc):
            rw = raws[cc % 2]
            for gi, (q0, qn_) in enumerate(tok_groups()):
                b = gi % 2

                def mm(e, q0=q0, qn_=qn_, b=b):
                    for c in range(8):
                        i = e.matmul(out=bank(k, b, qn_), lhsT=wf[0][:, c, cc * 128:(cc + 1) * 128], rhs=uT_all[0][:, c, q0:q0 + qn_],
                                     start=(c == 0), stop=(c == 7))
                    return i
                S.op("pe", mm, reads=[wf[1]] + [uTr[t] for t in range(q0 // 128, (q0 + qn_) // 128)], writes=[k.PB[b]])
                S.op("act", lambda e, q0=q0, qn_=qn_, b=b: e.activation(out=rw[0][:, 2 + q0:2 + q0 + qn_], in_=bank(k, b, qn_), func=AF.Copy),
                     reads=[k.PB[b]], writes=[rw[1]])
        def rest(cc):
            rw = raws[cc % 2]
            act = acts[cc % 2]
            S.op("dve", lambda e: e.tensor_scalar(out=acc[0][:], in0=rw[0][:, 0:LP], scalar1=cw[0][:, cc, 0:1], scalar2=None, op0=ALU.mult),
                 reads=[rw[1], cw[1]], writes=[acc[1]])
            for j in range(1, 5):
                S.op("dve", lambda e, j=j: e.scalar_tensor_tensor(out=acc[0][:], in0=rw[0][:, j:j + LP], scalar=cw[0][:, cc, j:j + 1],
                                                                  in1=acc[0][:], op0=ALU.mult, op1=ALU.add),
                     reads=[rw[1], cw[1], acc[1]], writes=[acc[1]])
            S.op("act", lambda e: e.activation(out=act[0][:], in_=acc[0][:], func=AF.Silu), reads=[acc[1]], writes=[act[1]])
            for t0 in range(0, NT, 8):
                nt_ = min(8, NT - t0)
                b = 2 + (t0 // 8) % 2
                pT = bank_bf(k, b)

                def tr(e, t0=t0, nt_=nt_, pT=pT):
                    for tt in range(nt_):
                        i = e.transpose(out=pT[:, tt * 128:(tt + 1) * 128], in_=act[0][:, (t0 + tt) * 128:(t0 + tt + 1) * 128],
                                        identity=k.ident[0][:])
                    return i
                S.op("pe", tr, reads=[act[1], k.ident[1]], writes=[k.PB[b]])
                S.op("act" if (t0 // 8) % 2 else "dve",
                     (lambda e, t0=t0, nt_=nt_, pT=pT: e.activation(out=stg[0][:, t0:t0 + nt_, (cc % 4) * 128:(cc % 4 + 1) * 128],
                                                                    in_=pT[:, 0:nt_ * 128].rearrange("p (t n) -> p t n", t=nt_), func=AF.Copy))
                     if (t0 // 8) % 2 else
                     (lambda e, t0=t0, nt_=nt_, pT=pT: e.tensor_copy(out=stg[0][:, t0:t0 + nt_, (cc % 4) * 128:(cc % 4 + 1) * 128],
                                                                     in_=pT[:, 0:nt_ * 128].rearrange("p (t n) -> p t n", t=nt_))),
                     reads=[k.PB[b]], writes=[stg[1]])
            if cc % 4 == 3:
                qi = cc // 4
                for t0 in range(0, NT, 8):
                    nt_ = min(8, NT - t0)
                    S.dma("sp", d["scr_a"][t0 * 128:(t0 + nt_) * 128, qi * 512:(qi + 1) * 512].rearrange("(t p) c -> p t c", p=128),
                          stg[0][:, t0:t0 + nt_, :], reads=[stg[1]])

        proj(0)
        for cc in range(12):
            if cc + 1 < 12:
                proj(cc + 1)
            rest(cc)
        S.barrier()


def layer_ab_delta(k, BETA, G):
    S = k.S
    d = k.d
    bctr = [0, 0]
    cur_dd = [0]

    def nb():
        dd = cur_dd[0]
        b = 4 * dd + bctr[dd] % 4
        bctr[dd] += 1
        return b

    with contextlib.ExitStack() as st:
        tri = _sb(k, st, "d_tri", [128, 4, 128], F32)
        identf = _sb(k, st, "d_identf", [128, 128], F32)
        onesf = _sb(k, st, "d_onesf", [128, 128], F32)
        S.dma("sp", tri[0][:], d["tri"].rearrange("m p n -> p m n"), writes=[tri[1]])
        S.op("dve", lambda e: e.tensor_tensor(out=identf[0][:], in0=tri[0][:, 0, :], in1=tri[0][:, 1, :], op=ALU.mult),
             reads=[tri[1]], writes=[identf[1]])
        S.op("pool", lambda e: e.memset(onesf[0][:], 1.0), writes=[onesf[1]])
        W = []
        for dd in range(2):
            w = {}
            def mk(name, shape, dt, dd=dd, w=w):
                w[name] = _sb(k, st, f"d_{name}{dd}", shape, dt)
            mk("At", [128, 1536], BF16)
            mk("sq", [128, 8, 128], F32)
            mk("ssq", [128, 8], F32)
            mk("ex12", [128, 12], F32)
            for nm in ("kn", "qn", "kg", "kd", "qd", "qdT", "qkT", "Rb", "wT", "vnew", "Sb", "Rb1", "Qb0", "Qb1", "QTb0", "QTb1"):
                mk(nm, [128, 4, 128], BF16)
            mk("kTq", [128, 8, 128], BF16)
            for nm in ("gSL", "dec", "decS", "decI", "tmp", "Bp", "Q0", "Q1", "QT1", "R0", "R1", "ub", "S32", "osb"):
                mk(nm, [128, 4, 128], F32)
            S.op("pool", lambda e, w=w: e.memset(w["S32"][0][:], 0.0), writes=[w["S32"][1]])
            S.op("pool", lambda e, w=w: e.memset(w["Sb"][0][:], 0.0), writes=[w["Sb"][1]])
            W.append(w)

        def b3(ap2):
            return ap2.unsqueeze(2).broadcast_to([128, 4, 128])

        def m3(ap2):
            return ap2.unsqueeze(1).broadcast_to([128, 4, 128])

        def bk4(b):
            return bank(k, b).rearrange("p (h n) -> p h n", h=4)

        class Rec:
            def __init__(self):
                self.ops = []

            def op(self, *a, **kw):
                self.ops.append(lambda: k.S.op(*a, **kw))

            def dma(self, *a, **kw):
                self.ops.append(lambda: k.S.dma(*a, **kw))

        def delta_tile(dd, n, S):
            cur_dd[0] = dd
            w = W[dd]
            U = tri[0][:, 0, :] if dd == 0 else tri[0][:, 1, :]
            SL = tri[0][:, 3, :] if dd == 0 else tri[0][:, 2, :]
            MincT = U
            MstrT = tri[0][:, 2, :] if dd == 0 else tri[0][:, 3, :]
            g4 = G[0][:, n, 4 * dd:4 * dd + 4]
            b4 = BETA[0][:, n, 4 * dd:4 * dd + 4]
            At, sq, ssq, ex12 = w["At"], w["sq"], w["ssq"], w["ex12"]
            S.dma("sp", At[0][:], d["scr_a"][n * 128:(n + 1) * 128, :], writes=[At[1]])
            A3 = At[0][:].rearrange("p (h n) -> p h n", h=12)
            S.op("act", lambda e: e.activation(out=sq[0][:], in_=A3[:, 0:8, :], func=AF.Square), reads=[At[1]], writes=[sq[1]])
            S.op("dve", lambda e: e.tensor_reduce(out=ssq[0][:], in_=sq[0][:], axis=AX.X, op=ALU.add), reads=[sq[1]], writes=[ssq[1]])
            S.op("act", lambda e: e.activation(out=ssq[0][:], in_=ssq[0][:], func=AF.Ln, bias=EPS), reads=[ssq[1]], writes=[ssq[1]])
            S.op("act", lambda e: e.activation(out=ssq[0][:], in_=ssq[0][:], func=AF.Exp, scale=-0.5), reads=[ssq[1]], writes=[ssq[1]])
            S.op("dve", lambda e: e.tensor_scalar(out=ssq[0][:, 0:4], in0=ssq[0][:, 0:4], scalar1=128.0 ** -0.5, scalar2=None, op0=ALU.mult),
                 reads=[ssq[1]], writes=[ssq[1]])
            bg = nb()

            def mmg(e):
                e.matmul(out=k.ps[:, bg * 512:bg * 512 + 4], lhsT=U, rhs=g4, start=True, stop=True)
                return e.matmul(out=k.ps[:, bg * 512 + 4:bg * 512 + 8], lhsT=onesf[0][:], rhs=g4, start=True, stop=True)
            S.op("pe", mmg, reads=[tri[1], onesf[1], G[1]], writes=[k.PB[bg]])
            S.op("act", lambda e: e.activation(out=ex12[0][:, 0:8], in_=k.ps[:, bg * 512:bg * 512 + 8], func=AF.Copy),
                 reads=[k.PB[bg]], writes=[ex12[1]])
            S.op("dve", lambda e: e.tensor_tensor(out=ex12[0][:, 8:12], in0=ex12[0][:, 4:8], in1=ex12[0][:, 0:4], op=ALU.subtract),
                 reads=[ex12[1]], writes=[ex12[1]])
            S.op("act", lambda e: e.activation(out=ex12[0][:], in_=ex12[0][:], func=AF.Exp), reads=[ex12[1]], writes=[ex12[1]])
            eg, glast, ekd = ex12[0][:, 0:4], ex12[0][:, 4:8], ex12[0][:, 8:12]
            kn, qn, kg, kd, qd = w["kn"], w["qn"], w["kg"], w["kd"], w["qd"]
            S.op("dve", lambda e: e.tensor_tensor(out=qn[0][:], in0=A3[:, 0:4, :], in1=b3(ssq[0][:, 0:4]), op=ALU.mult),
                 reads=[At[1], ssq[1]], writes=[qn[1]])
            S.op("pool", lambda e: e.tensor_tensor(out=kn[0][:], in0=A3[:, 4:8, :], in1=b3(ssq[0][:, 4:8]), op=ALU.mult),
                 reads=[At[1], ssq[1]], writes=[kn[1]])
            S.op("dve", lambda e: e.tensor_tensor(out=kg[0][:], in0=kn[0][:], in1=b3(eg), op=ALU.mult), reads=[kn[1], ex12[1]], writes=[kg[1]])
            S.op("pool", lambda e: e.tensor_tensor(out=kd[0][:], in0=kn[0][:], in1=b3(ekd), op=ALU.mult), reads=[kn[1], ex12[1]], writes=[kd[1]])
            S.op("pool", lambda e: e.tensor_tensor(out=qd[0][:], in0=qn[0][:], in1=b3(eg), op=ALU.mult), reads=[qn[1], ex12[1]], writes=[qd[1]])
            bt1, bt2 = nb(), nb()
            p1, p2 = bank_bf(k, bt1), bank_bf(k, bt2)

            def tr1(e):
                for h in range(4):
                    e.transpose(out=p1[:, h * 128:(h + 1) * 128], in_=kn[0][:, h, :], identity=k.ident[0][:])
                for h in range(4):
                    i = e.transpose(out=p1[:, (4 + h) * 128:(5 + h) * 128], in_=qn[0][:, h, :], identity=k.ident[0][:])
                return i
            S.op("pe", tr1, reads=[kn[1], qn[1], k.ident[1]], writes=[k.PB[bt1]])

            def tr2(e):
                for h in range(4):
                    i = e.transpose(out=p2[:, h * 128:(h + 1) * 128], in_=qd[0][:, h, :], identity=k.ident[0][:])
                return i
            S.op("pe", tr2, reads=[qd[1], k.ident[1]], writes=[k.PB[bt2]])
            kTq, qdT = w["kTq"], w["qdT"]
            S.op("act", lambda e: e.activation(out=kTq[0][:], in_=p1.rearrange("p (h n) -> p h n", h=8), func=AF.Copy),
                 reads=[k.PB[bt1]], writes=[kTq[1]])
            S.op("act", lambda e: e.activation(out=qdT[0][:], in_=p2[:, 0:512].rearrange("p (h n) -> p h n", h=4), func=AF.Copy),
                 reads=[k.PB[bt2]], writes=[qdT[1]])
            gSL, dec, decS, decI = w["gSL"], w["dec"], w["decS"], w["decI"]
            S.op("pool", lambda e: e.tensor_tensor(out=gSL[0][:], in0=m3(SL), in1=b3(g4), op=ALU.mult), reads=[tri[1], G[1]], writes=[gSL[1]])
            bd = nb()

            def mmd(e):
                for h in range(4):
                    i = e.matmul(out=bank(k, bd, 128, h * 128), lhsT=gSL[0][:, h, :], rhs=U, start=True, stop=True)
                return i
            S.op("pe", mmd, reads=[gSL[1], tri[1]], writes=[k.PB[bd]])
            S.op("act", lambda e: e.activation(out=dec[0][:], in_=bk4(bd), func=AF.Exp), reads=[k.PB[bd]], writes=[dec[1]])
            S.op("dve", lambda e: e.tensor_tensor(out=decS[0][:], in0=dec[0][:], in1=m3(MstrT), op=ALU.mult), reads=[dec[1], tri[1]], writes=[decS[1]])
            S.op("pool", lambda e: e.tensor_tensor(out=decI[0][:], in0=dec[0][:], in1=m3(MincT), op=ALU.mult), reads=[dec[1], tri[1]], writes=[decI[1]])
            bkk, bkq = nb(), nb()

            def mmk(e):
                for h in range(4):
                    i = e.matmul(out=bank(k, bkk, 128, h * 128), lhsT=kTq[0][:, h, :], rhs=kTq[0][:, h, :], start=True, stop=True)
                return i
            S.op("pe", mmk, reads=[kTq[1]], writes=[k.PB[bkk]])

            def mmq(e):
                for h in range(4):
                    i = e.matmul(out=bank(k, bkq, 128, h * 128), lhsT=kTq[0][:, h, :], rhs=kTq[0][:, 4 + h, :], start=True, stop=True)
                return i
            S.op("pe", mmq, reads=[kTq[1]], writes=[k.PB[bkq]])
            tmp, Bp, qkT = w["tmp"], w["Bp"], w["qkT"]
            S.op("dve", lambda e: e.tensor_tensor(out=tmp[0][:], in0=bk4(bkk), in1=decS[0][:], op=ALU.mult), reads=[k.PB[bkk], decS[1]], writes=[tmp[1]])
            S.op("pool", lambda e: e.tensor_tensor(out=Bp[0][:], in0=tmp[0][:], in1=b3(b4), op=ALU.mult), reads=[tmp[1], BETA[1]], writes=[Bp[1]])
            S.op("dve", lambda e: e.tensor_tensor(out=qkT[0][:], in0=bk4(bkq), in1=decI[0][:], op=ALU.mult), reads=[k.PB[bkq], decI[1]], writes=[qkT[1]])
            Q = [w["Q0"], w["Q1"]]
            QT = [Bp, w["QT1"]]
            R = [w["R0"], w["R1"]]
            ba = nb()

            def tra(e):
                for h in range(4):
                    i = e.transpose(out=bank(k, ba, 128, h * 128), in_=Bp[0][:, h, :], identity=identf[0][:])
                return i
            S.op("pe", tra, reads=[Bp[1], identf[1]], writes=[k.PB[ba]])
            S.op("act", lambda e: e.activation(out=Q[0][0][:], in_=bk4(ba), func=AF.Copy), reads=[k.PB[ba]], writes=[Q[0][1]])
            S.op("pool", lambda e: e.tensor_tensor(out=R[0][0][:], in0=m3(identf[0][:]), in1=Bp[0][:], op=ALU.subtract),
                 reads=[identf[1], Bp[1]], writes=[R[0][1]])
            Qb = [w["Qb0"], w["Qb1"]]
            QTb = [w["QTb0"], w["QTb1"]]
            Rbb = [w["Rb"], w["Rb1"]]
            qc, qtc, rc = Q[0], QT[0], R[0]
            f32_q = [Q[1], Q[0]]
            f32_qt = [w["QT1"], w["tmp"]]
            f32_r = [R[1], R[0]]
            for lev in range(1, 7):
                lowp = lev >= 4
                b1 = nb()

                def mq(e, qc=qc, qtc=qtc, b1=b1):
                    for h in range(4):
                        i = e.matmul(out=bank(k, b1, 128, h * 128), lhsT=qtc[0][:, h, :], rhs=qc[0][:, h, :], start=True, stop=True)
                    return i
                S.op("pe", mq, reads=[qtc[1], qc[1]], writes=[k.PB[b1]])
                if lev < 6:
                    b2 = nb()

                    def mqt(e, qc=qc, qtc=qtc, b2=b2):
                        for h in range(4):
                            i = e.matmul(out=bank(k, b2, 128, h * 128), lhsT=qc[0][:, h, :], rhs=qtc[0][:, h, :], start=True, stop=True)
                        return i
                    S.op("pe", mqt, reads=[qtc[1], qc[1]], writes=[k.PB[b2]])
                qn_ = Qb[lev % 2] if lowp else f32_q[(lev - 1) % 2]
                S.op("act", lambda e, qn_=qn_, b1=b1: e.activation(out=qn_[0][:], in_=bk4(b1), func=AF.Copy), reads=[k.PB[b1]], writes=[qn_[1]])
                q_next = qn_
                if lev == 3:
                    q_next = Qb[1]
                    S.op("act", lambda e, q_next=q_next, b1=b1: e.activation(out=q_next[0][:], in_=bk4(b1), func=AF.Copy),
                         reads=[k.PB[b1]], writes=[q_next[1]])
                if lev < 6:
                    qt_new = QTb[lev % 2] if lev >= 3 else f32_qt[(lev - 1) % 2]
                    S.op("act", lambda e, qt_new=qt_new, b2=b2: e.activation(out=qt_new[0][:], in_=bk4(b2), func=AF.Copy),
                         reads=[k.PB[b2]], writes=[qt_new[1]])
                    qtc = qt_new
                b3_ = nb()

                def mr(e, qn_=qn_, rc=rc, b3_=b3_):
                    for h in range(4):
                        i = e.matmul(out=bank(k, b3_, 128, h * 128), lhsT=qn_[0][:, h, :], rhs=rc[0][:, h, :], start=True, stop=True)
                    return i
                S.op("pe", mr, reads=[qn_[1], rc[1]], writes=[k.PB[b3_]])
                rn = Rbb[lev % 2] if lev >= 3 else f32_r[(lev - 1) % 2]
                S.op("dve", lambda e, rc=rc, rn=rn, b3_=b3_: e.tensor_tensor(out=rn[0][:], in0=bk4(b3_), in1=rc[0][:], op=ALU.add),
                     reads=[k.PB[b3_], rc[1]], writes=[rn[1]])
                rc = rn
                qc = q_next
            Rb = rc
            ub_, wT = w["ub"], w["wT"]
            bu, bw = nb(), nb()

            def mu(e):
                for h in range(4):
                    i = e.matmul(out=bank(k, bu, 128, h * 128), lhsT=Rb[0][:, h, :], rhs=A3[:, 8 + h, :], start=True, stop=True)
                return i
            S.op("pe", mu, reads=[Rb[1], At[1]], writes=[k.PB[bu]])

            def mw(e):
                for h in range(4):
                    i = e.matmul(out=bank(k, bw, 128, h * 128), lhsT=kg[0][:, h, :], rhs=Rb[0][:, h, :], start=True, stop=True)
                return i
            S.op("pe", mw, reads=[Rb[1], kg[1]], writes=[k.PB[bw]])
            S.op("dve", lambda e: e.tensor_tensor(out=ub_[0][:], in0=bk4(bu), in1=b3(b4), op=ALU.mult), reads=[k.PB[bu], BETA[1]], writes=[ub_[1]])
            S.op("act", lambda e: e.activation(out=wT[0][:], in_=bk4(bw), func=AF.Copy), reads=[k.PB[bw]], writes=[wT[1]])
            S32, Sb, vnew, osb = w["S32"], w["Sb"], w["vnew"], w["osb"]
            tmp2 = w["dec"]
            b1 = nb()

            def m1(e):
                for h in range(4):
                    i = e.matmul(out=bank(k, b1, 128, h * 128), lhsT=wT[0][:, h, :], rhs=Sb[0][:, h, :], start=True, stop=True)
                return i
            S.op("pe", m1, reads=[wT[1], Sb[1]], writes=[k.PB[b1]])
            S.op("dve", lambda e: e.tensor_tensor(out=tmp2[0][:], in0=bk4(b1), in1=b3(b4), op=ALU.mult), reads=[k.PB[b1], BETA[1]], writes=[tmp2[1]])
            S.op("dve", lambda e: e.tensor_tensor(out=vnew[0][:], in0=ub_[0][:], in1=tmp2[0][:], op=ALU.subtract),
                 reads=[ub_[1], tmp2[1]], writes=[vnew[1]])
            b2, b3b = nb(), nb()

            def m2(e):
                for h in range(4):
                    e.matmul(out=bank(k, b2, 128, h * 128), lhsT=qdT[0][:, h, :], rhs=Sb[0][:, h, :], start=True, stop=False)
                    i = e.matmul(out=bank(k, b2, 128, h * 128), lhsT=qkT[0][:, h, :], rhs=vnew[0][:, h, :], start=False, stop=True)
                return i
            S.op("pe", m2, reads=[qdT[1], Sb[1], qkT[1], vnew[1]], writes=[k.PB[b2]])

            def m3_(e):
                for h in range(4):
                    i = e.matmul(out=bank(k, b3b, 128, h * 128), lhsT=kd[0][:, h, :], rhs=vnew[0][:, h, :], start=True, stop=True)
                return i
            S.op("pe", m3_, reads=[kd[1], vnew[1]], writes=[k.PB[b3b]])
            S.op("act", lambda e: e.activation(out=osb[0][:], in_=bk4(b2), func=AF.Copy), reads=[k.PB[b2]], writes=[osb[1]])
            S.dma("act", d["scr_o"][dd, n * 128:(n + 1) * 128, :], osb[0][:].rearrange("p h n -> p (h n)"), reads=[osb[1]])
            S.op("dve", lambda e: e.tensor_tensor(out=tmp2[0][:], in0=S32[0][:], in1=b3(glast), op=ALU.mult),
                 reads=[S32[1], ex12[1]], writes=[tmp2[1]])
            S.op("dve", lambda e: e.tensor_tensor(out=S32[0][:], in0=bk4(b3b), in1=tmp2[0][:], op=ALU.add),
                 reads=[k.PB[b3b], tmp2[1]], writes=[S32[1]])
            S.op("act", lambda e: e.activation(out=Sb[0][:], in_=S32[0][:], func=AF.Copy), reads=[S32[1]], writes=[Sb[1]])

        jobs = getattr(k, "jobs", [])
        for step in range(NT):
            ra, rb = Rec(), Rec()
            njob = (len(jobs) + (NT - step) - 1) // (NT - step) if jobs else 0
            for _ in range(njob):
                ra.ops.append(jobs.pop(0))
            delta_tile(0, step, ra)
            delta_tile(1, NT - 1 - step, rb)
            for i in range(max(len(ra.ops), len(rb.ops))):
                if i < len(ra.ops):
                    ra.ops[i]()
                if i < len(rb.ops):
                    rb.ops[i]()
        S.barrier()


def layer_ab_out(k, src, dst):
    S = k.S
    d = k.d
    with contextlib.ExitStack() as st:
        wout = _sb(k, st, "o_wout", [128, 8, D], BF16)
        GO = _sb(k, st, "o_GO", [128, 128], F32)
        ofs = [_sb(k, st, f"o_of{i}", [128, 4, 128], F32) for i in range(2)]
        obs = [_sb(k, st, f"o_ob{i}", [128, 4, 128], F32) for i in range(2)]
        szs = [_sb(k, st, f"o_sz{i}", [128, 4, 128], BF16) for i in range(2)]
        mixs = [_sb(k, st, f"o_mix{i}", [128, D], BF16) for i in range(2)]
        hts = [_sb(k, st, f"o_ht{i}", [128, D], F32) for i in range(2)]
        hos = [_sb(k, st, f"o_ho{i}", [128, D], F32) for i in range(2)]
        sqs = [_sb(k, st, f"o_sq{i}", [128, 4, 128], F32) for i in range(2)]
        ssqs = [_sb(k, st, f"o_ssq{i}", [128, 4], F32) for i in range(2)]
        mixTs = [_sb(k, st, f"o_mixT{i}", [128, 8, 128], BF16) for i in range(2)]
        load_weight_bf16(k, wout, d["w_out_ab"], 8, D)
        S.dma("sp", GO[0][:], d["a_out_norm_g"].partition_broadcast(128), writes=[GO[1]])
        def body(S, t):
            of, ob, sz, mix, ht, ho = ofs[t % 2], obs[t % 2], szs[t % 2], mixs[t % 2], hts[t % 2], hos[t % 2]
            sq, ssq, mixT = sqs[t % 2], ssqs[t % 2], mixTs[t % 2]
            tbk = 0 if t % 2 == 0 else 5
            rows = slice(t * 128, (t + 1) * 128)
            S.dma("sp", of[0][:].rearrange("p h n -> p (h n)"), d["scr_o"][0, rows, :], writes=[of[1]])
            S.dma("sp", ob[0][:].rearrange("p h n -> p (h n)"), d["scr_o"][1, rows, :], writes=[ob[1]])
            S.dma("sp", sz[0][:].rearrange("p h n -> p (h n)"), d["scr_z"][rows, :], writes=[sz[1]])
            S.dma("sp", mix[0][:, 512:1024], d["scr_yb"][rows, :], writes=[mix[1]])
            S.dma("sp", ht[0][:], src(t), writes=[ht[1]])
            S.op("dve", lambda e: e.tensor_tensor(out=of[0][:], in0=of[0][:], in1=ob[0][:], op=ALU.add), reads=[of[1], ob[1]], writes=[of[1]])
            S.op("act", lambda e: e.activation(out=sq[0][:], in_=of[0][:], func=AF.Square), reads=[of[1]], writes=[sq[1]])
            S.op("dve", lambda e: e.tensor_reduce(out=ssq[0][:], in_=sq[0][:], axis=AX.X, op=ALU.add), reads=[sq[1]], writes=[ssq[1]])
            S.op("act", lambda e: e.activation(out=ssq[0][:], in_=ssq[0][:], func=AF.Ln, scale=1.0 / 128, bias=EPS), reads=[ssq[1]], writes=[ssq[1]])
            S.op("act", lambda e: e.activation(out=ssq[0][:], in_=ssq[0][:], func=AF.Exp, scale=-0.5), reads=[ssq[1]], writes=[ssq[1]])
            S.op("dve", lambda e: e.tensor_tensor(out=sq[0][:], in0=of[0][:], in1=ssq[0][:].unsqueeze(2).broadcast_to([128, 4, 128]), op=ALU.mult),
                 reads=[of[1], ssq[1]], writes=[sq[1]])
            S.op("pool", lambda e: e.tensor_tensor(out=sq[0][:], in0=sq[0][:], in1=GO[0][:].unsqueeze(1).broadcast_to([128, 4, 128]), op=ALU.mult),
                 reads=[sq[1], GO[1]], writes=[sq[1]])
            S.op("dve", lambda e: e.tensor_tensor(out=mix[0][:, 0:512].rearrange("p (h n) -> p h n", h=4), in0=sq[0][:], in1=sz[0][:], op=ALU.mult),
                 reads=[sq[1], sz[1]], writes=[mix[1]])
            pT = bank_bf(k, tbk)

            def tr(e):
                for c in range(8):
                    i = e.transpose(out=pT[:, c * 128:(c + 1) * 128], in_=mix[0][:, c * 128:(c + 1) * 128], identity=k.ident[0][:])
                return i
            S.op("pe", tr, reads=[mix[1], k.ident[1]], writes=[k.PB[tbk]])
            S.op("act", lambda e: e.activation(out=mixT[0][:], in_=pT.rearrange("p (c n) -> p c n", c=8), func=AF.Copy),
                 reads=[k.PB[tbk]], writes=[mixT[1]])
            for nbk in range(2):
                b = 1 + nbk + 2 * (t % 2)

                def mm(e, nbk=nbk, b=b):
                    for c in range(8):
                        i = e.matmul(out=bank(k, b), lhsT=mixT[0][:, c, :], rhs=wout[0][:, c, nbk * 512:(nbk + 1) * 512],
                                     start=(c == 0), stop=(c == 7))
                    return i
                S.op("pe", mm, reads=[mixT[1], wout[1]], writes=[k.PB[b]])
                S.op("dve", lambda e, nbk=nbk, b=b: e.tensor_tensor(out=ho[0][:, nbk * 512:(nbk + 1) * 512], in0=bank(k, b),
                                                                    in1=ht[0][:, nbk * 512:(nbk + 1) * 512], op=ALU.add),
                     reads=[k.PB[b], ht[1]], writes=[ho[1]])
            if t == 0:
                S.op("dve", lambda e: e.tensor_scalar(out=ho[0][:], in0=ho[0][:], scalar1=k.padmask[0][:, 0:1], scalar2=None, op0=ALU.mult),
                     reads=[ho[1], k.padmask[1]], writes=[ho[1]])
            S.dma("act", dst(t), ho[0][:], reads=[ho[1]])
        emit_skewed(k, [(lambda S, t=t: body(S, t)) for t in range(NT)])
        S.barrier()
```
